# Optimizing a Trainium2 kernel written in Bass

```python
import math
import jax, jax.numpy as jnp
from jax import lax
import numpy as np

D_MODEL = 1024
BATCH = 4
SEQ = 4096
DEPTH = 1

RWKV_HEADS = 8
RWKV_HEAD = 64
RWKV_W = RWKV_HEADS * RWKV_HEAD
DIFF_HEADS = 4
DIFF_QK = 64
DIFF_V = 2 * DIFF_QK
DIFF_W = DIFF_HEADS * DIFF_V
MIX_W = RWKV_W + DIFF_W
DECAY_LORA = 64
ICLR_LORA = 64
GATE_LORA = 160
RWKV_COLS = [RWKV_W, RWKV_W, RWKV_W, DECAY_LORA, ICLR_LORA, GATE_LORA]
SHIFT_COLS = sum(RWKV_COLS)
DIFF_COLS = [DIFF_HEADS * 2 * DIFF_QK, DIFF_HEADS * 2 * DIFF_QK, DIFF_W]
PROJ_W = SHIFT_COLS + sum(DIFF_COLS)
D_FF = 2816
FFN_RES = 0.5
Q_BLOCK = 128
NORM_EPS = 1e-6
GN_EPS = 64e-5
SUBLN_EPS = 1e-5

kernel_name = "hybrid_rwkv7_diffattn_macaron"


def _split(t, widths):
    idx = [int(v) for v in np.cumsum(widths)[:-1]]
    return jnp.split(t, idx, axis=-1)


def rmsnorm(x, g, eps=NORM_EPS):
    xf = x.astype(jnp.float32)
    y = xf * lax.rsqrt(jnp.mean(xf * xf, axis=-1, keepdims=True) + eps)
    return (y * g.astype(jnp.float32)).astype(x.dtype)


def swiglu(x, w_gate, w_up, w_down):
    return (jax.nn.silu(x @ w_gate) * (x @ w_up)) @ w_down


def token_shift(p):
    return jnp.pad(p, ((0, 0), (1, 0), (0, 0)))[:, :-1]


def rwkv7_scan(r, w, k, v, a, b):
    B, T, H, N = r.shape

    def step(S, inp):
        r_t, w_t, k_t, v_t, a_t, b_t = inp
        sa = jnp.einsum('bhij,bhj->bhi', S, a_t)
        S = (S * w_t[:, :, None, :] + sa[..., None] * b_t[:, :, None, :]
             + v_t[..., None] * k_t[:, :, None, :])
        y = jnp.einsum('bhij,bhj->bhi', S, r_t)
        return S, y

    S0 = jnp.zeros((B, H, N, N), jnp.float32)
    xs = tuple(jnp.moveaxis(t, 1, 0) for t in (r, w, k, v, a, b))
    _, ys = lax.scan(step, S0, xs)
    return jnp.moveaxis(ys, 0, 1)


def rwkv7_group(p_r, p_k, p_v, p_wd, p_ad, p_gd, w0, w_up, a0, a_up, g_up,
                k_k, k_a, r_k, gn_w, gn_b):
    B, T, _ = p_r.shape
    H, N = RWKV_HEADS, RWKV_HEAD
    f32 = jnp.float32
    logw = -jax.nn.softplus(-(w0 + jnp.tanh(p_wd) @ w_up)) - 0.5
    decay = jnp.exp(-jnp.exp(logw.astype(f32)))
    iclr = jax.nn.sigmoid(a0 + p_ad @ a_up)
    gate = jax.nn.sigmoid(p_gd) @ g_up
    kk = (p_k * k_k).reshape(B, T, H, N).astype(f32)
    kk = kk * lax.rsqrt(jnp.maximum(jnp.sum(kk * kk, -1, keepdims=True), 1e-24))
    k = p_k * (1.0 + (iclr - 1.0) * k_a)
    hs = lambda t: t.reshape(B, T, H, N).astype(f32)
    r_h, k_h, v_h, a_h = hs(p_r), hs(k), hs(p_v), hs(iclr)
    y = rwkv7_scan(r_h, hs(decay), k_h, v_h, -kk, kk * a_h)
    mu = jnp.mean(y, -1, keepdims=True)
    var = jnp.mean(jnp.square(y - mu), -1, keepdims=True)
    y = (y - mu) * lax.rsqrt(var + GN_EPS)
    y = y * gn_w.reshape(H, N).astype(f32) + gn_b.reshape(H, N).astype(f32)
    y = y + jnp.sum(r_h * k_h * r_k.astype(f32), -1, keepdims=True) * v_h
    y = y.reshape(B, T, RWKV_W)
    return (y * gate.astype(f32)).astype(p_r.dtype)


def diff_attention_group(q, k, v, lam_q1, lam_k1, lam_q2, lam_k2, subln_w, lambda_init):
    B, T, _ = q.shape
    H, D = DIFF_HEADS, DIFF_QK
    f32 = jnp.float32
    nb = T // Q_BLOCK
    qh = q.reshape(B, nb, Q_BLOCK, H, 2, D).transpose(1, 0, 2, 3, 4, 5).astype(f32)
    kh = k.reshape(B, T, H, 2, D).astype(f32)
    vh = v.reshape(B, T, H, DIFF_V).astype(f32)
    lam = (jnp.exp(jnp.sum(lam_q1.astype(f32) * lam_k1.astype(f32)))
           - jnp.exp(jnp.sum(lam_q2.astype(f32) * lam_k2.astype(f32))) + lambda_init)
    scale = D ** -0.5
    key_pos = jnp.arange(T)

    def block(args):
        qi, i = args
        s = jnp.einsum('bqhcd,bkhcd->bhcqk', qi, kh) * scale
        q_pos = i * Q_BLOCK + jnp.arange(Q_BLOCK)
        mask = key_pos[None, :] <= q_pos[:, None]
        s = jnp.where(mask, s, -jnp.inf)
        p = jax.nn.softmax(s, axis=-1)
        pd = p[:, :, 0] - lam * p[:, :, 1]
        return jnp.einsum('bhqk,bkhe->bqhe', pd, vh)

    o = lax.map(block, (qh, jnp.arange(nb)))
    o = jnp.moveaxis(o, 0, 1).reshape(B, T, H, DIFF_V)
    o = o * lax.rsqrt(jnp.mean(o * o, -1, keepdims=True) + SUBLN_EPS) * subln_w.astype(f32)
    o = o * (1.0 - lambda_init)
    return o.reshape(B, T, DIFF_W).astype(q.dtype)


def setup_inputs(seed: int = 0) -> dict:
    key = jax.random.key(seed)
    ks = iter(jax.random.split(key, 48))
    L, D, F = DEPTH, D_MODEL, D_FF
    nrm = lambda shape, s: jax.random.normal(next(ks), shape, jnp.float32) * s
    gain = lambda shape: 1.0 + nrm(shape, 0.02)
    n = jnp.arange(RWKV_W, dtype=jnp.float32) / (RWKV_W - 1)
    decay_speed = -7.0 + 5.0 * n ** 0.85 + 0.5
    return {
        "x": jax.random.normal(next(ks), (BATCH, SEQ, D), jnp.float32),
        "ffn1_pre_g": gain((L, D)),
        "ffn1_post_g": gain((L, D)),
        "ffn1_w_gate": nrm((L, D, F), D ** -0.5),
        "ffn1_w_up": nrm((L, D, F), D ** -0.5),
        "ffn1_w_down": nrm((L, F, D), F ** -0.5),
        "mix_pre_g": gain((L, D)),
        "mix_post_g": gain((L, D)),
        "w_in": nrm((L, D, PROJ_W), D ** -0.5),
        "shift_mu": jax.random.uniform(next(ks), (L, SHIFT_COLS), jnp.float32),
        "w_o": nrm((L, MIX_W, D), MIX_W ** -0.5),
        "rwkv_w0": decay_speed[None, :] + nrm((L, RWKV_W), 0.1),
        "rwkv_w_up": nrm((L, DECAY_LORA, RWKV_W), 0.5 * DECAY_LORA ** -0.5),
        "rwkv_a0": nrm((L, RWKV_W), 0.1),
        "rwkv_a_up": nrm((L, ICLR_LORA, RWKV_W), 0.5 * ICLR_LORA ** -0.5),
        "rwkv_g_up": nrm((L, GATE_LORA, RWKV_W), GATE_LORA ** -0.5),
        "rwkv_k_k": 0.85 + nrm((L, RWKV_W), 0.02),
        "rwkv_k_a": gain((L, RWKV_W)),
        "rwkv_r_k": nrm((L, RWKV_HEADS, RWKV_HEAD), 0.1),
        "rwkv_gn_w": gain((L, RWKV_W)),
        "rwkv_gn_b": nrm((L, RWKV_W), 0.02),
        "diff_lam_q1": nrm((L, DIFF_QK), 0.1),
        "diff_lam_k1": nrm((L, DIFF_QK), 0.1),
        "diff_lam_q2": nrm((L, DIFF_QK), 0.1),
        "diff_lam_k2": nrm((L, DIFF_QK), 0.1),
        "diff_subln_w": gain((L, DIFF_V)),
        "ffn2_pre_g": gain((L, D)),
        "ffn2_post_g": gain((L, D)),
        "ffn2_w_gate": nrm((L, D, F), D ** -0.5),
        "ffn2_w_up": nrm((L, D, F), D ** -0.5),
        "ffn2_w_down": nrm((L, F, D), F ** -0.5),
    }


def reference(x, ffn1_pre_g, ffn1_post_g, ffn1_w_gate, ffn1_w_up, ffn1_w_down,
              mix_pre_g, mix_post_g, w_in, shift_mu, w_o,
              rwkv_w0, rwkv_w_up, rwkv_a0, rwkv_a_up, rwkv_g_up, rwkv_k_k, rwkv_k_a,
              rwkv_r_k, rwkv_gn_w, rwkv_gn_b,
              diff_lam_q1, diff_lam_k1, diff_lam_q2, diff_lam_k2, diff_subln_w,
              ffn2_pre_g, ffn2_post_g, ffn2_w_gate, ffn2_w_up, ffn2_w_down):
    for l in range(DEPTH):
        lambda_init = 0.8 - 0.6 * math.exp(-0.3 * l)
        h = rmsnorm(x, ffn1_pre_g[l])
        x = x + FFN_RES * rmsnorm(swiglu(h, ffn1_w_gate[l], ffn1_w_up[l], ffn1_w_down[l]), ffn1_post_g[l])
        h = rmsnorm(x, mix_pre_g[l])
        p = h @ w_in[l]
        p_rw, p_diff = p[..., :SHIFT_COLS], p[..., SHIFT_COLS:]
        p_rw = p_rw + (token_shift(p_rw) - p_rw) * shift_mu[l]
        p_r, p_k, p_v, p_wd, p_ad, p_gd = _split(p_rw, RWKV_COLS)
        q_d, k_d, v_d = _split(p_diff, DIFF_COLS)
        y_rwkv = rwkv7_group(p_r, p_k, p_v, p_wd, p_ad, p_gd,
                             rwkv_w0[l], rwkv_w_up[l], rwkv_a0[l], rwkv_a_up[l], rwkv_g_up[l],
                             rwkv_k_k[l], rwkv_k_a[l], rwkv_r_k[l], rwkv_gn_w[l], rwkv_gn_b[l])
        y_diff = diff_attention_group(q_d, k_d, v_d, diff_lam_q1[l], diff_lam_k1[l],
                                      diff_lam_q2[l], diff_lam_k2[l], diff_subln_w[l], lambda_init)
        y = jnp.concatenate([y_rwkv, y_diff], axis=-1) @ w_o[l]
        x = x + rmsnorm(y, mix_post_g[l])
        h = rmsnorm(x, ffn2_pre_g[l])
        x = x + FFN_RES * rmsnorm(swiglu(h, ffn2_w_gate[l], ffn2_w_up[l], ffn2_w_down[l]), ffn2_post_g[l])
    return x
```

```python
import os
import numpy as np
from contextlib import ExitStack
import concourse.bass as bass
import concourse.mybir as mybir
from concourse.bass_utils import run_bass_kernel_spmd

F32 = mybir.dt.float32
BF16 = mybir.dt.bfloat16
AF = mybir.ActivationFunctionType
ALU = mybir.AluOpType
AX = mybir.AxisListType

D = 1024
DFF = 2816
NFC = 22
T = 4096
TH = 2048
TT = 1024
NCOLS = 1824
NORM_EPS = 1e-6
GN_EPS = 64e-5
SUBLN_EPS = 1e-5
LAMBDA_INIT = 0.8 - 0.6 * 1.0
PAIRS = [[0, 1], [2, 3], [4, 5], [6, 7]]
NCST = 1160


class Buf:
    __slots__ = ("name", "lw", "rd")

    def __init__(self, name, lw=None):
        self.name = name
        self.lw = lw
        self.rd = []


class Op:
    __slots__ = ("eng", "fn", "deps", "idx", "dma", "target", "val", "dsem", "dval", "prewait", "inc")


class Sched:
    NDSEM = 12

    def __init__(self, nc, stack):
        self.nc = nc
        self.stack = stack
        self.streams = {"pe": [], "act": [], "dve": [], "pool": [], "sp": []}
        self.epoch = None
        self.bufs = []

    def buf(self, name):
        b = Buf(name, self.epoch)
        self.bufs.append(b)
        return b

    def add(self, eng, fn, reads=(), writes=(), dma=False, inc=None):
        op = Op()
        op.eng = eng
        op.fn = fn
        op.dma = dma
        op.target = False
        op.val = None
        op.prewait = None
        op.dsem = None
        op.inc = inc
        deps = set()
        for b in reads:
            if b.lw is not None:
                deps.add(b.lw)
        for b in writes:
            if b.lw is not None:
                deps.add(b.lw)
            deps.update(b.rd)
        if eng == "pe" and not dma:
            deps = {d for d in deps if not (d.eng == "pe" and not d.dma)}
        op.deps = deps
        for b in reads:
            b.rd.append(op)
        for b in writes:
            b.lw = op
            b.rd = []
        op.idx = len(self.streams[eng])
        self.streams[eng].append(op)
        return op

    def barrier(self):
        scr = self._scr
        op = self.add("dve", lambda e: e.memset(scr[0:1, 0:1], 0.0), writes=list(self.bufs))
        self.epoch = op
        self.bufs = []
        return op

    def emit(self):
        nc = self.nc
        st = self.stack
        sems = {e: st.enter_context(nc.semaphore("s_" + e)) for e in ("pe", "act", "dve", "pool")}
        dsems = {q: [st.enter_context(nc.semaphore("d_%s%d" % (q, i))) for i in range(self.NDSEM)]
                 for q in ("sp", "pool")}
        for e, ops in self.streams.items():
            for op in ops:
                best = {}
                dd = []
                for d in op.deps:
                    if d.dma:
                        dd.append(d)
                    elif d.eng not in best or best[d.eng].idx < d.idx:
                        best[d.eng] = d
                op.deps = list(best.values()) + dd
                for d in op.deps:
                    d.target = True
        lastops = []
        for e in ("pe", "act", "dve", "pool"):
            comp = [op for op in self.streams[e] if not op.dma]
            if comp:
                comp[-1].target = True
                lastops.append(comp[-1])
            c = 0
            for op in self.streams[e]:
                if op.dma:
                    continue
                if op.target:
                    c += 1
                    op.val = c
        final_waits = []
        for q in ("sp", "pool"):
            i = 0
            last = {}
            for op in self.streams[q]:
                if not op.dma:
                    continue
                k = i % self.NDSEM
                inc = op.inc if op.inc is not None else 16
                prev = last.get(k, 0)
                op.dsem = dsems[q][k]
                op.dval = prev + inc
                if prev > 0:
                    op.prewait = (op.dsem, prev)
                last[k] = op.dval
                i += 1
            final_waits += [(dsems[q][k], v) for k, v in last.items()]
        engobj = {"pe": "tensor", "act": "scalar", "dve": "vector", "pool": "gpsimd", "sp": "sync"}
        self.nwaits = 0
        self.ninst = 0

        def run_stream(e, eng):
            waited = {}

            def wait(sem, val):
                k = id(sem)
                if waited.get(k, 0) >= val:
                    return
                waited[k] = val
                eng.wait_ge(sem, val)
                self.nwaits += 1
            for op in self.streams[e]:
                if op.prewait is not None:
                    wait(*op.prewait)
                for d in op.deps:
                    if d.dma:
                        wait(d.dsem, d.dval)
                    else:
                        wait(sems[d.eng], d.val)
                ins = op.fn(eng)
                self.ninst += 1
                if op.dma:
                    ins.then_inc(op.dsem, op.inc if op.inc is not None else 16)
                elif op.target:
                    ins.then_inc(sems[e], 1)
            if e == "sp":
                for s, v in final_waits:
                    wait(s, v)
                for lo in lastops:
                    wait(sems[lo.eng], lo.val)

        with nc.Block() as block:
            for e in ("sp", "pool", "act", "dve", "pe"):
                getattr(block, engobj[e])(lambda eng, e=e: run_stream(e, eng))


class Tn:
    __slots__ = ("ap", "b")

    def __init__(self, ap, b):
        self.ap = ap
        self.b = b

    def __getitem__(self, k):
        return self.ap[k]


class Arena:
    def __init__(self, S, ap, nwords):
        self.S = S
        self.ap = ap
        self.n = nwords
        self.off = 0
        self.peak = 0

    def alloc(self, name, shape, dt):
        free = 1
        for s in shape[1:]:
            free *= s
        esz = 4 if dt == F32 else 2
        words = (free * esz + 3) // 4
        words = (words + 7) // 8 * 8
        assert self.off + words <= self.n, "arena overflow at %s: %d + %d > %d" % (name, self.off, words, self.n)
        v = self.ap[:, self.off:self.off + words]
        self.off += words
        self.peak = max(self.peak, self.off)
        if dt != F32:
            v = v.bitcast(dt)
        v = v[0:shape[0], 0:free]
        if len(shape) == 3:
            v = v.rearrange("p (a b) -> p a b", a=shape[1])
        elif len(shape) == 4:
            v = v.rearrange("p (a b c) -> p a b c", a=shape[1], b=shape[2])
        return Tn(v, self.S.buf(name))

    def allocn(self, name, n, shape, dt):
        return [self.alloc("%s%d" % (name, i), shape, dt) for i in range(n)]


def _bl(x):
    out = []
    for t in x:
        if t is None:
            continue
        out.append(t.b if isinstance(t, Tn) else t)
    return out


class KB:
    def __init__(self, nc, st, mode):
        self.nc = nc
        self.mode = mode
        self.S = Sched(nc, st)
        S = self.S
        self.banks = []
        for i in range(8):
            t = st.enter_context(nc.psum_tensor("psb%d" % i, [128, 512], F32))
            self.banks.append(Tn(t, None))
        self.bank_i = 0
        self.reserved = set()
        arena_words = 53208 - NCST - 16
        at = st.enter_context(nc.sbuf_tensor("arena", [128, arena_words], F32))
        self.cst = st.enter_context(nc.sbuf_tensor("cst_sb", [128, NCST], F32))
        self.A = Arena(S, at, arena_words)
        self.bscr = st.enter_context(nc.sbuf_tensor("bscr", [128, 8], F32))
        S._scr = self.bscr
        self.new_epoch_banks()

    def new_epoch_banks(self):
        for t in self.banks:
            t.b = self.S.buf("bank")

    def bank(self):
        while True:
            i = self.bank_i
            self.bank_i = (self.bank_i + 1) % 8
            if i not in self.reserved:
                return self.banks[i]

    def mm(self, out, lhsT, rhs, start=True, stop=True, r=(), w=(), tp=None):
        kw = {} if tp is None else {"tile_position": tp}
        self.S.add("pe", lambda e: e.matmul(out, lhsT=lhsT, rhs=rhs, start=start, stop=stop, **kw),
                   reads=_bl(r), writes=_bl(w))

    def tr(self, out, in_, ident, r=(), w=()):
        self.S.add("pe", lambda e: e.transpose(out, in_, ident), reads=_bl(r), writes=_bl(w))

    def act(self, out, in_, func, r=(), w=(), bias=None, scale=None):
        kw = {}
        if bias is not None:
            kw["bias"] = bias
        if scale is not None:
            kw["scale"] = scale
        self.S.add("act", lambda e: e.activation(out=out, in_=in_, func=func, **kw), reads=_bl(r), writes=_bl(w))

    def ts(self, out, in0, s1, s2, op0, op1=None, r=(), w=(), eng="dve"):
        if op1 is None:
            self.S.add(eng, lambda e: e.tensor_scalar(out=out, in0=in0, scalar1=s1, scalar2=None, op0=op0),
                       reads=_bl(r), writes=_bl(w))
        else:
            self.S.add(eng, lambda e: e.tensor_scalar(out=out, in0=in0, scalar1=s1, scalar2=s2, op0=op0, op1=op1),
                       reads=_bl(r), writes=_bl(w))

    def tt(self, out, in0, in1, op, r=(), w=(), eng="dve"):
        self.S.add(eng, lambda e: e.tensor_tensor(out=out, in0=in0, in1=in1, op=op), reads=_bl(r), writes=_bl(w))

    def stt(self, out, in0, scalar, in1, op0, op1, r=(), w=()):
        self.S.add("dve", lambda e: e.scalar_tensor_tensor(out=out, in0=in0, scalar=scalar, in1=in1, op0=op0, op1=op1),
                   reads=_bl(r), writes=_bl(w))

    def cp(self, out, in_, r=(), w=(), eng="dve"):
        if eng == "act":
            self.S.add("act", lambda e: e.copy(out, in_), reads=_bl(r), writes=_bl(w))
        else:
            self.S.add(eng, lambda e: e.tensor_copy(out=out, in_=in_), reads=_bl(r), writes=_bl(w))

    def dma(self, q, out, in_, r=(), w=()):
        self.S.add(q, lambda e: e.dma_start(out=out, in_=in_), reads=_bl(r), writes=_bl(w), dma=True)


def make_consts():
    c = np.zeros((128, NCST), np.float32)
    c[:, 0:128] = np.eye(128)
    c[:, 128:256] = 1.0
    bo = np.zeros((128, 128), np.float32)
    bo[0:64, 0:64] = 1.0
    bo[64:, 64:] = 1.0
    c[:, 256:384] = bo
    k = np.arange(128)[:, None]
    q = np.arange(128)[None, :]
    c[:, 384:512] = np.where(k > q, -30000.0, 0.0)
    c[:, 512:640] = (k < q).astype(np.float32)
    c[:, 640:768] = (k <= q).astype(np.float32)
    c[:, 768] = D * NORM_EPS
    c[:, 769] = 128 * SUBLN_EPS
    c[:, 770] = 64 * GN_EPS
    c[:, 771] = 0.0
    c[:, 772] = 1.0
    c[:, 776:904] = (k <= q).astype(np.float32) * (-float(np.exp(-0.5)))
    c[:, 904:1032] = (k < q).astype(np.float32) * (-float(np.exp(-0.5)))
    c[:, 1032:1160] = (q < k).astype(np.float32)
    return c


P_F1PRE, P_F1POST, P_MPRE, P_MPOST, P_F2PRE, P_F2POST = 0, 8, 16, 24, 32, 40
P_MU = 48
P_KK = 57
P_KA = 59
P_A0 = 61
P_RK = 63
P_GNW = 65
P_GNB = 67
P_SUBLN = 69
P_SEL = 70
NPRM = 72
Q_F1PRE, Q_F1POST, Q_MPRE, Q_MPOST, Q_F2PRE, Q_F2POST = 0, 8, 16, 24, 32, 40
Q_OMU = 48
Q_SUBLN = 57
Q_LAM = 58
Q_NLAM = 59
Q_GNW8 = 60
Q_A0H = 62
NDER = 64


def build(mode="full"):
    nc = bass.Bass("TRN2", target_bir_lowering=False)
    ph1 = mode in ("full", "p1")
    ph2 = mode in ("full", "p2")
    ph3 = mode in ("full", "p3")

    def dram(name, shape, dt, kind):
        if kind == "Internal":
            return nc.dram_tensor(name, shape, dt).ap()
        return nc.dram_tensor(name, shape, dt, kind=kind).ap()

    IN, OUT, INT = "ExternalInput", "ExternalOutput", "Internal"
    io = {}
    io["cst"] = dram("cst", [128, NCST], F32, IN)
    io["prm"] = dram("prm", [128, NPRM], F32, IN)
    if ph1:
        io["xT"] = dram("xT", [D, TH], F32, IN)
        io["f1g"] = dram("f1g", [D, DFF], F32, IN)
        io["f1u"] = dram("f1u", [D, DFF], F32, IN)
        io["f1d"] = dram("f1d", [DFF, D], F32, IN)
    if ph2:
        io["win"] = dram("win", [D, NCOLS], F32, IN)
        io["wupw"] = dram("wupw", [128, 256], F32, IN)
        io["aup"] = dram("aup", [128, 256], F32, IN)
        io["gup"] = dram("gup", [128, 2, 256], F32, IN)
        io["lamv"] = dram("lamv", [1, 256], F32, IN)
    if ph3:
        io["wo"] = dram("wo", [D, D], F32, IN)
        io["f2g"] = dram("f2g", [D, DFF], F32, IN)
        io["f2u"] = dram("f2u", [D, DFF], F32, IN)
        io["f2d"] = dram("f2d", [DFF, D], F32, IN)
        io["outT"] = dram("outT", [D, TH], F32, OUT)
    def parts(name, n, shape, dt, kind):
        for i in range(n):
            io["%s%d" % (name, i)] = dram("%s%d" % (name, i), shape, dt, kind)

    if mode == "full":
        io["x1T"] = dram("x1T", [D, TH], F32, INT)
        parts("hb", 2, [D, TT], BF16, INT)
        parts("hg", 2, [2 * D, TT], BF16, INT)
        parts("yb", 4, [512, 1024], BF16, INT)
        parts("yg", 4, [1024, 1024], BF16, INT)
    elif mode == "p1":
        io["x1T"] = dram("x1T", [D, TH], F32, OUT)
        parts("hb", 2, [D, TT], BF16, OUT)
    elif mode == "p2":
        parts("hg", 2, [2 * D, TT], BF16, IN)
        parts("yb", 4, [512, 1024], BF16, OUT)
    elif mode == "p3":
        io["x1T"] = dram("x1T", [D, TH], F32, IN)
        parts("yg", 4, [1024, 1024], BF16, IN)

    with ExitStack() as st:
        kb = KB(nc, st, mode)
        kb.io = io
        prologue(kb)
        if ph1:
            phase_ffn(kb, 1)
        if ph2:
            kb.S.barrier()
            kb.new_epoch_banks()
            kb.A.off = kb.base_off
            phase_mixer(kb)
        if ph3:
            kb.S.barrier()
            kb.new_epoch_banks()
            kb.A.off = kb.base_off
            phase_ffn(kb, 3)
        kb.S.emit()
        kb.stats = (kb.S.ninst, kb.S.nwaits, kb.A.peak)
    return nc, kb


def collective(kb, srcname, dstname):
    S = kb.S
    if kb.mode != "full":
        return
    src, dst = kb.io[srcname], kb.io[dstname]
    bs = kb.dbuf[srcname]
    bd = kb.dbuf[dstname]
    S.add("pool", lambda e: e.collective_compute("AllGather", ALU.bypass, replica_groups=PAIRS,
                                                 ins=[src[:, :]], outs=[dst[:, :]]),
          reads=[bs], writes=[bd], dma=True, inc=1)


def prologue(kb):
    S, A, io = kb.S, kb.A, kb.io
    cb = S.buf("cst")
    kb.cstb = cb
    kb.dma("sp", kb.cst[:, :], io["cst"][:, :], w=[cb])
    kb.prm = A.alloc("prm", [128, NPRM], F32)
    kb.der = A.alloc("der", [128, NDER], F32)
    kb.dma("sp", kb.prm[:, :], io["prm"][:, :], w=[kb.prm])
    kb.cbf = A.alloc("cbf", [128, 512], BF16)
    kb.cp(kb.cbf[:, :], kb.cst[:, 0:512], r=[cb], w=[kb.cbf])
    kb.identb = kb.cbf.ap[:, 0:128]
    kb.onesb = kb.cbf.ap[:, 128:256]
    kb.blockb = kb.cbf.ap[:, 256:384]
    kb.maskb = kb.cbf.ap[:, 384:512]
    p, d = kb.prm, kb.der
    rw = dict(r=[p], w=[d])
    kb.ts(d[:, Q_F1PRE:Q_F1PRE + 8], p[:, P_F1PRE:P_F1PRE + 8], 32.0, None, ALU.mult, **rw)
    kb.ts(d[:, Q_F1POST:Q_F1POST + 8], p[:, P_F1POST:P_F1POST + 8], 16.0, None, ALU.mult, **rw)
    kb.ts(d[:, Q_MPRE:Q_MPRE + 8], p[:, P_MPRE:P_MPRE + 8], 32.0, None, ALU.mult, **rw)
    kb.ts(d[:, Q_MPOST:Q_MPOST + 8], p[:, P_MPOST:P_MPOST + 8], 32.0, None, ALU.mult, **rw)
    kb.ts(d[:, Q_F2PRE:Q_F2PRE + 8], p[:, P_F2PRE:P_F2PRE + 8], 32.0, None, ALU.mult, **rw)
    kb.ts(d[:, Q_F2POST:Q_F2POST + 8], p[:, P_F2POST:P_F2POST + 8], 16.0, None, ALU.mult, **rw)
    kb.ts(d[:, Q_OMU:Q_OMU + 9], p[:, P_MU:P_MU + 9], -1.0, 1.0, ALU.mult, ALU.add, **rw)
    kb.ts(d[:, Q_SUBLN:Q_SUBLN + 1], p[:, P_SUBLN:P_SUBLN + 1], (1.0 - LAMBDA_INIT) * float(np.sqrt(128.0)), None,
          ALU.mult, **rw)
    kb.ts(d[:, Q_A0H:Q_A0H + 2], p[:, P_A0:P_A0 + 2], 0.5, None, ALU.mult, **rw)
    kb.dbuf = {k: S.buf(k) for k in ["x1T"] + ["hb%d" % i for i in range(2)] + ["hg%d" % i for i in range(2)]
               + ["yb%d" % i for i in range(4)] + ["yg%d" % i for i in range(4)]}
    kb.base_off = A.off


def rms_rstd(kb, sq, sqr, rstd, ncols, nchunk=8, epscol=768):
    bk = kb.bank()
    for dc in range(nchunk):
        kb.mm(bk[:, 0:ncols], kb.onesb, sq[:, dc, 0:ncols], start=(dc == 0), stop=(dc == nchunk - 1),
              r=[kb.cbf] + sqr, w=[bk])
    kb.act(rstd[:, 0:ncols], bk[:, 0:ncols], AF.Sqrt, r=[bk, kb.cstb], w=[rstd], bias=kb.cst[:, epscol:epscol + 1], scale=1.0)
    kb.S.add("dve", lambda e: e.reciprocal(out=rstd[:, 0:ncols], in_=rstd[:, 0:ncols]), reads=_bl([rstd]), writes=_bl([rstd]))


def phase_ffn(kb, which):
    S, A, io = kb.S, kb.A, kb.io
    d = kb.der
    NTS = TT // 512
    xt = [[A.alloc("xt%d_%d" % (dc, t_), [128, 512], F32) for t_ in range(NTS)] for dc in range(8)]
    fo = [[A.alloc("fo%d_%d" % (dc, t_), [128, 512], F32) for t_ in range(NTS)] for dc in range(8)]
    hT = A.alloc("hT", [128, 8, TT], BF16)
    AT = [A.alloc("AT%d" % i, [128, TT], BF16) for i in range(NFC)]
    wg = A.allocn("wg", 3, [128, 8, 256], BF16)
    wu = A.allocn("wu", 3, [128, 8, 256], BF16)
    wd = A.allocn("wd", 2, [128, NFC, 256], BF16)
    sq = A.alloc("sq", [128, 8, 512], BF16)
    rstd = A.allocn("rstd", 2, [128, 512], F32)
    xt_all = [t for row in xt for t in row]
    if which == 3:
        wo = A.alloc("wo", [128, 8, D], BF16)
        kb.dma("pool", wo[:, :, :], io["wo"].rearrange("(kc p) f -> p kc f", p=128), w=[wo])
        Wg, Wu, Wd = io["f2g"], io["f2u"], io["f2d"]
        qpre, qpost = Q_F2PRE, Q_F2POST
    else:
        Wg, Wu, Wd = io["f1g"], io["f1u"], io["f1d"]
        qpre, qpost = Q_F1PRE, Q_F1POST

    def load_gu(fg):
        s = fg % 3
        kb.dma("pool", wg[s][:, :, :], Wg[:, fg * 256:(fg + 1) * 256].rearrange("(kc p) f -> p kc f", p=128), w=[wg[s]])
        kb.dma("pool", wu[s][:, :, :], Wu[:, fg * 256:(fg + 1) * 256].rearrange("(kc p) f -> p kc f", p=128), w=[wu[s]])

    def load_d(dcp):
        s = dcp % 2
        kb.dma("pool", wd[s][:, :, :], Wd[:, dcp * 256:(dcp + 1) * 256].rearrange("(fc p) d -> p fc d", p=128), w=[wd[s]])

    def norm_to_bf16(src, qcol, dst):
        for ts_ in range(NTS):
            cols = slice(ts_ * 512, (ts_ + 1) * 512)
            for dc in range(8):
                kb.act(sq[:, dc, :], src[dc][ts_][:, :], AF.Square, r=[src[dc][ts_]], w=[sq])
            rs = rstd[ts_ % 2]
            rms_rstd(kb, sq, [sq], rs, 512)
            for dc in range(8):
                kb.stt(dst[:, dc, cols], src[dc][ts_][:, :], d[:, qcol + dc:qcol + dc + 1], rs[:, :], ALU.mult, ALU.mult,
                       r=[src[dc][ts_], d, rs], w=[dst])

    def post_norm_residual(qcol):
        for ts_ in range(NTS):
            for dc in range(8):
                kb.act(sq[:, dc, :], fo[dc][ts_][:, :], AF.Square, r=[fo[dc][ts_]], w=[sq])
            rs = rstd[ts_ % 2]
            rms_rstd(kb, sq, [sq], rs, 512)
            for dc in range(8):
                f_, x_ = fo[dc][ts_], xt[dc][ts_]
                kb.stt(f_[:, :], f_[:, :], d[:, qcol + dc:qcol + dc + 1], rs[:, :], ALU.mult, ALU.mult,
                       r=[f_, d, rs], w=[f_])
                kb.tt(x_[:, :], x_[:, :], f_[:, :], ALU.add, r=[x_, f_], w=[x_])

    def xt_dma(dram_ap, tcols, to_dram, r=(), w=()):
        for dc in range(8):
            for ts_ in range(NTS):
                c0 = tcols.start + ts_ * 512
                dr = dram_ap[dc * 128:(dc + 1) * 128, c0:c0 + 512]
                if to_dram:
                    kb.dma("sp", dr, xt[dc][ts_][:, :], r=[xt[dc][ts_]] + list(r), w=list(w))
                else:
                    kb.dma("sp", xt[dc][ts_][:, :], dr, r=list(r), w=[xt[dc][ts_]] + list(w))

    ntile = TH // TT
    for ti in range(ntile):
        tcols = slice(ti * TT, (ti + 1) * TT)
        if which == 1:
            xt_dma(io["xT"], tcols, False)
        else:
            xt_dma(io["x1T"], tcols, False, r=[kb.dbuf["x1T"]])
        load_gu(0)
        load_gu(1)
        if which == 3:
            yA = [AT[i] for i in range(0, 8)]
            yB = [AT[i] for i in range(8, 16)]
            for kc in range(8):
                kb.dma("sp", yA[kc][:, :], io["yg%d" % ti][kc * 128:(kc + 1) * 128, :],
                       r=[kb.dbuf["yg%d" % ti]], w=[yA[kc]])
                kb.dma("sp", yB[kc][:, :], io["yg%d" % (2 + ti)][kc * 128:(kc + 1) * 128, :],
                       r=[kb.dbuf["yg%d" % (2 + ti)]], w=[yB[kc]])
            p = kb.prm
            for kc in range(8):
                kb.ts(yA[kc][:, :], yA[kc][:, :], p[:, P_SEL:P_SEL + 1], None, ALU.mult, r=[yA[kc], p], w=[yA[kc]])
                kb.stt(hT[:, kc, :], yB[kc][:, :], p[:, P_SEL + 1:P_SEL + 2], yA[kc][:, :], ALU.mult, ALU.add,
                       r=[yB[kc], yA[kc], p], w=[hT])
            for dc in range(8):
                for ts_ in range(NTS):
                    cols = slice(ts_ * 512, (ts_ + 1) * 512)
                    bk = kb.bank()
                    for kc in range(8):
                        kb.mm(bk[:, :], wo[:, kc, dc * 128:(dc + 1) * 128], hT[:, kc, cols], start=(kc == 0), stop=(kc == 7),
                              r=[wo, hT], w=[bk])
                    kb.cp(fo[dc][ts_][:, :], bk[:, :], r=[bk], w=[fo[dc][ts_]], eng="act")
            post_norm_residual(Q_MPOST)
        norm_to_bf16(xt, qpre, hT)
        load_d(0)
        load_d(1)
        sgi = 0
        fo_all = [t for row in fo for t in row]
        for fg in range(NFC // 2):
            s = fg % 3
            for fi in range(2):
                fc = fg * 2 + fi
                bg = [kb.bank() for _ in range(NTS)]
                bu = [kb.bank() for _ in range(NTS)]
                for (wt, bks) in ((wg[s], bg), (wu[s], bu)):
                    for ts_ in range(NTS):
                        for kc in range(8):
                            kb.mm(bks[ts_][:, :], wt[:, kc, fi * 128:(fi + 1) * 128], hT[:, kc, ts_ * 512:(ts_ + 1) * 512],
                                  start=(kc == 0), stop=(kc == 7), r=[wt, hT], w=[bks[ts_]])
                for ts_ in range(NTS):
                    sg = fo_all[sgi % 16]
                    sgi += 1
                    kb.act(sg[:, :], bg[ts_][:, :], AF.Silu, r=[bg[ts_]], w=[sg])
                    kb.tt(AT[fc][:, ts_ * 512:(ts_ + 1) * 512], sg[:, :], bu[ts_][:, :], ALU.mult, r=[sg, bu[ts_]], w=[AT[fc]])
            if fg + 2 < NFC // 2:
                load_gu(fg + 2)
        for dcp in range(4):
            s = dcp % 2
            for di in range(2):
                dc = dcp * 2 + di
                for ts_ in range(NTS):
                    cols = slice(ts_ * 512, (ts_ + 1) * 512)
                    bk = kb.bank()
                    for fc in range(NFC):
                        kb.mm(bk[:, :], wd[s][:, fc, di * 128:(di + 1) * 128], AT[fc][:, cols], start=(fc == 0), stop=(fc == NFC - 1),
                              r=[wd[s], AT[fc]], w=[bk])
                    kb.cp(fo[dc][ts_][:, :], bk[:, :], r=[bk], w=[fo[dc][ts_]], eng="act")
            if dcp + 2 < 4:
                load_d(dcp + 2)
        post_norm_residual(qpost)
        if which == 1:
            xt_dma(io["x1T"], tcols, True, w=[kb.dbuf["x1T"]])
            norm_to_bf16(xt, Q_MPRE, hT)
            kb.dma("sp", io["hb%d" % ti][:, :].rearrange("(dc p) t -> p dc t", p=128), hT[:, :, :], r=[hT], w=[kb.dbuf["hb%d" % ti]])
            collective(kb, "hb%d" % ti, "hg%d" % ti)
        else:
            xt_dma(io["outT"], tcols, True)


def phase_mixer(kb):
    S, A, io = kb.S, kb.A, kb.io
    d, p = kb.der, kb.prm
    cst, cstb = kb.cst, kb.cstb
    TT2 = 512
    NT2 = T // TT2
    ones32 = cst[:, 128:256]
    block32 = cst[:, 256:384]
    tri32 = cst[:, 776:1032]

    win = A.alloc("win", [128, 8, NCOLS], BF16)
    kb.dma("pool", win[:, :, :], io["win"].rearrange("(kc p) f -> p kc f", p=128), w=[win])
    wupw = A.alloc("wupw", [128, 256], F32)
    kb.dma("sp", wupw[:, :], io["wupw"][:, :], w=[wupw])
    aupb = A.alloc("aupb", [128, 256], BF16)
    kb.dma("pool", aupb[:, :], io["aup"][:, :], w=[aupb])
    gupb = A.alloc("gupb", [128, 2, 256], BF16)
    kb.dma("pool", gupb[:, :, :], io["gup"][:, :, :], w=[gupb])
    lamv = A.alloc("lamv", [128, 256], F32)
    kb.dma("sp", lamv[:, :], io["lamv"].partition_broadcast(128), w=[lamv])
    ltmp = A.alloc("ltmp", [128, 128], F32)
    lsum = A.alloc("lsum", [128, 2], F32)
    kb.tt(ltmp[:, 0:64], lamv[:, 0:64], lamv[:, 64:128], ALU.mult, r=[lamv], w=[ltmp])
    kb.tt(ltmp[:, 64:128], lamv[:, 128:192], lamv[:, 192:256], ALU.mult, r=[lamv], w=[ltmp])
    S.add("dve", lambda e: e.reduce_sum(out=lsum[:, 0:1], in_=ltmp[:, 0:64], axis=AX.X), reads=_bl([ltmp]), writes=_bl([lsum]))
    S.add("dve", lambda e: e.reduce_sum(out=lsum[:, 1:2], in_=ltmp[:, 64:128], axis=AX.X), reads=_bl([ltmp]), writes=_bl([lsum]))
    kb.act(lsum[:, :], lsum[:, :], AF.Exp, r=[lsum], w=[lsum])
    kb.tt(d[:, Q_LAM:Q_LAM + 1], lsum[:, 0:1], lsum[:, 1:2], ALU.subtract, r=[lsum], w=[d])
    kb.ts(d[:, Q_LAM:Q_LAM + 1], d[:, Q_LAM:Q_LAM + 1], LAMBDA_INIT, None, ALU.add, r=[d], w=[d])
    kb.ts(d[:, Q_NLAM:Q_NLAM + 1], d[:, Q_LAM:Q_LAM + 1], -1.0, None, ALU.mult, r=[d], w=[d])
    kb.ts(d[:, Q_GNW8:Q_GNW8 + 2], p[:, P_GNW:P_GNW + 2], 8.0, None, ALU.mult, r=[p], w=[d])
    mask512 = A.alloc("mask512", [128, 2, 256], BF16)
    for h in range(2):
        kb.cp(mask512[:, h, :], cst[:, 512:768], r=[cstb], w=[mask512])
    lowm = A.alloc("lowm", [128, 128], BF16)
    kb.cp(lowm[:, :], cst[:, 1032:1160], r=[cstb], w=[lowm])
    i2 = A.alloc("i2", [128, 64], F32)
    kb.tt(i2[:, :], cst[:, 0:64], cst[:, 64:128], ALU.add, r=[cstb], w=[i2])
    identb, onesb, blockb, maskb = kb.identb, kb.onesb, kb.blockb, kb.maskb
    cbf = kb.cbf

    KT = A.allocn("KT", 2, [128, T], BF16)
    Vtm = A.alloc("Vtm", [128, T // 128, 256], BF16)
    hT = A.alloc("hT2", [128, 8, TT2], BF16)
    rawt = A.allocn("rawt", 2, [128, TT2 + 1], F32)
    carry = A.alloc("carry", [128, 9], F32)
    S.add("dve", lambda e: e.memset(carry[:, :], 0.0), writes=_bl([carry]))
    psh = A.allocn("psh", 3, [128, TT2], F32)
    pl = A.allocn("pl", 2, [128, TT2], F32)
    tanhwd = A.alloc("tanhwd", [128, TT2], F32)
    lorab = A.alloc("lorab", [128, TT2], BF16)
    sgd0 = A.alloc("sgd0", [128, TT2], BF16)
    sgd1 = A.alloc("sgd1", [128, TT2], BF16)
    sgw = A.alloc("sgw", [128, 4, 256], F32)
    S.add("dve", lambda e: e.memset(tanhwd[:, :], 0.0), writes=_bl([tanhwd]))
    S.add("dve", lambda e: e.memset(tanhwd[64:65, :], 1.0), writes=_bl([tanhwd]))
    S.add("dve", lambda e: e.memset(lorab[:, :], 0.0), writes=_bl([lorab]))
    S.add("dve", lambda e: e.memset(sgd1[:, :], 0.0), writes=_bl([sgd1]))
    Qz = [A.allocn("Qz%d_" % hd, 2, [128, TT2], BF16) for hd in range(2)]
    for hd in range(2):
        for c in range(2):
            S.add("dve", lambda e, hd=hd, c=c: e.memset(Qz[hd][c][:, :], 0.0), writes=_bl([Qz[hd][c]]))
    ft = A.allocn("ft", 6, [128, TT2], F32)
    E12 = A.alloc("E12", [128, 4, 256], F32)
    E3 = A.alloc("E3", [128, 4, 128], F32)
    E4 = A.alloc("E4", [128, 4, 128], F32)
    gC = A.allocn("gC", 2, [128, 4], F32)
    bonv = A.allocn("bonv", 2, [128, TT2], F32)
    sqb = A.alloc("sqb", [128, TT2], BF16)
    ARl = A.allocn("AR", 2, [128, 4 * 2 * 2 * 128], BF16)
    AR = [Tn(t_.ap.rearrange("p (c h a x) -> p c h a x", c=4, h=2, a=2), t_.b) for t_ in ARl]
    for hp in range(2):
        S.add("dve", lambda e, hp=hp: e.memset(ARl[hp][:, :], 0.0), writes=_bl([ARl[hp]]))
    atT = A.allocn("atT", 2, [128, TT2], BF16)
    rtT = A.allocn("rtT", 2, [128, TT2], BF16)
    btT = A.allocn("btT", 2, [128, TT2], BF16)
    ktT = A.allocn("ktT", 2, [128, TT2], BF16)
    bhT = A.allocn("bhT", 2, [128, TT2], BF16)
    khT = A.allocn("khT", 2, [128, TT2], BF16)
    vTb = A.allocn("vTb", 2, [128, TT2], BF16)
    TM4 = [[A.alloc("TM4_%d_%d" % (hp, ck), [128, 4, 128], BF16) for ck in range(4)] for hp in range(2)]
    NSL = 4
    A1m = A.allocn("A1m", NSL, [128, 2, 256], BF16)
    A2m = A.allocn("A2m", NSL, [128, 2, 256], BF16)
    QT0 = A.allocn("QT0", NSL, [128, 2, 128], BF16)
    QX = [A.allocn("QX%d_" % s_, 2, [128, 2, 256], BF16) for s_ in range(NSL)]
    MTb = [A.allocn("MT%d_" % s_, 2, [128, 2, 128], BF16) for s_ in range(NSL)]
    Wl = A.allocn("Wl", NSL, [128, 2, 64], BF16)
    AU = A.allocn("AU", NSL, [128, 2, 128], BF16)
    RpT = A.allocn("RpT", NSL, [128, 128], BF16)
    Sloc = A.allocn("Sloc", NSL, [128, 64], F32)
    STb = A.allocn("STb", 2, [128, 64], BF16)
    STk = A.allocn("STk", 2, [128, 128], BF16)
    PTk = A.allocn("PTk", NSL, [128, 128], BF16)
    for hp in range(2):
        S.add("dve", lambda e, hp=hp: e.memset(STb[hp][:, :], 0.0), writes=_bl([STb[hp]]))
        S.add("dve", lambda e, hp=hp: e.memset(STk[hp][:, :], 0.0), writes=_bl([STk[hp]]))
    for s_ in range(NSL):
        S.add("dve", lambda e, s_=s_: e.memset(PTk[s_][:, :], 0.0), writes=_bl([PTk[s_]]))
    YT = A.allocn("YT", 2, [128, TT2], F32)
    ybf = A.allocn("ybf", 2, [128, TT2], BF16)
    PTb = A.allocn("PTb", 3, [128, TT2], BF16)
    osb = [ft[3], ft[4]]
    rec = ft[5]
    od = ft[0]
    ydb = A.allocn("ydb", 2, [128, TT2], BF16)

    def sigmoid_from(out, in_, r, w, tmp, bias=None):
        if bias is None:
            kb.act(tmp, in_, AF.Tanh, r=r, w=w, scale=0.5)
        else:
            kb.act(tmp, in_, AF.Tanh, r=r, w=w, bias=bias, scale=0.5)
        kb.ts(out, tmp, 0.5, 0.5, ALU.mult, ALU.add, r=w, w=w)

    def project_shift(c, col0, M, dst, ti):
        bk = kb.bank()
        for kc in range(8):
            kb.mm(bk[0:M, :], win[:, kc, col0:col0 + M], hT[:, kc, :], start=(kc == 0), stop=(kc == 7), r=[win, hT], w=[bk])
        rt = rawt[c % 2]
        kb.cp(rt[0:M, 1:TT2 + 1], bk[0:M, :], r=[bk], w=[rt], eng="act")
        kb.cp(rt[0:M, 0:1], carry[0:M, c:c + 1], r=[carry], w=[rt])
        kb.ts(dst[0:M, :], rt[0:M, 0:TT2], p[0:M, P_MU + c:P_MU + c + 1], None, ALU.mult, r=[rt, p], w=[dst])
        kb.stt(dst[0:M, :], rt[0:M, 1:TT2 + 1], d[0:M, Q_OMU + c:Q_OMU + c + 1], dst[0:M, :], ALU.mult, ALU.add,
               r=[rt, d, dst], w=[dst])
        kb.cp(carry[0:M, c:c + 1], rt[0:M, TT2:TT2 + 1], r=[rt], w=[carry])

    STOP = float(os.environ.get('K_STOP', '99'))
    NTI = int(os.environ.get('K_NTI', str(NT2)))
    for ti in range(NTI):
        t0 = ti * TT2
        rk, tok = ti // 4, (ti % 4) * TT2
        hpart, c0 = tok // TT, tok % TT
        kb.dma("sp", hT[:, :, :], io["hg%d" % hpart][rk * D:(rk + 1) * D, c0:c0 + TT2].rearrange("(dc p) t -> p dc t", p=128),
               r=[kb.dbuf["hg%d" % hpart]], w=[hT])
        SK = os.environ.get('K_SKIP', '')
        for hd in range(2):
            if 'q' in SK:
                continue
            bk = kb.bank()
            for kc in range(8):
                kb.mm(bk[:, :], win[:, kc, 1056 + hd * 128:1056 + (hd + 1) * 128], hT[:, kc, :], start=(kc == 0), stop=(kc == 7),
                      r=[win, hT], w=[bk])
            kb.cp(Qz[hd][0][0:64, :], bk[0:64, :], r=[bk], w=[Qz[hd][0]], eng="act")
            kb.cp(Qz[hd][1][64:128, :], bk[64:128, :], r=[bk], w=[Qz[hd][1]], eng="act")
            bk = kb.bank()
            for kc in range(8):
                kb.mm(bk[:, :], win[:, kc, 1312 + hd * 128:1312 + (hd + 1) * 128], hT[:, kc, :], start=(kc == 0), stop=(kc == 7),
                      r=[win, hT], w=[bk])
            kb.cp(KT[hd][:, t0:t0 + TT2], bk[:, :], r=[bk], w=[KT[hd]], eng="act")
        for blk in range(4):
            if 'v' in SK:
                continue
            bk = kb.bank()
            for kc in range(8):
                kb.mm(bk[:, 0:256], hT[:, kc, blk * 128:(blk + 1) * 128], win[:, kc, 1568:1824], start=(kc == 0), stop=(kc == 7),
                      r=[win, hT], w=[bk])
            kb.cp(Vtm[:, ti * 4 + blk, :], bk[:, 0:256], r=[bk], w=[Vtm])
        if STOP <= 1:
            continue
        project_shift(6, 768, 128, pl[0], ti)
        kb.act(tanhwd[0:64, :], pl[0][0:64, :], AF.Tanh, r=[pl[0]], w=[tanhwd])
        kb.cp(lorab[64:128, :], pl[0][64:128, :], r=[pl[0]], w=[lorab])
        project_shift(7, 896, 128, pl[1], ti)
        sigmoid_from(sgd0[:, :], pl[1][:, :], [pl[1]], [sgd0, pl[1]], pl[1][:, :])
        project_shift(8, 1024, 32, pl[0], ti)
        sigmoid_from(sgd1[0:32, :], pl[0][0:32, :], [pl[0]], [sgd1, pl[0]], pl[0][0:32, :])
        for ck in range(4):
            bk = kb.bank()
            kb.mm(bk[:, 0:256], tanhwd[:, ck * 128:(ck + 1) * 128], wupw[:, :], r=[tanhwd, wupw], w=[bk])
            sigmoid_from(sgw[:, ck, :], bk[:, 0:256], [bk], [sgw], sgw[:, ck, :])
        if STOP <= 2:
            continue
        for hp in range(2):
            hs = slice(hp * 128, (hp + 1) * 128)
            cb_ = [kb.bank(), kb.bank()]
            for ck in range(4):
                kb.mm(cb_[ck // 2][:, (ck % 2) * 256:(ck % 2 + 1) * 256], sgw[:, ck, hs], tri32, start=True, stop=True,
                      r=[sgw, cstb], w=[cb_[ck // 2]])
            for b_ in range(2):
                if 'e' not in SK:
                    kb.act(E12[:, 2 * b_:2 * b_ + 2, :], cb_[b_][:, :].rearrange("p (c x) -> p c x", c=2), AF.Exp, r=[cb_[b_]], w=[E12])
                if '3' not in SK:
                    kb.act(E3[:, 2 * b_:2 * b_ + 2, :], cb_[b_][:, :].rearrange("p (c x) -> p c x", c=2)[:, :, 0:128], AF.Exp,
                           r=[cb_[b_]], w=[E3], scale=-1.0)
            kb.cp(gC[hp][:, :], E12[:, :, 127], r=[E12], w=[gC[hp]])
            kb.tt(E4[:, :, :], E3[:, :, :], gC[hp][:, :].unsqueeze(2).to_broadcast([128, 4, 128]), ALU.mult, r=[E3, gC[hp]], w=[E4])
            if STOP <= 2.1:
                continue
            pr, pk, pv = psh
            project_shift(0 + hp, 0 + hp * 128, 128, pr, ti)
            project_shift(2 + hp, 256 + hp * 128, 128, pk, ti)
            project_shift(4 + hp, 512 + hp * 128, 128, pv, ti)
            if STOP <= 2.2:
                continue
            kkr, rs, iclr, kf, bf_, t1 = ft
            kb.ts(kkr[:, :], pk[:, :], p[:, P_KK + hp:P_KK + hp + 1], None, ALU.mult, r=[pk, p], w=[kkr])
            kb.act(sqb[:, :], kkr[:, :], AF.Square, r=[kkr], w=[sqb])
            bk = kb.bank()
            kb.mm(bk[:, :], blockb, sqb[:, :], r=[cbf, sqb], w=[bk])
            kb.ts(rs[:, :], bk[:, :], 1e-24, None, ALU.max, r=[bk], w=[rs])
            kb.act(rs[:, :], rs[:, :], AF.Sqrt, r=[rs], w=[rs])
            S.add("dve", lambda e, rs=rs: e.reciprocal(out=rs[:, :], in_=rs[:, :]), reads=_bl([rs]), writes=_bl([rs]))
            kb.tt(kkr[:, :], kkr[:, :], rs[:, :], ALU.mult, r=[kkr, rs], w=[kkr])
            bk = kb.bank()
            kb.mm(bk[:, :], aupb[:, hs], lorab[:, :], r=[aupb, lorab], w=[bk])
            sigmoid_from(iclr[:, :], bk[:, :], [bk, d], [iclr], iclr[:, :], bias=d[:, Q_A0H + hp:Q_A0H + hp + 1])
            kb.ts(t1[:, :], iclr[:, :], -1.0, p[:, P_KA + hp:P_KA + hp + 1], ALU.add, ALU.mult, r=[iclr, p], w=[t1])
            kb.stt(kf[:, :], t1[:, :], 1.0, pk[:, :], ALU.add, ALU.mult, r=[t1, pk], w=[kf])
            kb.tt(bf_[:, :], kkr[:, :], iclr[:, :], ALU.mult, r=[kkr, iclr], w=[bf_])
            if STOP <= 2.3:
                continue
            e1v = E12[:, :, 0:128]
            e2v = E12[:, :, 128:256]
            v3 = lambda t_: t_[:, :].rearrange("p (c x) -> p c x", c=4)
            kb.stt(v3(atT[hp]), v3(kkr), -1.0, e2v, ALU.mult, ALU.mult, r=[kkr, E12], w=[atT[hp]])
            kb.tt(v3(rtT[hp]), v3(pr), e1v, ALU.mult, r=[pr, E12], w=[rtT[hp]])
            for h in range(2):
                hr = slice(h * 64, (h + 1) * 64)
                kb.stt(AR[hp][hr, :, h, 0, :], v3(kkr)[hr], -1.0, e2v[hr], ALU.mult, ALU.mult, r=[kkr, E12], w=[AR[hp]])
                kb.tt(AR[hp][hr, :, h, 1, :], v3(pr)[hr], e1v[hr], ALU.mult, r=[pr, E12], w=[AR[hp]])
            kb.tt(v3(btT[hp]), v3(bf_), E3[:, :, :], ALU.mult, r=[bf_, E3], w=[btT[hp]])
            kb.tt(v3(ktT[hp]), v3(kf), E3[:, :, :], ALU.mult, r=[kf, E3], w=[ktT[hp]])
            kb.tt(v3(bhT[hp]), v3(bf_), E4[:, :, :], ALU.mult, r=[bf_, E4], w=[bhT[hp]])
            kb.tt(v3(khT[hp]), v3(kf), E4[:, :, :], ALU.mult, r=[kf, E4], w=[khT[hp]])
            kb.cp(vTb[hp][:, :], pv[:, :], r=[pv], w=[vTb[hp]], eng="act")
            if STOP <= 2.4:
                continue
            kb.stt(sqb[:, :], pr[:, :], p[:, P_RK + hp:P_RK + hp + 1], kf[:, :], ALU.mult, ALU.mult, r=[pr, p, kf], w=[sqb])
            bk = kb.bank()
            kb.mm(bk[:, :], blockb, sqb[:, :], r=[cbf, sqb], w=[bk])
            kb.tt(bonv[hp][:, :], bk[:, :], pv[:, :], ALU.mult, r=[bk, pv], w=[bonv[hp]])
            if STOP <= 2.5:
                continue
            for ck in range(4):
                cs = slice(ck * 128, (ck + 1) * 128)
                bk = kb.bank()
                bkb = bk[:, :].bitcast(BF16)
                srcs = (atT[hp][:, cs], bhT[hp][:, cs], khT[hp][:, cs], vTb[hp][:, cs])
                for i_, s_ in enumerate(srcs):
                    kb.tr(bkb[:, i_ * 128:(i_ + 1) * 128], s_, identb, r=[atT[hp], bhT[hp], khT[hp], vTb[hp], cbf], w=[bk])
                kb.cp(TM4[hp][ck][:, :, :], bkb[:, 0:512].rearrange("p (a x) -> p a x", a=4), r=[bk], w=[TM4[hp][ck]])

        if STOP <= 3:
            continue
        for batch in range(2):
            pairs = [(hp, ck) for ck in (2 * batch, 2 * batch + 1) for hp in range(2)]
            sl = {pr_: i_ for i_, pr_ in enumerate(pairs)}
            for (hp, ck) in pairs:
                s_ = sl[(hp, ck)]
                cs = slice(ck * 128, (ck + 1) * 128)
                b1, b2 = kb.bank(), kb.bank()
                arv = AR[hp][:, ck, :, :, :].rearrange("p h a x -> p (h a x)")
                kb.mm(b1[:, :], btT[hp][:, cs], arv, r=[btT[hp], AR[hp]], w=[b1])
                kb.mm(b2[:, :], ktT[hp][:, cs], arv, r=[ktT[hp], AR[hp]], w=[b2])
                kb.tt(A1m[s_][:, :, :], b1[:, :].rearrange("p (h x) -> p h x", h=2), mask512[:, :, :], ALU.mult,
                      r=[b1, mask512], w=[A1m[s_]])
                kb.tt(A2m[s_][:, :, :], b2[:, :].rearrange("p (h x) -> p h x", h=2), mask512[:, :, :], ALU.mult,
                      r=[b2, mask512], w=[A2m[s_]])
                b3 = kb.bank()
                b3b = b3[:, :].bitcast(BF16)
                for h in range(2):
                    kb.tr(b3b[:, h * 128:(h + 1) * 128], A1m[s_][:, h, 0:128], identb, r=[A1m[s_], cbf], w=[b3])
                kb.cp(QT0[s_][:, :, :], b3b[:, 0:256].rearrange("p (h x) -> p h x", h=2), r=[b3], w=[QT0[s_]], eng="act")
                kb.tt(MTb[s_][0][:, :, :], A1m[s_][:, :, 0:128], identb.unsqueeze(1).to_broadcast([128, 2, 128]), ALU.add,
                      r=[A1m[s_], cbf], w=[MTb[s_][0]])
            if STOP <= 3.1:
                continue
            for k in range(1, 7):
                for (hp, ck) in pairs:
                    s_ = sl[(hp, ck)]
                    bx = kb.bank()
                    cur = QX[s_][k % 2]
                    for h in range(2):
                        if k == 1:
                            X, XT, rd = A1m[s_][:, h, 0:128], QT0[s_][:, h, :], [A1m[s_], QT0[s_]]
                        else:
                            prv = QX[s_][(k - 1) % 2]
                            X, XT, rd = prv[:, h, 0:128], prv[:, h, 128:256], [prv]
                        if k < 6:
                            kb.mm(bx[:, h * 256:h * 256 + 128], XT, X, r=rd, w=[bx])
                        kb.mm(bx[:, h * 256 + 128:h * 256 + 256], X, XT, r=rd, w=[bx])
                    if k < 6:
                        kb.cp(cur[:, :, :], bx[:, :].rearrange("p (h x) -> p h x", h=2), r=[bx], w=[cur], eng="act")
                    else:
                        kb.cp(cur[:, :, 128:256], bx[:, :].rearrange("p (h x) -> p h x", h=2)[:, :, 128:256], r=[bx], w=[cur], eng="act")
                    bm = kb.bank()
                    mprev = MTb[s_][(k - 1) % 2]
                    mcur = MTb[s_][k % 2]
                    for h in range(2):
                        kb.mm(bm[:, h * 128:(h + 1) * 128], cur[:, h, 128:256], mprev[:, h, :], r=[cur, mprev], w=[bm])
                    kb.tt(mcur[:, :, :], bm[:, 0:256].rearrange("p (h x) -> p h x", h=2), mprev[:, :, :], ALU.add,
                          r=[bm, mprev], w=[mcur])
            if STOP <= 3.2:
                continue
            for (hp, ck) in pairs:
                s_ = sl[(hp, ck)]
                tm = TM4[hp][ck]
                bw = kb.bank()
                for h in range(2):
                    kb.mm(bw[:, h * 64:(h + 1) * 64], A2m[s_][:, h, 0:128], tm[:, 3, h * 64:(h + 1) * 64], r=[A2m[s_], tm], w=[bw])
                kb.cp(Wl[s_][:, :, :], bw[:, 0:128].rearrange("p (h x) -> p h x", h=2), r=[bw], w=[Wl[s_]], eng="act")
            if STOP <= 3.3:
                continue
            for (hp, ck) in pairs:
                s_ = sl[(hp, ck)]
                tm = TM4[hp][ck]
                mt = MTb[s_][0]
                ba = kb.bank()
                for h in range(2):
                    kb.mm(ba[:, h * 128:h * 128 + 64], mt[:, h, :], tm[:, 0, h * 64:(h + 1) * 64], r=[mt, tm], w=[ba])
                    kb.mm(ba[:, h * 128 + 64:(h + 1) * 128], mt[:, h, :], Wl[s_][:, h, :], r=[mt, Wl[s_]], w=[ba])
                kb.cp(AU[s_][:, :, :], ba[:, 0:256].rearrange("p (h x) -> p h x", h=2), r=[ba], w=[AU[s_]])
            if STOP <= 3.4:
                continue
            for (hp, ck) in pairs:
                s_ = sl[(hp, ck)]
                tm = TM4[hp][ck]
                br = kb.bank()
                for h in range(2):
                    hr = slice(h * 64, (h + 1) * 64)
                    kb.mm(br[hr, 0:128], AU[s_][:, h, 0:64], A1m[s_][:, h, 128:256], start=True, stop=False,
                          r=[AU[s_], A1m[s_]], w=[br], tp=(0, h * 64))
                    kb.mm(br[hr, 0:128], identb[:, hr], rtT[hp][:, ck * 128:(ck + 1) * 128], start=False, stop=True,
                          r=[cbf, rtT[hp]], w=[br], tp=(0, h * 64))
                kb.cp(RpT[s_][:, :], br[:, 0:128], r=[br], w=[RpT[s_]], eng="act")
                bp = kb.bank()
                for h in range(2):
                    hr = slice(h * 64, (h + 1) * 64)
                    kb.mm(bp[hr, 0:64], AU[s_][:, h, 0:64], tm[:, 1, hr], r=[AU[s_], tm], w=[bp], tp=(0, h * 64))
                    kb.mm(bp[hr, 64:128], tm[:, 1, hr], AU[s_][:, h, 64:128], start=True, stop=False, r=[AU[s_], tm], w=[bp],
                          tp=(0, h * 64))
                    kb.mm(bp[hr, 64:128], tm[:, 2, hr], tm[:, 3, hr], start=False, stop=True, r=[tm], w=[bp], tp=(0, h * 64))
                for h in range(2):
                    hr = slice(h * 64, (h + 1) * 64)
                    kb.stt(PTk[s_][hr, hr], i2[hr, :], gC[hp][hr, ck:ck + 1], bp[hr, 0:64], ALU.mult, ALU.add,
                           r=[i2, gC[hp], bp], w=[PTk[s_]])
                kb.cp(Sloc[s_][:, :], bp[:, 64:128], r=[bp], w=[Sloc[s_]], eng="act")
            if STOP <= 3.5:
                continue
            for ck in (2 * batch, 2 * batch + 1):
                for hp in range(2):
                    s_ = sl[(hp, ck)]
                    tm = TM4[hp][ck]
                    by = kb.bank()
                    for h in range(2):
                        hr = slice(h * 64, (h + 1) * 64)
                        kb.mm(by[hr, 0:128], AU[s_][:, h, 64:128], A1m[s_][:, h, 128:256], start=True, stop=False,
                              r=[AU[s_], A1m[s_]], w=[by], tp=(0, h * 64))
                        kb.mm(by[hr, 0:128], tm[:, 3, hr], A2m[s_][:, h, 128:256], start=False, stop=False,
                              r=[tm, A2m[s_]], w=[by], tp=(0, h * 64))
                        kb.mm(by[hr, 0:128], STk[hp][:, hr], RpT[s_][:, :], start=False, stop=True,
                              r=[STk[hp], RpT[s_]], w=[by], tp=(0, h * 64))
                    kb.cp(YT[hp][:, ck * 128:(ck + 1) * 128], by[:, 0:128], r=[by], w=[YT[hp]], eng="act")
                    bs = kb.bank()
                    kb.mm(bs[:, 0:64], PTk[s_][:, :], STb[hp][:, :], r=[PTk[s_], STb[hp]], w=[bs])
                    kb.tt(STb[hp][:, :], bs[:, 0:64], Sloc[s_][:, :], ALU.add, r=[bs, Sloc[s_]], w=[STb[hp]])
                    for h in range(2):
                        hr = slice(h * 64, (h + 1) * 64)
                        kb.cp(STk[hp][hr, hr], STb[hp][hr, :], r=[STb[hp]], w=[STk[hp]], eng="act")

        if STOP <= 4:
            continue
        for hp in range(2):
            hs = slice(hp * 128, (hp + 1) * 128)
            yc, ysq, rsd = ft[0], ft[1], ft[2]
            bk = kb.bank()
            kb.mm(bk[:, :], block32, YT[hp][:, :], r=[cstb, YT[hp]], w=[bk])
            kb.stt(yc[:, :], bk[:, :], -1.0 / 64.0, YT[hp][:, :], ALU.mult, ALU.add, r=[bk, YT[hp]], w=[yc])
            kb.act(ysq[:, :], yc[:, :], AF.Square, r=[yc], w=[ysq])
            bk = kb.bank()
            kb.mm(bk[:, :], block32, ysq[:, :], r=[cstb, ysq], w=[bk])
            kb.act(rsd[:, :], bk[:, :], AF.Sqrt, r=[bk, cstb], w=[rsd], bias=cst[:, 770:771], scale=1.0)
            S.add("dve", lambda e, rsd=rsd: e.reciprocal(out=rsd[:, :], in_=rsd[:, :]), reads=_bl([rsd]), writes=_bl([rsd]))
            kb.tt(yc[:, :], yc[:, :], rsd[:, :], ALU.mult, r=[yc, rsd], w=[yc])
            kb.ts(yc[:, :], yc[:, :], d[:, Q_GNW8 + hp:Q_GNW8 + hp + 1], p[:, P_GNB + hp:P_GNB + hp + 1], ALU.mult, ALU.add,
                  r=[yc, d, p], w=[yc])
            kb.tt(yc[:, :], yc[:, :], bonv[hp][:, :], ALU.add, r=[yc, bonv[hp]], w=[yc])
            bk = kb.bank()
            kb.mm(bk[:, :], gupb[:, 0, hs], sgd0[:, :], start=True, stop=False, r=[gupb, sgd0], w=[bk])
            kb.mm(bk[:, :], gupb[:, 1, hs], sgd1[:, :], start=False, stop=True, r=[gupb, sgd1], w=[bk])
            kb.tt(ybf[hp][:, :], yc[:, :], bk[:, :], ALU.mult, r=[yc, bk], w=[ybf[hp]])
            yq, yc0 = ti // 2, (ti % 2) * TT2
            kb.dma("sp", io["yb%d" % yq][hp * 128:(hp + 1) * 128, yc0:yc0 + TT2], ybf[hp][:, :], r=[ybf[hp]], w=[kb.dbuf["yb%d" % yq]])

        if STOP <= 5:
            continue
        nkb = 4 * ti + 4
        pti = 0
        for hd in range(2):
            vs = slice(hd * 128, (hd + 1) * 128)
            for c in range(2):
                hr = slice(c * 64, (c + 1) * 64)
                bo, bl = kb.bank(), kb.bank()
                kb.reserved = {i_ for i_, t_ in enumerate(kb.banks) if t_ is bo or t_ is bl}
                for kbi in range(nkb):
                    off = max(0, kbi - 4 * ti) * 128
                    n = TT2 - off
                    diag = kbi >= 4 * ti
                    bsc = kb.bank()
                    kb.mm(bsc[:, 0:n], KT[hd][:, kbi * 128:(kbi + 1) * 128], Qz[hd][c][:, off:TT2], start=True, stop=(not diag),
                          r=[KT[hd], Qz[hd][c]], w=[bsc])
                    if diag:
                        kb.mm(bsc[:, 0:128], identb, maskb, start=False, stop=True, r=[cbf], w=[bsc])
                    pt_ = PTb[pti % 3]
                    pti += 1
                    kb.act(pt_[:, 0:n], bsc[:, 0:n], AF.Exp, r=[bsc], w=[pt_], scale=0.125)
                    kb.mm(bo[:, off:TT2], Vtm[:, kbi, vs], pt_[:, 0:n], start=(kbi == 0), stop=(kbi == nkb - 1), r=[Vtm, pt_], w=[bo])
                    kb.mm(bl[:, off:TT2], onesb, pt_[:, 0:n], start=(kbi == 0), stop=(kbi == nkb - 1), r=[cbf, pt_], w=[bl])
                kb.reserved = set()
                S.add("dve", lambda e, bl=bl: e.reciprocal(out=rec[:, :], in_=bl[:, :]), reads=_bl([bl]), writes=_bl([rec]))
                kb.tt(osb[c][:, :], bo[:, :], rec[:, :], ALU.mult, r=[bo, rec], w=[osb[c]])
            kb.stt(od[:, :], osb[1][:, :], d[:, Q_NLAM:Q_NLAM + 1], osb[0][:, :], ALU.mult, ALU.add, r=[osb[0], osb[1], d], w=[od])
            kb.act(osb[0][:, :], od[:, :], AF.Square, r=[od], w=[osb[0]])
            bk = kb.bank()
            kb.mm(bk[:, :], ones32, osb[0][:, :], r=[cstb, osb[0]], w=[bk])
            kb.act(rec[:, :], bk[:, :], AF.Sqrt, r=[bk, cstb], w=[rec], bias=cst[:, 769:770], scale=1.0)
            S.add("dve", lambda e: e.reciprocal(out=rec[:, :], in_=rec[:, :]), reads=_bl([rec]), writes=_bl([rec]))
            kb.stt(ydb[hd][:, :], od[:, :], d[:, Q_SUBLN:Q_SUBLN + 1], rec[:, :], ALU.mult, ALU.mult, r=[od, d, rec], w=[ydb[hd]])
            yq, yc0 = ti // 2, (ti % 2) * TT2
            kb.dma("sp", io["yb%d" % yq][256 + hd * 128:256 + (hd + 1) * 128, yc0:yc0 + TT2], ydb[hd][:, :], r=[ydb[hd]],
                   w=[kb.dbuf["yb%d" % yq]])
        if ti % 2 == 1:
            collective(kb, "yb%d" % (ti // 2), "yg%d" % (ti // 2))


def _chunks(v):
    return np.ascontiguousarray(v.reshape(8, 128).T)


def own_cols(g):
    a = np.arange
    return np.concatenate([g * 256 + a(256), 512 + g * 256 + a(256), 1024 + g * 256 + a(256), 1536 + a(288),
                           1824 + g * 256 + a(256), 2336 + g * 256 + a(256), 2848 + g * 256 + a(256)])


def make_prm(inp, g):
    p = np.zeros((128, NPRM), np.float32)
    p[:, P_F1PRE:P_F1PRE + 8] = _chunks(inp["ffn1_pre_g"][0])
    p[:, P_F1POST:P_F1POST + 8] = _chunks(inp["ffn1_post_g"][0])
    p[:, P_MPRE:P_MPRE + 8] = _chunks(inp["mix_pre_g"][0])
    p[:, P_MPOST:P_MPOST + 8] = _chunks(inp["mix_post_g"][0])
    p[:, P_F2PRE:P_F2PRE + 8] = _chunks(inp["ffn2_pre_g"][0])
    p[:, P_F2POST:P_F2POST + 8] = _chunks(inp["ffn2_post_g"][0])
    oc = own_cols(g)
    mu = inp["shift_mu"][0]
    for c in range(8):
        p[:, P_MU + c] = mu[oc[c * 128:(c + 1) * 128]]
    p[0:32, P_MU + 8] = mu[oc[1024:1056]]
    ch = slice(g * 256, (g + 1) * 256)
    for name, col in (("rwkv_k_k", P_KK), ("rwkv_k_a", P_KA), ("rwkv_a0", P_A0), ("rwkv_r_k", P_RK),
                      ("rwkv_gn_w", P_GNW), ("rwkv_gn_b", P_GNB)):
        v = inp[name][0].reshape(-1)[ch]
        p[:, col] = v[0:128]
        p[:, col + 1] = v[128:256]
    p[:, P_SUBLN] = inp["diff_subln_w"][0]
    p[:, P_SEL + g] = 1.0
    return p


def make_core_inputs(inp, core, shared):
    b, g = core // 2, core % 2
    oc = own_cols(g)
    ch = slice(g * 256, (g + 1) * 256)
    m = dict(shared)
    m["prm"] = make_prm(inp, g)
    m["xT"] = np.ascontiguousarray(inp["x"][b, g * TH:(g + 1) * TH, :].T)
    m["win"] = np.ascontiguousarray(inp["w_in"][0][:, oc])
    z63 = np.zeros((63, 256), np.float32)
    z64 = np.zeros((64, 256), np.float32)
    m["wupw"] = np.ascontiguousarray(np.concatenate([inp["rwkv_w_up"][0][:, ch], inp["rwkv_w0"][0][ch][None, :], z63], 0))
    m["aup"] = np.ascontiguousarray(np.concatenate([z64, inp["rwkv_a_up"][0][:, ch]], 0))
    gu = np.zeros((128, 2, 256), np.float32)
    gu[:, 0, :] = inp["rwkv_g_up"][0][0:128, ch]
    gu[0:32, 1, :] = inp["rwkv_g_up"][0][128:160, ch]
    m["gup"] = gu
    return m


def make_shared(inp):
    wo = inp["w_o"][0]
    return {
        "cst": make_consts(),
        "f1g": inp["ffn1_w_gate"][0], "f1u": inp["ffn1_w_up"][0], "f1d": inp["ffn1_w_down"][0],
        "f2g": inp["ffn2_w_gate"][0], "f2u": inp["ffn2_w_up"][0], "f2d": inp["ffn2_w_down"][0],
        "wo": np.ascontiguousarray(np.concatenate([wo[0:256], wo[512:768], wo[256:512], wo[768:1024]], 0)),
        "lamv": np.ascontiguousarray(np.concatenate([inp["diff_lam_q1"][0], inp["diff_lam_k1"][0],
                                                     inp["diff_lam_q2"][0], inp["diff_lam_k2"][0]])[None, :]),
    }


_CACHE = {}


def kernel(**inputs):
    inp = {k: np.asarray(v, dtype=np.float32) for k, v in inputs.items()}
    if "nc" not in _CACHE:
        _CACHE["nc"] = build("full")[0]
    nc = _CACHE["nc"]
    shared = make_shared(inp)
    in_maps = [make_core_inputs(inp, c, shared) for c in range(8)]
    res = run_bass_kernel_spmd(nc, in_maps, core_ids=list(range(8)))
    out = np.empty((4, T, D), np.float32)
    for c in range(8):
        b, g = c // 2, c % 2
        out[b, g * TH:(g + 1) * TH, :] = res.results[c]["outT"].T
    return out
```

```python
import os
import numpy as np
from contextlib import ExitStack
import concourse.bass as bass
import concourse.mybir as mybir
from concourse.bass_utils import run_bass_kernel_spmd

F32 = mybir.dt.float32
BF16 = mybir.dt.bfloat16
AF = mybir.ActivationFunctionType
ALU = mybir.AluOpType
AX = mybir.AxisListType

D = 1024
DFF = 2816
NFC = 22
T = 4096
TH = 2048
TT = 1024
NCOLS = 1824
NORM_EPS = 1e-6
GN_EPS = 64e-5
SUBLN_EPS = 1e-5
LAMBDA_INIT = 0.8 - 0.6 * 1.0
PAIRS = [[0, 1], [2, 3], [4, 5], [6, 7]]
NCST = 1160


class Buf:
    __slots__ = ("name", "lw", "rd")

    def __init__(self, name, lw=None):
        self.name = name
        self.lw = lw
        self.rd = []


class Op:
    __slots__ = ("eng", "fn", "deps", "idx", "dma", "target", "val", "dsem", "dval", "prewait", "inc")


class Sched:
    NDSEM = 12

    def __init__(self, nc, stack):
        self.nc = nc
        self.stack = stack
        self.streams = {"pe": [], "act": [], "dve": [], "pool": [], "sp": []}
        self.epoch = None
        self.bufs = []

    def buf(self, name):
        b = Buf(name, self.epoch)
        self.bufs.append(b)
        return b

    def add(self, eng, fn, reads=(), writes=(), dma=False, inc=None):
        op = Op()
        op.eng = eng
        op.fn = fn
        op.dma = dma
        op.target = False
        op.val = None
        op.prewait = None
        op.dsem = None
        op.inc = inc
        deps = set()
        for b in reads:
            if b.lw is not None:
                deps.add(b.lw)
        for b in writes:
            if b.lw is not None:
                deps.add(b.lw)
            deps.update(b.rd)
        if eng == "pe" and not dma:
            deps = {d for d in deps if not (d.eng == "pe" and not d.dma)}
        op.deps = deps
        for b in reads:
            b.rd.append(op)
        for b in writes:
            b.lw = op
            b.rd = []
        op.idx = len(self.streams[eng])
        self.streams[eng].append(op)
        return op

    def barrier(self):
        scr = self._scr
        op = self.add("dve", lambda e: e.memset(scr[0:1, 0:1], 0.0), writes=list(self.bufs))
        self.epoch = op
        self.bufs = []
        return op

    def emit(self):
        nc = self.nc
        st = self.stack
        sems = {e: st.enter_context(nc.semaphore("s_" + e)) for e in ("pe", "act", "dve", "pool")}
        dsems = {q: [st.enter_context(nc.semaphore("d_%s%d" % (q, i))) for i in range(self.NDSEM)]
                 for q in ("sp", "pool")}
        for e, ops in self.streams.items():
            for op in ops:
                best = {}
                dd = []
                for d in op.deps:
                    if d.dma:
                        dd.append(d)
                    elif d.eng not in best or best[d.eng].idx < d.idx:
                        best[d.eng] = d
                op.deps = list(best.values()) + dd
                for d in op.deps:
                    d.target = True
        lastops = []
        for e in ("pe", "act", "dve", "pool"):
            comp = [op for op in self.streams[e] if not op.dma]
            if comp:
                comp[-1].target = True
                lastops.append(comp[-1])
            c = 0
            for op in self.streams[e]:
                if op.dma:
                    continue
                if op.target:
                    c += 1
                    op.val = c
        final_waits = []
        for q in ("sp", "pool"):
            i = 0
            last = {}
            for op in self.streams[q]:
                if not op.dma:
                    continue
                k = i % self.NDSEM
                inc = op.inc if op.inc is not None else 16
                prev = last.get(k, 0)
                op.dsem = dsems[q][k]
                op.dval = prev + inc
                if prev > 0:
                    op.prewait = (op.dsem, prev)
                last[k] = op.dval
                i += 1
            final_waits += [(dsems[q][k], v) for k, v in last.items()]
        engobj = {"pe": "tensor", "act": "scalar", "dve": "vector", "pool": "gpsimd", "sp": "sync"}
        self.nwaits = 0
        self.ninst = 0

        def run_stream(e, eng):
            waited = {}

            def wait(sem, val):
                k = id(sem)
                if waited.get(k, 0) >= val:
                    return
                waited[k] = val
                eng.wait_ge(sem, val)
                self.nwaits += 1
            for op in self.streams[e]:
                if op.prewait is not None:
                    wait(*op.prewait)
                for d in op.deps:
                    if d.dma:
                        wait(d.dsem, d.dval)
                    else:
                        wait(sems[d.eng], d.val)
                ins = op.fn(eng)
                self.ninst += 1
                if op.dma:
                    ins.then_inc(op.dsem, op.inc if op.inc is not None else 16)
                elif op.target:
                    ins.then_inc(sems[e], 1)
            if e == "sp":
                for s, v in final_waits:
                    wait(s, v)
                for lo in lastops:
                    wait(sems[lo.eng], lo.val)

        with nc.Block() as block:
            for e in ("sp", "pool", "act", "dve", "pe"):
                getattr(block, engobj[e])(lambda eng, e=e: run_stream(e, eng))


class Tn:
    __slots__ = ("ap", "b")

    def __init__(self, ap, b):
        self.ap = ap
        self.b = b

    def __getitem__(self, k):
        return self.ap[k]


class Arena:
    def __init__(self, S, ap, nwords):
        self.S = S
        self.ap = ap
        self.n = nwords
        self.off = 0
        self.peak = 0

    def alloc(self, name, shape, dt):
        free = 1
        for s in shape[1:]:
            free *= s
        esz = 4 if dt == F32 else 2
        words = (free * esz + 3) // 4
        words = (words + 7) // 8 * 8
        assert self.off + words <= self.n, "arena overflow at %s: %d + %d > %d" % (name, self.off, words, self.n)
        v = self.ap[:, self.off:self.off + words]
        self.off += words
        self.peak = max(self.peak, self.off)
        if dt != F32:
            v = v.bitcast(dt)
        v = v[0:shape[0], 0:free]
        if len(shape) == 3:
            v = v.rearrange("p (a b) -> p a b", a=shape[1])
        elif len(shape) == 4:
            v = v.rearrange("p (a b c) -> p a b c", a=shape[1], b=shape[2])
        return Tn(v, self.S.buf(name))

    def allocn(self, name, n, shape, dt):
        return [self.alloc("%s%d" % (name, i), shape, dt) for i in range(n)]


def _bl(x):
    out = []
    for t in x:
        if t is None:
            continue
        out.append(t.b if isinstance(t, Tn) else t)
    return out


class KB:
    def __init__(self, nc, st, mode):
        self.nc = nc
        self.mode = mode
        self.S = Sched(nc, st)
        S = self.S
        self.banks = []
        for i in range(8):
            t = st.enter_context(nc.psum_tensor("psb%d" % i, [128, 512], F32))
            self.banks.append(Tn(t, None))
        self.bank_i = 0
        self.reserved = set()
        arena_words = 53208 - NCST - 16
        at = st.enter_context(nc.sbuf_tensor("arena", [128, arena_words], F32))
        self.cst = st.enter_context(nc.sbuf_tensor("cst_sb", [128, NCST], F32))
        self.A = Arena(S, at, arena_words)
        self.bscr = st.enter_context(nc.sbuf_tensor("bscr", [128, 8], F32))
        S._scr = self.bscr
        self.new_epoch_banks()

    def new_epoch_banks(self):
        for t in self.banks:
            t.b = self.S.buf("bank")

    def bank(self):
        while True:
            i = self.bank_i
            self.bank_i = (self.bank_i + 1) % 8
            if i not in self.reserved:
                return self.banks[i]

    def mm(self, out, lhsT, rhs, start=True, stop=True, r=(), w=(), tp=None):
        kw = {} if tp is None else {"tile_position": tp}
        self.S.add("pe", lambda e: e.matmul(out, lhsT=lhsT, rhs=rhs, start=start, stop=stop, **kw),
                   reads=_bl(r), writes=_bl(w))

    def tr(self, out, in_, ident, r=(), w=()):
        self.S.add("pe", lambda e: e.transpose(out, in_, ident), reads=_bl(r), writes=_bl(w))

    def act(self, out, in_, func, r=(), w=(), bias=None, scale=None):
        kw = {}
        if bias is not None:
            kw["bias"] = bias
        if scale is not None:
            kw["scale"] = scale
        self.S.add("act", lambda e: e.activation(out=out, in_=in_, func=func, **kw), reads=_bl(r), writes=_bl(w))

    def ts(self, out, in0, s1, s2, op0, op1=None, r=(), w=(), eng="dve"):
        if op1 is None:
            self.S.add(eng, lambda e: e.tensor_scalar(out=out, in0=in0, scalar1=s1, scalar2=None, op0=op0),
                       reads=_bl(r), writes=_bl(w))
        else:
            self.S.add(eng, lambda e: e.tensor_scalar(out=out, in0=in0, scalar1=s1, scalar2=s2, op0=op0, op1=op1),
                       reads=_bl(r), writes=_bl(w))

    def tt(self, out, in0, in1, op, r=(), w=(), eng="dve"):
        self.S.add(eng, lambda e: e.tensor_tensor(out=out, in0=in0, in1=in1, op=op), reads=_bl(r), writes=_bl(w))

    def stt(self, out, in0, scalar, in1, op0, op1, r=(), w=()):
        self.S.add("dve", lambda e: e.scalar_tensor_tensor(out=out, in0=in0, scalar=scalar, in1=in1, op0=op0, op1=op1),
                   reads=_bl(r), writes=_bl(w))

    def cp(self, out, in_, r=(), w=(), eng="dve"):
        if eng == "act":
            self.S.add("act", lambda e: e.copy(out, in_), reads=_bl(r), writes=_bl(w))
        else:
            self.S.add(eng, lambda e: e.tensor_copy(out=out, in_=in_), reads=_bl(r), writes=_bl(w))

    def dma(self, q, out, in_, r=(), w=()):
        self.S.add(q, lambda e: e.dma_start(out=out, in_=in_), reads=_bl(r), writes=_bl(w), dma=True)


def make_consts():
    c = np.zeros((128, NCST), np.float32)
    c[:, 0:128] = np.eye(128)
    c[:, 128:256] = 1.0
    bo = np.zeros((128, 128), np.float32)
    bo[0:64, 0:64] = 1.0
    bo[64:, 64:] = 1.0
    c[:, 256:384] = bo
    k = np.arange(128)[:, None]
    q = np.arange(128)[None, :]
    c[:, 384:512] = np.where(k > q, -30000.0, 0.0)
    c[:, 512:640] = (k < q).astype(np.float32)
    c[:, 640:768] = (k <= q).astype(np.float32)
    c[:, 768] = D * NORM_EPS
    c[:, 769] = 128 * SUBLN_EPS
    c[:, 770] = 64 * GN_EPS
    c[:, 771] = 0.0
    c[:, 772] = 1.0
    c[:, 776:904] = (k <= q).astype(np.float32) * (-float(np.exp(-0.5)))
    c[:, 904:1032] = (k < q).astype(np.float32) * (-float(np.exp(-0.5)))
    c[:, 1032:1160] = (q < k).astype(np.float32)
    return c


P_F1PRE, P_F1POST, P_MPRE, P_MPOST, P_F2PRE, P_F2POST = 0, 8, 16, 24, 32, 40
P_MU = 48
P_KK = 57
P_KA = 59
P_A0 = 61
P_RK = 63
P_GNW = 65
P_GNB = 67
P_SUBLN = 69
P_SEL = 70
NPRM = 72
Q_F1PRE, Q_F1POST, Q_MPRE, Q_MPOST, Q_F2PRE, Q_F2POST = 0, 8, 16, 24, 32, 40
Q_OMU = 48
Q_SUBLN = 57
Q_LAM = 58
Q_NLAM = 59
Q_GNW8 = 60
Q_A0H = 62
NDER = 64


def build(mode="full"):
    nc = bass.Bass("TRN2", target_bir_lowering=False)
    ph1 = mode in ("full", "p1")
    ph2 = mode in ("full", "p2")
    ph3 = mode in ("full", "p3")

    def dram(name, shape, dt, kind):
        if kind == "Internal":
            return nc.dram_tensor(name, shape, dt).ap()
        return nc.dram_tensor(name, shape, dt, kind=kind).ap()

    IN, OUT, INT = "ExternalInput", "ExternalOutput", "Internal"
    io = {}
    io["cst"] = dram("cst", [128, NCST], F32, IN)
    io["prm"] = dram("prm", [128, NPRM], F32, IN)
    if ph1:
        io["xT"] = dram("xT", [D, TH], F32, IN)
        io["f1g"] = dram("f1g", [D, DFF], F32, IN)
        io["f1u"] = dram("f1u", [D, DFF], F32, IN)
        io["f1d"] = dram("f1d", [DFF, D], F32, IN)
    if ph2:
        io["win"] = dram("win", [D, NCOLS], F32, IN)
        io["wupw"] = dram("wupw", [128, 256], F32, IN)
        io["aup"] = dram("aup", [128, 256], F32, IN)
        io["gup"] = dram("gup", [128, 2, 256], F32, IN)
        io["lamv"] = dram("lamv", [1, 256], F32, IN)
    if ph3:
        io["wo"] = dram("wo", [D, D], F32, IN)
        io["f2g"] = dram("f2g", [D, DFF], F32, IN)
        io["f2u"] = dram("f2u", [D, DFF], F32, IN)
        io["f2d"] = dram("f2d", [DFF, D], F32, IN)
        io["outT"] = dram("outT", [D, TH], F32, OUT)
    def parts(name, n, shape, dt, kind):
        for i in range(n):
            io["%s%d" % (name, i)] = dram("%s%d" % (name, i), shape, dt, kind)

    if mode == "full":
        io["x1T"] = dram("x1T", [D, TH], F32, INT)
        parts("hb", 2, [D, TT], BF16, INT)
        parts("hg", 2, [2 * D, TT], BF16, INT)
        parts("yb", 4, [512, 1024], BF16, INT)
        parts("yg", 4, [1024, 1024], BF16, INT)
    elif mode == "p1":
        io["x1T"] = dram("x1T", [D, TH], F32, OUT)
        parts("hb", 2, [D, TT], BF16, OUT)
    elif mode == "p2":
        parts("hg", 2, [2 * D, TT], BF16, IN)
        parts("yb", 4, [512, 1024], BF16, OUT)
    elif mode == "p3":
        io["x1T"] = dram("x1T", [D, TH], F32, IN)
        parts("yg", 4, [1024, 1024], BF16, IN)

    with ExitStack() as st:
        kb = KB(nc, st, mode)
        kb.io = io
        prologue(kb)
        if ph1:
            phase_ffn(kb, 1)
        if ph2:
            kb.S.barrier()
            kb.new_epoch_banks()
            kb.A.off = kb.base_off
            phase_mixer(kb)
        if ph3:
            kb.S.barrier()
            kb.new_epoch_banks()
            kb.A.off = kb.base_off
            phase_ffn(kb, 3)
        kb.S.emit()
        kb.stats = (kb.S.ninst, kb.S.nwaits, kb.A.peak)
    return nc, kb


def collective(kb, srcname, dstname):
    S = kb.S
    if kb.mode != "full":
        return
    src, dst = kb.io[srcname], kb.io[dstname]
    bs = kb.dbuf[srcname]
    bd = kb.dbuf[dstname]
    S.add("pool", lambda e: e.collective_compute("AllGather", ALU.bypass, replica_groups=PAIRS,
                                                 ins=[src[:, :]], outs=[dst[:, :]]),
          reads=[bs], writes=[bd], dma=True, inc=1)


def prologue(kb):
    S, A, io = kb.S, kb.A, kb.io
    cb = S.buf("cst")
    kb.cstb = cb
    kb.dma("sp", kb.cst[:, :], io["cst"][:, :], w=[cb])
    kb.prm = A.alloc("prm", [128, NPRM], F32)
    kb.der = A.alloc("der", [128, NDER], F32)
    kb.dma("sp", kb.prm[:, :], io["prm"][:, :], w=[kb.prm])
    kb.cbf = A.alloc("cbf", [128, 512], BF16)
    kb.cp(kb.cbf[:, :], kb.cst[:, 0:512], r=[cb], w=[kb.cbf])
    kb.identb = kb.cbf.ap[:, 0:128]
    kb.onesb = kb.cbf.ap[:, 128:256]
    kb.blockb = kb.cbf.ap[:, 256:384]
    kb.maskb = kb.cbf.ap[:, 384:512]
    p, d = kb.prm, kb.der
    rw = dict(r=[p], w=[d])
    kb.ts(d[:, Q_F1PRE:Q_F1PRE + 8], p[:, P_F1PRE:P_F1PRE + 8], 32.0, None, ALU.mult, **rw)
    kb.ts(d[:, Q_F1POST:Q_F1POST + 8], p[:, P_F1POST:P_F1POST + 8], 16.0, None, ALU.mult, **rw)
    kb.ts(d[:, Q_MPRE:Q_MPRE + 8], p[:, P_MPRE:P_MPRE + 8], 32.0, None, ALU.mult, **rw)
    kb.ts(d[:, Q_MPOST:Q_MPOST + 8], p[:, P_MPOST:P_MPOST + 8], 32.0, None, ALU.mult, **rw)
    kb.ts(d[:, Q_F2PRE:Q_F2PRE + 8], p[:, P_F2PRE:P_F2PRE + 8], 32.0, None, ALU.mult, **rw)
    kb.ts(d[:, Q_F2POST:Q_F2POST + 8], p[:, P_F2POST:P_F2POST + 8], 16.0, None, ALU.mult, **rw)
    kb.ts(d[:, Q_OMU:Q_OMU + 9], p[:, P_MU:P_MU + 9], -1.0, 1.0, ALU.mult, ALU.add, **rw)
    kb.ts(d[:, Q_SUBLN:Q_SUBLN + 1], p[:, P_SUBLN:P_SUBLN + 1], (1.0 - LAMBDA_INIT) * float(np.sqrt(128.0)), None,
          ALU.mult, **rw)
    kb.ts(d[:, Q_A0H:Q_A0H + 2], p[:, P_A0:P_A0 + 2], 0.5, None, ALU.mult, **rw)
    kb.dbuf = {k: S.buf(k) for k in ["x1T"] + ["hb%d" % i for i in range(2)] + ["hg%d" % i for i in range(2)]
               + ["yb%d" % i for i in range(4)] + ["yg%d" % i for i in range(4)]}
    kb.base_off = A.off


def rms_rstd(kb, sq, sqr, rstd, ncols, nchunk=8, epscol=768):
    bk = kb.bank()
    for dc in range(nchunk):
        kb.mm(bk[:, 0:ncols], kb.onesb, sq[:, dc, 0:ncols], start=(dc == 0), stop=(dc == nchunk - 1),
              r=[kb.cbf] + sqr, w=[bk])
    kb.act(rstd[:, 0:ncols], bk[:, 0:ncols], AF.Sqrt, r=[bk, kb.cstb], w=[rstd], bias=kb.cst[:, epscol:epscol + 1], scale=1.0)
    kb.S.add("dve", lambda e: e.reciprocal(out=rstd[:, 0:ncols], in_=rstd[:, 0:ncols]), reads=_bl([rstd]), writes=_bl([rstd]))


def phase_ffn(kb, which):
    S, A, io = kb.S, kb.A, kb.io
    d = kb.der
    NTS = TT // 512
    xt = [[A.alloc("xt%d_%d" % (dc, t_), [128, 512], F32) for t_ in range(NTS)] for dc in range(8)]
    fo = [[A.alloc("fo%d_%d" % (dc, t_), [128, 512], F32) for t_ in range(NTS)] for dc in range(8)]
    hT = A.alloc("hT", [128, 8, TT], BF16)
    AT = [A.alloc("AT%d" % i, [128, TT], BF16) for i in range(NFC)]
    wg = A.allocn("wg", 3, [128, 8, 256], BF16)
    wu = A.allocn("wu", 3, [128, 8, 256], BF16)
    wd = A.allocn("wd", 2, [128, NFC, 256], BF16)
    sq = A.alloc("sq", [128, 8, 512], BF16)
    rstd = A.allocn("rstd", 2, [128, 512], F32)
    xt_all = [t for row in xt for t in row]
    if which == 3:
        wo = A.alloc("wo", [128, 8, D], BF16)
        kb.dma("pool", wo[:, :, :], io["wo"].rearrange("(kc p) f -> p kc f", p=128), w=[wo])
        Wg, Wu, Wd = io["f2g"], io["f2u"], io["f2d"]
        qpre, qpost = Q_F2PRE, Q_F2POST
    else:
        Wg, Wu, Wd = io["f1g"], io["f1u"], io["f1d"]
        qpre, qpost = Q_F1PRE, Q_F1POST

    def load_gu(fg):
        s = fg % 3
        kb.dma("pool", wg[s][:, :, :], Wg[:, fg * 256:(fg + 1) * 256].rearrange("(kc p) f -> p kc f", p=128), w=[wg[s]])
        kb.dma("pool", wu[s][:, :, :], Wu[:, fg * 256:(fg + 1) * 256].rearrange("(kc p) f -> p kc f", p=128), w=[wu[s]])

    def load_d(dcp):
        s = dcp % 2
        kb.dma("pool", wd[s][:, :, :], Wd[:, dcp * 256:(dcp + 1) * 256].rearrange("(fc p) d -> p fc d", p=128), w=[wd[s]])

    def norm_to_bf16(src, qcol, dst):
        for ts_ in range(NTS):
            cols = slice(ts_ * 512, (ts_ + 1) * 512)
            for dc in range(8):
                kb.act(sq[:, dc, :], src[dc][ts_][:, :], AF.Square, r=[src[dc][ts_]], w=[sq])
            rs = rstd[ts_ % 2]
            rms_rstd(kb, sq, [sq], rs, 512)
            for dc in range(8):
                kb.stt(dst[:, dc, cols], src[dc][ts_][:, :], d[:, qcol + dc:qcol + dc + 1], rs[:, :], ALU.mult, ALU.mult,
                       r=[src[dc][ts_], d, rs], w=[dst])

    def post_norm_residual(qcol):
        for ts_ in range(NTS):
            for dc in range(8):
                kb.act(sq[:, dc, :], fo[dc][ts_][:, :], AF.Square, r=[fo[dc][ts_]], w=[sq])
            rs = rstd[ts_ % 2]
            rms_rstd(kb, sq, [sq], rs, 512)
            for dc in range(8):
                f_, x_ = fo[dc][ts_], xt[dc][ts_]
                kb.stt(f_[:, :], f_[:, :], d[:, qcol + dc:qcol + dc + 1], rs[:, :], ALU.mult, ALU.mult,
                       r=[f_, d, rs], w=[f_])
                kb.tt(x_[:, :], x_[:, :], f_[:, :], ALU.add, r=[x_, f_], w=[x_])

    def xt_dma(dram_ap, tcols, to_dram, r=(), w=()):
        for dc in range(8):
            for ts_ in range(NTS):
                c0 = tcols.start + ts_ * 512
                dr = dram_ap[dc * 128:(dc + 1) * 128, c0:c0 + 512]
                if to_dram:
                    kb.dma("sp", dr, xt[dc][ts_][:, :], r=[xt[dc][ts_]] + list(r), w=list(w))
                else:
                    kb.dma("sp", xt[dc][ts_][:, :], dr, r=list(r), w=[xt[dc][ts_]] + list(w))

    ntile = TH // TT
    for ti in range(ntile):
        tcols = slice(ti * TT, (ti + 1) * TT)
        if which == 1:
            xt_dma(io["xT"], tcols, False)
        else:
            xt_dma(io["x1T"], tcols, False, r=[kb.dbuf["x1T"]])
        load_gu(0)
        load_gu(1)
        if which == 3:
            yA = [AT[i] for i in range(0, 8)]
            yB = [AT[i] for i in range(8, 16)]
            for kc in range(8):
                kb.dma("sp", yA[kc][:, :], io["yg%d" % ti][kc * 128:(kc + 1) * 128, :],
                       r=[kb.dbuf["yg%d" % ti]], w=[yA[kc]])
                kb.dma("sp", yB[kc][:, :], io["yg%d" % (2 + ti)][kc * 128:(kc + 1) * 128, :],
                       r=[kb.dbuf["yg%d" % (2 + ti)]], w=[yB[kc]])
            p = kb.prm
            for kc in range(8):
                kb.ts(yA[kc][:, :], yA[kc][:, :], p[:, P_SEL:P_SEL + 1], None, ALU.mult, r=[yA[kc], p], w=[yA[kc]])
                kb.stt(hT[:, kc, :], yB[kc][:, :], p[:, P_SEL + 1:P_SEL + 2], yA[kc][:, :], ALU.mult, ALU.add,
                       r=[yB[kc], yA[kc], p], w=[hT])
            for dc in range(8):
                for ts_ in range(NTS):
                    cols = slice(ts_ * 512, (ts_ + 1) * 512)
                    bk = kb.bank()
                    for kc in range(8):
                        kb.mm(bk[:, :], wo[:, kc, dc * 128:(dc + 1) * 128], hT[:, kc, cols], start=(kc == 0), stop=(kc == 7),
                              r=[wo, hT], w=[bk])
                    kb.cp(fo[dc][ts_][:, :], bk[:, :], r=[bk], w=[fo[dc][ts_]], eng="act")
            post_norm_residual(Q_MPOST)
        norm_to_bf16(xt, qpre, hT)
        load_d(0)
        load_d(1)
        sgi = 0
        fo_all = [t for row in fo for t in row]
        for fg in range(NFC // 2):
            s = fg % 3
            for fi in range(2):
                fc = fg * 2 + fi
                bg = [kb.bank() for _ in range(NTS)]
                bu = [kb.bank() for _ in range(NTS)]
                for (wt, bks) in ((wg[s], bg), (wu[s], bu)):
                    for ts_ in range(NTS):
                        for kc in range(8):
                            kb.mm(bks[ts_][:, :], wt[:, kc, fi * 128:(fi + 1) * 128], hT[:, kc, ts_ * 512:(ts_ + 1) * 512],
                                  start=(kc == 0), stop=(kc == 7), r=[wt, hT], w=[bks[ts_]])
                for ts_ in range(NTS):
                    sg = fo_all[sgi % 16]
                    sgi += 1
                    kb.act(sg[:, :], bg[ts_][:, :], AF.Silu, r=[bg[ts_]], w=[sg])
                    kb.tt(AT[fc][:, ts_ * 512:(ts_ + 1) * 512], sg[:, :], bu[ts_][:, :], ALU.mult, r=[sg, bu[ts_]], w=[AT[fc]])
            if fg + 2 < NFC // 2:
                load_gu(fg + 2)
        for dcp in range(4):
            s = dcp % 2
            for di in range(2):
                dc = dcp * 2 + di
                for ts_ in range(NTS):
                    cols = slice(ts_ * 512, (ts_ + 1) * 512)
                    bk = kb.bank()
                    for fc in range(NFC):
                        kb.mm(bk[:, :], wd[s][:, fc, di * 128:(di + 1) * 128], AT[fc][:, cols], start=(fc == 0), stop=(fc == NFC - 1),
                              r=[wd[s], AT[fc]], w=[bk])
                    kb.cp(fo[dc][ts_][:, :], bk[:, :], r=[bk], w=[fo[dc][ts_]], eng="act")
            if dcp + 2 < 4:
                load_d(dcp + 2)
        post_norm_residual(qpost)
        if which == 1:
            xt_dma(io["x1T"], tcols, True, w=[kb.dbuf["x1T"]])
            norm_to_bf16(xt, Q_MPRE, hT)
            kb.dma("sp", io["hb%d" % ti][:, :].rearrange("(dc p) t -> p dc t", p=128), hT[:, :, :], r=[hT], w=[kb.dbuf["hb%d" % ti]])
            collective(kb, "hb%d" % ti, "hg%d" % ti)
        else:
            xt_dma(io["outT"], tcols, True)


def phase_mixer(kb):
    S, A, io = kb.S, kb.A, kb.io
    d, p = kb.der, kb.prm
    cst, cstb = kb.cst, kb.cstb
    TT2 = 512
    NT2 = T // TT2
    ones32 = cst[:, 128:256]
    block32 = cst[:, 256:384]
    tri32 = cst[:, 776:1032]

    win = A.alloc("win", [128, 8, NCOLS], BF16)
    kb.dma("pool", win[:, :, :], io["win"].rearrange("(kc p) f -> p kc f", p=128), w=[win])
    wupw = A.alloc("wupw", [128, 256], F32)
    kb.dma("sp", wupw[:, :], io["wupw"][:, :], w=[wupw])
    aupb = A.alloc("aupb", [128, 256], BF16)
    kb.dma("pool", aupb[:, :], io["aup"][:, :], w=[aupb])
    gupb = A.alloc("gupb", [128, 2, 256], BF16)
    kb.dma("pool", gupb[:, :, :], io["gup"][:, :, :], w=[gupb])
    lamv = A.alloc("lamv", [128, 256], F32)
    kb.dma("sp", lamv[:, :], io["lamv"].partition_broadcast(128), w=[lamv])
    ltmp = A.alloc("ltmp", [128, 128], F32)
    lsum = A.alloc("lsum", [128, 2], F32)
    kb.tt(ltmp[:, 0:64], lamv[:, 0:64], lamv[:, 64:128], ALU.mult, r=[lamv], w=[ltmp])
    kb.tt(ltmp[:, 64:128], lamv[:, 128:192], lamv[:, 192:256], ALU.mult, r=[lamv], w=[ltmp])
    S.add("dve", lambda e: e.reduce_sum(out=lsum[:, 0:1], in_=ltmp[:, 0:64], axis=AX.X), reads=_bl([ltmp]), writes=_bl([lsum]))
    S.add("dve", lambda e: e.reduce_sum(out=lsum[:, 1:2], in_=ltmp[:, 64:128], axis=AX.X), reads=_bl([ltmp]), writes=_bl([lsum]))
    kb.act(lsum[:, :], lsum[:, :], AF.Exp, r=[lsum], w=[lsum])
    kb.tt(d[:, Q_LAM:Q_LAM + 1], lsum[:, 0:1], lsum[:, 1:2], ALU.subtract, r=[lsum], w=[d])
    kb.ts(d[:, Q_LAM:Q_LAM + 1], d[:, Q_LAM:Q_LAM + 1], LAMBDA_INIT, None, ALU.add, r=[d], w=[d])
    kb.ts(d[:, Q_NLAM:Q_NLAM + 1], d[:, Q_LAM:Q_LAM + 1], -1.0, None, ALU.mult, r=[d], w=[d])
    kb.ts(d[:, Q_GNW8:Q_GNW8 + 2], p[:, P_GNW:P_GNW + 2], 8.0, None, ALU.mult, r=[p], w=[d])
    mask512 = A.alloc("mask512", [128, 2, 256], BF16)
    for h in range(2):
        kb.cp(mask512[:, h, :], cst[:, 512:768], r=[cstb], w=[mask512])
    lowm = A.alloc("lowm", [128, 128], BF16)
    kb.cp(lowm[:, :], cst[:, 1032:1160], r=[cstb], w=[lowm])
    i2 = A.alloc("i2", [128, 64], F32)
    kb.tt(i2[:, :], cst[:, 0:64], cst[:, 64:128], ALU.add, r=[cstb], w=[i2])
    identb, onesb, blockb, maskb = kb.identb, kb.onesb, kb.blockb, kb.maskb
    cbf = kb.cbf

    KT = A.allocn("KT", 2, [128, T], BF16)
    Vtm = A.alloc("Vtm", [128, T // 128, 256], BF16)
    hT = A.alloc("hT2", [128, 8, TT2], BF16)
    rawt = A.allocn("rawt", 2, [128, TT2 + 1], F32)
    carry = A.alloc("carry", [128, 9], F32)
    S.add("dve", lambda e: e.memset(carry[:, :], 0.0), writes=_bl([carry]))
    psh = A.allocn("psh", 3, [128, TT2], F32)
    pl = A.allocn("pl", 2, [128, TT2], F32)
    tanhwd = A.alloc("tanhwd", [128, TT2], F32)
    lorab = A.alloc("lorab", [128, TT2], BF16)
    sgd0 = A.alloc("sgd0", [128, TT2], BF16)
    sgd1 = A.alloc("sgd1", [128, TT2], BF16)
    sgw = A.alloc("sgw", [128, 4, 256], F32)
    S.add("dve", lambda e: e.memset(tanhwd[:, :], 0.0), writes=_bl([tanhwd]))
    S.add("dve", lambda e: e.memset(tanhwd[64:65, :], 1.0), writes=_bl([tanhwd]))
    S.add("dve", lambda e: e.memset(lorab[:, :], 0.0), writes=_bl([lorab]))
    S.add("dve", lambda e: e.memset(sgd1[:, :], 0.0), writes=_bl([sgd1]))
    Qz = [A.allocn("Qz%d_" % hd, 2, [128, TT2], BF16) for hd in range(2)]
    for hd in range(2):
        for c in range(2):
            S.add("dve", lambda e, hd=hd, c=c: e.memset(Qz[hd][c][:, :], 0.0), writes=_bl([Qz[hd][c]]))
    ft = A.allocn("ft", 6, [128, TT2], F32)
    E12 = A.alloc("E12", [128, 4, 256], F32)
    E3 = A.alloc("E3", [128, 4, 128], F32)
    E4 = A.alloc("E4", [128, 4, 128], F32)
    gC = A.allocn("gC", 2, [128, 4], F32)
    bonv = A.allocn("bonv", 2, [128, TT2], F32)
    sqb = A.alloc("sqb", [128, TT2], BF16)
    ARl = A.allocn("AR", 2, [128, 4 * 2 * 2 * 128], BF16)
    AR = [Tn(t_.ap.rearrange("p (c h a x) -> p c h a x", c=4, h=2, a=2), t_.b) for t_ in ARl]
    for hp in range(2):
        S.add("dve", lambda e, hp=hp: e.memset(ARl[hp][:, :], 0.0), writes=_bl([ARl[hp]]))
    atT = A.allocn("atT", 2, [128, TT2], BF16)
    rtT = A.allocn("rtT", 2, [128, TT2], BF16)
    btT = A.allocn("btT", 2, [128, TT2], BF16)
    ktT = A.allocn("ktT", 2, [128, TT2], BF16)
    bhT = A.allocn("bhT", 2, [128, TT2], BF16)
    khT = A.allocn("khT", 2, [128, TT2], BF16)
    vTb = A.allocn("vTb", 2, [128, TT2], BF16)
    TM4 = [[A.alloc("TM4_%d_%d" % (hp, ck), [128, 4, 128], BF16) for ck in range(4)] for hp in range(2)]
    NSL = 4
    A1m = A.allocn("A1m", NSL, [128, 2, 256], BF16)
    A2m = A.allocn("A2m", NSL, [128, 2, 256], BF16)
    QT0 = A.allocn("QT0", NSL, [128, 2, 128], BF16)
    QX = [A.allocn("QX%d_" % s_, 2, [128, 2, 256], BF16) for s_ in range(NSL)]
    MTb = [A.allocn("MT%d_" % s_, 2, [128, 2, 128], BF16) for s_ in range(NSL)]
    Wl = A.allocn("Wl", NSL, [128, 2, 64], BF16)
    AU = A.allocn("AU", NSL, [128, 2, 128], BF16)
    RpT = A.allocn("RpT", NSL, [128, 128], BF16)
    Sloc = A.allocn("Sloc", NSL, [128, 64], F32)
    STb = A.allocn("STb", 2, [128, 64], BF16)
    STk = A.allocn("STk", 2, [128, 128], BF16)
    PTk = A.allocn("PTk", NSL, [128, 128], BF16)
    for hp in range(2):
        S.add("dve", lambda e, hp=hp: e.memset(STb[hp][:, :], 0.0), writes=_bl([STb[hp]]))
        S.add("dve", lambda e, hp=hp: e.memset(STk[hp][:, :], 0.0), writes=_bl([STk[hp]]))
    for s_ in range(NSL):
        S.add("dve", lambda e, s_=s_: e.memset(PTk[s_][:, :], 0.0), writes=_bl([PTk[s_]]))
    YT = A.allocn("YT", 2, [128, TT2], F32)
    ybf = A.allocn("ybf", 2, [128, TT2], BF16)
    LOOK = 2
    PTb = A.allocn("PTb", LOOK + 2, [128, TT2], BF16)
    osb = [ft[3], ft[4]]
    rec = ft[5]
    od = ft[0]
    ydb = A.allocn("ydb", 2, [128, TT2], BF16)

    def sigmoid_from(out, in_, r, w, tmp, bias=None):
        if bias is None:
            kb.act(tmp, in_, AF.Tanh, r=r, w=w, scale=0.5)
        else:
            kb.act(tmp, in_, AF.Tanh, r=r, w=w, bias=bias, scale=0.5)
        kb.ts(out, tmp, 0.5, 0.5, ALU.mult, ALU.add, r=w, w=w)

    def project_shift(c, col0, M, dst, ti):
        bk = kb.bank()
        for kc in range(8):
            kb.mm(bk[0:M, :], win[:, kc, col0:col0 + M], hT[:, kc, :], start=(kc == 0), stop=(kc == 7), r=[win, hT], w=[bk])
        rt = rawt[c % 2]
        kb.cp(rt[0:M, 1:TT2 + 1], bk[0:M, :], r=[bk], w=[rt], eng="act")
        kb.cp(rt[0:M, 0:1], carry[0:M, c:c + 1], r=[carry], w=[rt])
        kb.ts(dst[0:M, :], rt[0:M, 0:TT2], p[0:M, P_MU + c:P_MU + c + 1], None, ALU.mult, r=[rt, p], w=[dst])
        kb.stt(dst[0:M, :], rt[0:M, 1:TT2 + 1], d[0:M, Q_OMU + c:Q_OMU + c + 1], dst[0:M, :], ALU.mult, ALU.add,
               r=[rt, d, dst], w=[dst])
        kb.cp(carry[0:M, c:c + 1], rt[0:M, TT2:TT2 + 1], r=[rt], w=[carry])

    STOP = float(os.environ.get('K_STOP', '99'))
    NTI = int(os.environ.get('K_NTI', str(NT2)))
    for ti in range(NTI):
        t0 = ti * TT2
        rk, tok = ti // 4, (ti % 4) * TT2
        hpart, c0 = tok // TT, tok % TT
        kb.dma("sp", hT[:, :, :], io["hg%d" % hpart][rk * D:(rk + 1) * D, c0:c0 + TT2].rearrange("(dc p) t -> p dc t", p=128),
               r=[kb.dbuf["hg%d" % hpart]], w=[hT])
        SK = os.environ.get('K_SKIP', '')
        for hd in range(2):
            if 'q' in SK:
                continue
            bk = kb.bank()
            for kc in range(8):
                kb.mm(bk[:, :], win[:, kc, 1056 + hd * 128:1056 + (hd + 1) * 128], hT[:, kc, :], start=(kc == 0), stop=(kc == 7),
                      r=[win, hT], w=[bk])
            kb.cp(Qz[hd][0][0:64, :], bk[0:64, :], r=[bk], w=[Qz[hd][0]], eng="act")
            kb.cp(Qz[hd][1][64:128, :], bk[64:128, :], r=[bk], w=[Qz[hd][1]], eng="act")
            bk = kb.bank()
            for kc in range(8):
                kb.mm(bk[:, :], win[:, kc, 1312 + hd * 128:1312 + (hd + 1) * 128], hT[:, kc, :], start=(kc == 0), stop=(kc == 7),
                      r=[win, hT], w=[bk])
            kb.cp(KT[hd][:, t0:t0 + TT2], bk[:, :], r=[bk], w=[KT[hd]], eng="act")
        for blk in range(4):
            if 'v' in SK:
                continue
            bk = kb.bank()
            for kc in range(8):
                kb.mm(bk[:, 0:256], hT[:, kc, blk * 128:(blk + 1) * 128], win[:, kc, 1568:1824], start=(kc == 0), stop=(kc == 7),
                      r=[win, hT], w=[bk])
            kb.cp(Vtm[:, ti * 4 + blk, :], bk[:, 0:256], r=[bk], w=[Vtm])
        if STOP <= 1:
            continue
        project_shift(6, 768, 128, pl[0], ti)
        kb.act(tanhwd[0:64, :], pl[0][0:64, :], AF.Tanh, r=[pl[0]], w=[tanhwd])
        kb.cp(lorab[64:128, :], pl[0][64:128, :], r=[pl[0]], w=[lorab])
        project_shift(7, 896, 128, pl[1], ti)
        sigmoid_from(sgd0[:, :], pl[1][:, :], [pl[1]], [sgd0, pl[1]], pl[1][:, :])
        project_shift(8, 1024, 32, pl[0], ti)
        sigmoid_from(sgd1[0:32, :], pl[0][0:32, :], [pl[0]], [sgd1, pl[0]], pl[0][0:32, :])
        for ck in range(4):
            bk = kb.bank()
            kb.mm(bk[:, 0:256], tanhwd[:, ck * 128:(ck + 1) * 128], wupw[:, :], r=[tanhwd, wupw], w=[bk])
            sigmoid_from(sgw[:, ck, :], bk[:, 0:256], [bk], [sgw], sgw[:, ck, :])
        if STOP <= 2:
            continue
        for hp in range(2):
            hs = slice(hp * 128, (hp + 1) * 128)
            cb_ = [kb.bank(), kb.bank()]
            for ck in range(4):
                kb.mm(cb_[ck // 2][:, (ck % 2) * 256:(ck % 2 + 1) * 256], sgw[:, ck, hs], tri32, start=True, stop=True,
                      r=[sgw, cstb], w=[cb_[ck // 2]])
            for b_ in range(2):
                if 'e' not in SK:
                    kb.act(E12[:, 2 * b_:2 * b_ + 2, :], cb_[b_][:, :].rearrange("p (c x) -> p c x", c=2), AF.Exp, r=[cb_[b_]], w=[E12])
                if '3' not in SK:
                    kb.act(E3[:, 2 * b_:2 * b_ + 2, :], cb_[b_][:, :].rearrange("p (c x) -> p c x", c=2)[:, :, 0:128], AF.Exp,
                           r=[cb_[b_]], w=[E3], scale=-1.0)
            kb.cp(gC[hp][:, :], E12[:, :, 127], r=[E12], w=[gC[hp]])
            kb.tt(E4[:, :, :], E3[:, :, :], gC[hp][:, :].unsqueeze(2).to_broadcast([128, 4, 128]), ALU.mult, r=[E3, gC[hp]], w=[E4])
            if STOP <= 2.1:
                continue
            pr, pk, pv = psh
            project_shift(0 + hp, 0 + hp * 128, 128, pr, ti)
            project_shift(2 + hp, 256 + hp * 128, 128, pk, ti)
            project_shift(4 + hp, 512 + hp * 128, 128, pv, ti)
            if STOP <= 2.2:
                continue
            kkr, rs, iclr, kf, bf_, t1 = ft
            kb.ts(kkr[:, :], pk[:, :], p[:, P_KK + hp:P_KK + hp + 1], None, ALU.mult, r=[pk, p], w=[kkr])
            kb.act(sqb[:, :], kkr[:, :], AF.Square, r=[kkr], w=[sqb])
            bk = kb.bank()
            kb.mm(bk[:, :], blockb, sqb[:, :], r=[cbf, sqb], w=[bk])
            kb.ts(rs[:, :], bk[:, :], 1e-24, None, ALU.max, r=[bk], w=[rs])
            kb.act(rs[:, :], rs[:, :], AF.Sqrt, r=[rs], w=[rs])
            S.add("dve", lambda e, rs=rs: e.reciprocal(out=rs[:, :], in_=rs[:, :]), reads=_bl([rs]), writes=_bl([rs]))
            kb.tt(kkr[:, :], kkr[:, :], rs[:, :], ALU.mult, r=[kkr, rs], w=[kkr])
            bk = kb.bank()
            kb.mm(bk[:, :], aupb[:, hs], lorab[:, :], r=[aupb, lorab], w=[bk])
            sigmoid_from(iclr[:, :], bk[:, :], [bk, d], [iclr], iclr[:, :], bias=d[:, Q_A0H + hp:Q_A0H + hp + 1])
            kb.ts(t1[:, :], iclr[:, :], -1.0, p[:, P_KA + hp:P_KA + hp + 1], ALU.add, ALU.mult, r=[iclr, p], w=[t1])
            kb.stt(kf[:, :], t1[:, :], 1.0, pk[:, :], ALU.add, ALU.mult, r=[t1, pk], w=[kf])
            kb.tt(bf_[:, :], kkr[:, :], iclr[:, :], ALU.mult, r=[kkr, iclr], w=[bf_])
            if STOP <= 2.3:
                continue
            e1v = E12[:, :, 0:128]
            e2v = E12[:, :, 128:256]
            v3 = lambda t_: t_[:, :].rearrange("p (c x) -> p c x", c=4)
            kb.stt(v3(atT[hp]), v3(kkr), -1.0, e2v, ALU.mult, ALU.mult, r=[kkr, E12], w=[atT[hp]])
            kb.tt(v3(rtT[hp]), v3(pr), e1v, ALU.mult, r=[pr, E12], w=[rtT[hp]])
            for h in range(2):
                hr = slice(h * 64, (h + 1) * 64)
                kb.stt(AR[hp][hr, :, h, 0, :], v3(kkr)[hr], -1.0, e2v[hr], ALU.mult, ALU.mult, r=[kkr, E12], w=[AR[hp]])
                kb.tt(AR[hp][hr, :, h, 1, :], v3(pr)[hr], e1v[hr], ALU.mult, r=[pr, E12], w=[AR[hp]])
            kb.tt(v3(btT[hp]), v3(bf_), E3[:, :, :], ALU.mult, r=[bf_, E3], w=[btT[hp]])
            kb.tt(v3(ktT[hp]), v3(kf), E3[:, :, :], ALU.mult, r=[kf, E3], w=[ktT[hp]])
            kb.tt(v3(bhT[hp]), v3(bf_), E4[:, :, :], ALU.mult, r=[bf_, E4], w=[bhT[hp]])
            kb.tt(v3(khT[hp]), v3(kf), E4[:, :, :], ALU.mult, r=[kf, E4], w=[khT[hp]])
            kb.cp(vTb[hp][:, :], pv[:, :], r=[pv], w=[vTb[hp]], eng="act")
            if STOP <= 2.4:
                continue
            kb.stt(sqb[:, :], pr[:, :], p[:, P_RK + hp:P_RK + hp + 1], kf[:, :], ALU.mult, ALU.mult, r=[pr, p, kf], w=[sqb])
            bk = kb.bank()
            kb.mm(bk[:, :], blockb, sqb[:, :], r=[cbf, sqb], w=[bk])
            kb.tt(bonv[hp][:, :], bk[:, :], pv[:, :], ALU.mult, r=[bk, pv], w=[bonv[hp]])
            if STOP <= 2.5:
                continue
            for ck in range(4):
                cs = slice(ck * 128, (ck + 1) * 128)
                bk = kb.bank()
                bkb = bk[:, :].bitcast(BF16)
                srcs = (atT[hp][:, cs], bhT[hp][:, cs], khT[hp][:, cs], vTb[hp][:, cs])
                for i_, s_ in enumerate(srcs):
                    kb.tr(bkb[:, i_ * 128:(i_ + 1) * 128], s_, identb, r=[atT[hp], bhT[hp], khT[hp], vTb[hp], cbf], w=[bk])
                kb.cp(TM4[hp][ck][:, :, :], bkb[:, 0:512].rearrange("p (a x) -> p a x", a=4), r=[bk], w=[TM4[hp][ck]])

        if STOP <= 3:
            continue
        for batch in range(2):
            pairs = [(hp, ck) for ck in (2 * batch, 2 * batch + 1) for hp in range(2)]
            sl = {pr_: i_ for i_, pr_ in enumerate(pairs)}
            for (hp, ck) in pairs:
                s_ = sl[(hp, ck)]
                cs = slice(ck * 128, (ck + 1) * 128)
                b1, b2 = kb.bank(), kb.bank()
                arv = AR[hp][:, ck, :, :, :].rearrange("p h a x -> p (h a x)")
                kb.mm(b1[:, :], btT[hp][:, cs], arv, r=[btT[hp], AR[hp]], w=[b1])
                kb.mm(b2[:, :], ktT[hp][:, cs], arv, r=[ktT[hp], AR[hp]], w=[b2])
                kb.tt(A1m[s_][:, :, :], b1[:, :].rearrange("p (h x) -> p h x", h=2), mask512[:, :, :], ALU.mult,
                      r=[b1, mask512], w=[A1m[s_]])
                kb.tt(A2m[s_][:, :, :], b2[:, :].rearrange("p (h x) -> p h x", h=2), mask512[:, :, :], ALU.mult,
                      r=[b2, mask512], w=[A2m[s_]])
                b3 = kb.bank()
                b3b = b3[:, :].bitcast(BF16)
                for h in range(2):
                    kb.tr(b3b[:, h * 128:(h + 1) * 128], A1m[s_][:, h, 0:128], identb, r=[A1m[s_], cbf], w=[b3])
                kb.cp(QT0[s_][:, :, :], b3b[:, 0:256].rearrange("p (h x) -> p h x", h=2), r=[b3], w=[QT0[s_]], eng="act")
                kb.tt(MTb[s_][0][:, :, :], A1m[s_][:, :, 0:128], identb.unsqueeze(1).to_broadcast([128, 2, 128]), ALU.add,
                      r=[A1m[s_], cbf], w=[MTb[s_][0]])
            if STOP <= 3.1:
                continue
            for k in range(1, 7):
                for (hp, ck) in pairs:
                    s_ = sl[(hp, ck)]
                    bx = kb.bank()
                    cur = QX[s_][k % 2]
                    for h in range(2):
                        if k == 1:
                            X, XT, rd = A1m[s_][:, h, 0:128], QT0[s_][:, h, :], [A1m[s_], QT0[s_]]
                        else:
                            prv = QX[s_][(k - 1) % 2]
                            X, XT, rd = prv[:, h, 0:128], prv[:, h, 128:256], [prv]
                        if k < 6:
                            kb.mm(bx[:, h * 256:h * 256 + 128], XT, X, r=rd, w=[bx])
                        kb.mm(bx[:, h * 256 + 128:h * 256 + 256], X, XT, r=rd, w=[bx])
                    if k < 6:
                        kb.cp(cur[:, :, :], bx[:, :].rearrange("p (h x) -> p h x", h=2), r=[bx], w=[cur], eng="act")
                    else:
                        kb.cp(cur[:, :, 128:256], bx[:, :].rearrange("p (h x) -> p h x", h=2)[:, :, 128:256], r=[bx], w=[cur], eng="act")
                    bm = kb.bank()
                    mprev = MTb[s_][(k - 1) % 2]
                    mcur = MTb[s_][k % 2]
                    for h in range(2):
                        kb.mm(bm[:, h * 128:(h + 1) * 128], cur[:, h, 128:256], mprev[:, h, :], r=[cur, mprev], w=[bm])
                    kb.tt(mcur[:, :, :], bm[:, 0:256].rearrange("p (h x) -> p h x", h=2), mprev[:, :, :], ALU.add,
                          r=[bm, mprev], w=[mcur])
            if STOP <= 3.2:
                continue
            for (hp, ck) in pairs:
                s_ = sl[(hp, ck)]
                tm = TM4[hp][ck]
                bw = kb.bank()
                for h in range(2):
                    kb.mm(bw[:, h * 64:(h + 1) * 64], A2m[s_][:, h, 0:128], tm[:, 3, h * 64:(h + 1) * 64], r=[A2m[s_], tm], w=[bw])
                kb.cp(Wl[s_][:, :, :], bw[:, 0:128].rearrange("p (h x) -> p h x", h=2), r=[bw], w=[Wl[s_]], eng="act")
            if STOP <= 3.3:
                continue
            for (hp, ck) in pairs:
                s_ = sl[(hp, ck)]
                tm = TM4[hp][ck]
                mt = MTb[s_][0]
                ba = kb.bank()
                for h in range(2):
                    kb.mm(ba[:, h * 128:h * 128 + 64], mt[:, h, :], tm[:, 0, h * 64:(h + 1) * 64], r=[mt, tm], w=[ba])
                    kb.mm(ba[:, h * 128 + 64:(h + 1) * 128], mt[:, h, :], Wl[s_][:, h, :], r=[mt, Wl[s_]], w=[ba])
                kb.cp(AU[s_][:, :, :], ba[:, 0:256].rearrange("p (h x) -> p h x", h=2), r=[ba], w=[AU[s_]])
            if STOP <= 3.4:
                continue
            for (hp, ck) in pairs:
                s_ = sl[(hp, ck)]
                tm = TM4[hp][ck]
                br = kb.bank()
                for h in range(2):
                    hr = slice(h * 64, (h + 1) * 64)
                    kb.mm(br[hr, 0:128], AU[s_][:, h, 0:64], A1m[s_][:, h, 128:256], start=True, stop=False,
                          r=[AU[s_], A1m[s_]], w=[br], tp=(0, h * 64))
                    kb.mm(br[hr, 0:128], identb[:, hr], rtT[hp][:, ck * 128:(ck + 1) * 128], start=False, stop=True,
                          r=[cbf, rtT[hp]], w=[br], tp=(0, h * 64))
                kb.cp(RpT[s_][:, :], br[:, 0:128], r=[br], w=[RpT[s_]], eng="act")
                bp = kb.bank()
                for h in range(2):
                    hr = slice(h * 64, (h + 1) * 64)
                    kb.mm(bp[hr, 0:64], AU[s_][:, h, 0:64], tm[:, 1, hr], r=[AU[s_], tm], w=[bp], tp=(0, h * 64))
                    kb.mm(bp[hr, 64:128], tm[:, 1, hr], AU[s_][:, h, 64:128], start=True, stop=False, r=[AU[s_], tm], w=[bp],
                          tp=(0, h * 64))
                    kb.mm(bp[hr, 64:128], tm[:, 2, hr], tm[:, 3, hr], start=False, stop=True, r=[tm], w=[bp], tp=(0, h * 64))
                for h in range(2):
                    hr = slice(h * 64, (h + 1) * 64)
                    kb.stt(PTk[s_][hr, hr], i2[hr, :], gC[hp][hr, ck:ck + 1], bp[hr, 0:64], ALU.mult, ALU.add,
                           r=[i2, gC[hp], bp], w=[PTk[s_]])
                kb.cp(Sloc[s_][:, :], bp[:, 64:128], r=[bp], w=[Sloc[s_]], eng="act")
            if STOP <= 3.5:
                continue
            for ck in (2 * batch, 2 * batch + 1):
                for hp in range(2):
                    s_ = sl[(hp, ck)]
                    tm = TM4[hp][ck]
                    by = kb.bank()
                    for h in range(2):
                        hr = slice(h * 64, (h + 1) * 64)
                        kb.mm(by[hr, 0:128], AU[s_][:, h, 64:128], A1m[s_][:, h, 128:256], start=True, stop=False,
                              r=[AU[s_], A1m[s_]], w=[by], tp=(0, h * 64))
                        kb.mm(by[hr, 0:128], tm[:, 3, hr], A2m[s_][:, h, 128:256], start=False, stop=False,
                              r=[tm, A2m[s_]], w=[by], tp=(0, h * 64))
                        kb.mm(by[hr, 0:128], STk[hp][:, hr], RpT[s_][:, :], start=False, stop=True,
                              r=[STk[hp], RpT[s_]], w=[by], tp=(0, h * 64))
                    kb.cp(YT[hp][:, ck * 128:(ck + 1) * 128], by[:, 0:128], r=[by], w=[YT[hp]], eng="act")
                    bs = kb.bank()
                    kb.mm(bs[:, 0:64], PTk[s_][:, :], STb[hp][:, :], r=[PTk[s_], STb[hp]], w=[bs])
                    kb.tt(STb[hp][:, :], bs[:, 0:64], Sloc[s_][:, :], ALU.add, r=[bs, Sloc[s_]], w=[STb[hp]])
                    for h in range(2):
                        hr = slice(h * 64, (h + 1) * 64)
                        kb.cp(STk[hp][hr, hr], STb[hp][hr, :], r=[STb[hp]], w=[STk[hp]], eng="act")

        if STOP <= 4:
            continue
        for hp in range(2):
            hs = slice(hp * 128, (hp + 1) * 128)
            yc, ysq, rsd = ft[0], ft[1], ft[2]
            bk = kb.bank()
            kb.mm(bk[:, :], block32, YT[hp][:, :], r=[cstb, YT[hp]], w=[bk])
            kb.stt(yc[:, :], bk[:, :], -1.0 / 64.0, YT[hp][:, :], ALU.mult, ALU.add, r=[bk, YT[hp]], w=[yc])
            kb.act(ysq[:, :], yc[:, :], AF.Square, r=[yc], w=[ysq])
            bk = kb.bank()
            kb.mm(bk[:, :], block32, ysq[:, :], r=[cstb, ysq], w=[bk])
            kb.act(rsd[:, :], bk[:, :], AF.Sqrt, r=[bk, cstb], w=[rsd], bias=cst[:, 770:771], scale=1.0)
            S.add("dve", lambda e, rsd=rsd: e.reciprocal(out=rsd[:, :], in_=rsd[:, :]), reads=_bl([rsd]), writes=_bl([rsd]))
            kb.tt(yc[:, :], yc[:, :], rsd[:, :], ALU.mult, r=[yc, rsd], w=[yc])
            kb.ts(yc[:, :], yc[:, :], d[:, Q_GNW8 + hp:Q_GNW8 + hp + 1], p[:, P_GNB + hp:P_GNB + hp + 1], ALU.mult, ALU.add,
                  r=[yc, d, p], w=[yc])
            kb.tt(yc[:, :], yc[:, :], bonv[hp][:, :], ALU.add, r=[yc, bonv[hp]], w=[yc])
            bk = kb.bank()
            kb.mm(bk[:, :], gupb[:, 0, hs], sgd0[:, :], start=True, stop=False, r=[gupb, sgd0], w=[bk])
            kb.mm(bk[:, :], gupb[:, 1, hs], sgd1[:, :], start=False, stop=True, r=[gupb, sgd1], w=[bk])
            kb.tt(ybf[hp][:, :], yc[:, :], bk[:, :], ALU.mult, r=[yc, bk], w=[ybf[hp]])
            yq, yc0 = ti // 2, (ti % 2) * TT2
            kb.dma("sp", io["yb%d" % yq][hp * 128:(hp + 1) * 128, yc0:yc0 + TT2], ybf[hp][:, :], r=[ybf[hp]], w=[kb.dbuf["yb%d" % yq]])

        if STOP <= 5:
            continue
        nkb = 4 * ti + 4
        pti = 0
        for hd in range(2):
            vs = slice(hd * 128, (hd + 1) * 128)
            for c in range(2):
                hr = slice(c * 64, (c + 1) * 64)
                bo, bl = kb.bank(), kb.bank()
                kb.reserved = {i_ for i_, t_ in enumerate(kb.banks) if t_ is bo or t_ is bl}
                def score(kbi, hd=hd, c=c):
                    nonlocal pti
                    off = max(0, kbi - 4 * ti) * 128
                    n = TT2 - off
                    diag = kbi >= 4 * ti
                    bsc = kb.bank()
                    kb.mm(bsc[:, 0:n], KT[hd][:, kbi * 128:(kbi + 1) * 128], Qz[hd][c][:, off:TT2], start=True, stop=(not diag),
                          r=[KT[hd], Qz[hd][c]], w=[bsc])
                    if diag:
                        kb.mm(bsc[:, 0:128], identb, maskb, start=False, stop=True, r=[cbf], w=[bsc])
                    pt_ = PTb[pti % len(PTb)]
                    pti += 1
                    kb.act(pt_[:, 0:n], bsc[:, 0:n], AF.Exp, r=[bsc], w=[pt_], scale=0.125)
                    return (kbi, pt_, off, n)

                def consume(item, vs=vs, bo=bo, bl=bl):
                    kbi, pt_, off, n = item
                    kb.mm(bo[:, off:TT2], Vtm[:, kbi, vs], pt_[:, 0:n], start=(kbi == 0), stop=(kbi == nkb - 1), r=[Vtm, pt_], w=[bo])
                    kb.mm(bl[:, off:TT2], onesb, pt_[:, 0:n], start=(kbi == 0), stop=(kbi == nkb - 1), r=[cbf, pt_], w=[bl])

                pend = []
                for kbi in range(nkb):
                    pend.append(score(kbi))
                    if len(pend) > LOOK:
                        consume(pend.pop(0))
                while pend:
                    consume(pend.pop(0))
                kb.reserved = set()
                S.add("dve", lambda e, bl=bl: e.reciprocal(out=rec[:, :], in_=bl[:, :]), reads=_bl([bl]), writes=_bl([rec]))
                kb.tt(osb[c][:, :], bo[:, :], rec[:, :], ALU.mult, r=[bo, rec], w=[osb[c]])
            kb.stt(od[:, :], osb[1][:, :], d[:, Q_NLAM:Q_NLAM + 1], osb[0][:, :], ALU.mult, ALU.add, r=[osb[0], osb[1], d], w=[od])
            kb.act(osb[0][:, :], od[:, :], AF.Square, r=[od], w=[osb[0]])
            bk = kb.bank()
            kb.mm(bk[:, :], ones32, osb[0][:, :], r=[cstb, osb[0]], w=[bk])
            kb.act(rec[:, :], bk[:, :], AF.Sqrt, r=[bk, cstb], w=[rec], bias=cst[:, 769:770], scale=1.0)
            S.add("dve", lambda e: e.reciprocal(out=rec[:, :], in_=rec[:, :]), reads=_bl([rec]), writes=_bl([rec]))
            kb.stt(ydb[hd][:, :], od[:, :], d[:, Q_SUBLN:Q_SUBLN + 1], rec[:, :], ALU.mult, ALU.mult, r=[od, d, rec], w=[ydb[hd]])
            yq, yc0 = ti // 2, (ti % 2) * TT2
            kb.dma("sp", io["yb%d" % yq][256 + hd * 128:256 + (hd + 1) * 128, yc0:yc0 + TT2], ydb[hd][:, :], r=[ydb[hd]],
                   w=[kb.dbuf["yb%d" % yq]])
        if ti % 2 == 1:
            collective(kb, "yb%d" % (ti // 2), "yg%d" % (ti // 2))


def _chunks(v):
    return np.ascontiguousarray(v.reshape(8, 128).T)


def own_cols(g):
    a = np.arange
    return np.concatenate([g * 256 + a(256), 512 + g * 256 + a(256), 1024 + g * 256 + a(256), 1536 + a(288),
                           1824 + g * 256 + a(256), 2336 + g * 256 + a(256), 2848 + g * 256 + a(256)])


def make_prm(inp, g):
    p = np.zeros((128, NPRM), np.float32)
    p[:, P_F1PRE:P_F1PRE + 8] = _chunks(inp["ffn1_pre_g"][0])
    p[:, P_F1POST:P_F1POST + 8] = _chunks(inp["ffn1_post_g"][0])
    p[:, P_MPRE:P_MPRE + 8] = _chunks(inp["mix_pre_g"][0])
    p[:, P_MPOST:P_MPOST + 8] = _chunks(inp["mix_post_g"][0])
    p[:, P_F2PRE:P_F2PRE + 8] = _chunks(inp["ffn2_pre_g"][0])
    p[:, P_F2POST:P_F2POST + 8] = _chunks(inp["ffn2_post_g"][0])
    oc = own_cols(g)
    mu = inp["shift_mu"][0]
    for c in range(8):
        p[:, P_MU + c] = mu[oc[c * 128:(c + 1) * 128]]
    p[0:32, P_MU + 8] = mu[oc[1024:1056]]
    ch = slice(g * 256, (g + 1) * 256)
    for name, col in (("rwkv_k_k", P_KK), ("rwkv_k_a", P_KA), ("rwkv_a0", P_A0), ("rwkv_r_k", P_RK),
                      ("rwkv_gn_w", P_GNW), ("rwkv_gn_b", P_GNB)):
        v = inp[name][0].reshape(-1)[ch]
        p[:, col] = v[0:128]
        p[:, col + 1] = v[128:256]
    p[:, P_SUBLN] = inp["diff_subln_w"][0]
    p[:, P_SEL + g] = 1.0
    return p


def make_core_inputs(inp, core, shared):
    b, g = core // 2, core % 2
    oc = own_cols(g)
    ch = slice(g * 256, (g + 1) * 256)
    m = dict(shared)
    m["prm"] = make_prm(inp, g)
    m["xT"] = np.ascontiguousarray(inp["x"][b, g * TH:(g + 1) * TH, :].T)
    m["win"] = np.ascontiguousarray(inp["w_in"][0][:, oc])
    z63 = np.zeros((63, 256), np.float32)
    z64 = np.zeros((64, 256), np.float32)
    m["wupw"] = np.ascontiguousarray(np.concatenate([inp["rwkv_w_up"][0][:, ch], inp["rwkv_w0"][0][ch][None, :], z63], 0))
    m["aup"] = np.ascontiguousarray(np.concatenate([z64, inp["rwkv_a_up"][0][:, ch]], 0))
    gu = np.zeros((128, 2, 256), np.float32)
    gu[:, 0, :] = inp["rwkv_g_up"][0][0:128, ch]
    gu[0:32, 1, :] = inp["rwkv_g_up"][0][128:160, ch]
    m["gup"] = gu
    return m


def make_shared(inp):
    wo = inp["w_o"][0]
    return {
        "cst": make_consts(),
        "f1g": inp["ffn1_w_gate"][0], "f1u": inp["ffn1_w_up"][0], "f1d": inp["ffn1_w_down"][0],
        "f2g": inp["ffn2_w_gate"][0], "f2u": inp["ffn2_w_up"][0], "f2d": inp["ffn2_w_down"][0],
        "wo": np.ascontiguousarray(np.concatenate([wo[0:256], wo[512:768], wo[256:512], wo[768:1024]], 0)),
        "lamv": np.ascontiguousarray(np.concatenate([inp["diff_lam_q1"][0], inp["diff_lam_k1"][0],
                                                     inp["diff_lam_q2"][0], inp["diff_lam_k2"][0]])[None, :]),
    }


_CACHE = {}


def kernel(**inputs):
    inp = {k: np.asarray(v, dtype=np.float32) for k, v in inputs.items()}
    if "nc" not in _CACHE:
        _CACHE["nc"] = build("full")[0]
    nc = _CACHE["nc"]
    shared = make_shared(inp)
    in_maps = [make_core_inputs(inp, c, shared) for c in range(8)]
    res = run_bass_kernel_spmd(nc, in_maps, core_ids=list(range(8)))
    out = np.empty((4, T, D), np.float32)
    for c in range(8):
        b, g = c // 2, c % 2
        out[b, g * TH:(g + 1) * TH, :] = res.results[c]["outT"].T
    return out
```

```python
import os
import numpy as np
from contextlib import ExitStack
import concourse.bass as bass
import concourse.mybir as mybir
from concourse.bass_utils import run_bass_kernel_spmd

F32 = mybir.dt.float32
BF16 = mybir.dt.bfloat16
AF = mybir.ActivationFunctionType
ALU = mybir.AluOpType
AX = mybir.AxisListType

D = 1024
DFF = 2816
NFC = 22
T = 4096
TH = 2048
TT = 1024
NCOLS = 1824
NORM_EPS = 1e-6
GN_EPS = 64e-5
SUBLN_EPS = 1e-5
LAMBDA_INIT = 0.8 - 0.6 * 1.0
PAIRS = [[0, 1], [2, 3], [4, 5], [6, 7]]
NCST = 1160


class Buf:
    __slots__ = ("name", "lw", "rd")

    def __init__(self, name, lw=None):
        self.name = name
        self.lw = lw
        self.rd = []


class Op:
    __slots__ = ("eng", "fn", "deps", "idx", "dma", "target", "val", "dsem", "dval", "prewait", "inc")


class Sched:
    NDSEM = 12

    def __init__(self, nc, stack):
        self.nc = nc
        self.stack = stack
        self.streams = {"pe": [], "act": [], "dve": [], "pool": [], "sp": []}
        self.epoch = None
        self.bufs = []
        self.capture = None

    def buf(self, name):
        b = Buf(name, self.epoch)
        self.bufs.append(b)
        return b

    def add(self, eng, fn, reads=(), writes=(), dma=False, inc=None):
        if self.capture is not None:
            self.capture.append((eng, fn, list(reads), list(writes), dma, inc))
            return None
        op = Op()
        op.eng = eng
        op.fn = fn
        op.dma = dma
        op.target = False
        op.val = None
        op.prewait = None
        op.dsem = None
        op.inc = inc
        deps = set()
        for b in reads:
            if b.lw is not None:
                deps.add(b.lw)
        for b in writes:
            if b.lw is not None:
                deps.add(b.lw)
            deps.update(b.rd)
        if eng == "pe" and not dma:
            deps = {d for d in deps if not (d.eng == "pe" and not d.dma)}
        op.deps = deps
        for b in reads:
            b.rd.append(op)
        for b in writes:
            b.lw = op
            b.rd = []
        op.idx = len(self.streams[eng])
        self.streams[eng].append(op)
        return op

    def barrier(self):
        scr = self._scr
        op = self.add("dve", lambda e: e.memset(scr[0:1, 0:1], 0.0), writes=list(self.bufs))
        self.epoch = op
        self.bufs = []
        return op

    def emit(self):
        nc = self.nc
        st = self.stack
        sems = {e: st.enter_context(nc.semaphore("s_" + e)) for e in ("pe", "act", "dve", "pool")}
        dsems = {q: [st.enter_context(nc.semaphore("d_%s%d" % (q, i))) for i in range(self.NDSEM)]
                 for q in ("sp", "pool")}
        for e, ops in self.streams.items():
            for op in ops:
                best = {}
                dd = []
                for d in op.deps:
                    if d.dma:
                        dd.append(d)
                    elif d.eng not in best or best[d.eng].idx < d.idx:
                        best[d.eng] = d
                op.deps = list(best.values()) + dd
                for d in op.deps:
                    d.target = True
        lastops = []
        for e in ("pe", "act", "dve", "pool"):
            comp = [op for op in self.streams[e] if not op.dma]
            if comp:
                comp[-1].target = True
                lastops.append(comp[-1])
            c = 0
            for op in self.streams[e]:
                if op.dma:
                    continue
                if op.target:
                    c += 1
                    op.val = c
        final_waits = []
        for q in ("sp", "pool"):
            i = 0
            last = {}
            for op in self.streams[q]:
                if not op.dma:
                    continue
                k = i % self.NDSEM
                inc = op.inc if op.inc is not None else 16
                prev = last.get(k, 0)
                op.dsem = dsems[q][k]
                op.dval = prev + inc
                if prev > 0:
                    op.prewait = (op.dsem, prev)
                last[k] = op.dval
                i += 1
            final_waits += [(dsems[q][k], v) for k, v in last.items()]
        engobj = {"pe": "tensor", "act": "scalar", "dve": "vector", "pool": "gpsimd", "sp": "sync"}
        self.nwaits = 0
        self.ninst = 0

        def run_stream(e, eng):
            waited = {}

            def wait(sem, val):
                k = id(sem)
                if waited.get(k, 0) >= val:
                    return
                waited[k] = val
                eng.wait_ge(sem, val)
                self.nwaits += 1
            for op in self.streams[e]:
                if op.prewait is not None:
                    wait(*op.prewait)
                for d in op.deps:
                    if d.dma:
                        wait(d.dsem, d.dval)
                    else:
                        wait(sems[d.eng], d.val)
                ins = op.fn(eng)
                self.ninst += 1
                if op.dma:
                    ins.then_inc(op.dsem, op.inc if op.inc is not None else 16)
                elif op.target:
                    ins.then_inc(sems[e], 1)
            if e == "sp":
                for s, v in final_waits:
                    wait(s, v)
                for lo in lastops:
                    wait(sems[lo.eng], lo.val)

        with nc.Block() as block:
            for e in ("sp", "pool", "act", "dve", "pe"):
                getattr(block, engobj[e])(lambda eng, e=e: run_stream(e, eng))


class Tn:
    __slots__ = ("ap", "b")

    def __init__(self, ap, b):
        self.ap = ap
        self.b = b

    def __getitem__(self, k):
        return self.ap[k]


class Arena:
    def __init__(self, S, ap, nwords):
        self.S = S
        self.ap = ap
        self.n = nwords
        self.off = 0
        self.peak = 0

    def alloc(self, name, shape, dt):
        free = 1
        for s in shape[1:]:
            free *= s
        esz = 4 if dt == F32 else 2
        words = (free * esz + 3) // 4
        words = (words + 7) // 8 * 8
        assert self.off + words <= self.n, "arena overflow at %s: %d + %d > %d" % (name, self.off, words, self.n)
        v = self.ap[:, self.off:self.off + words]
        self.off += words
        self.peak = max(self.peak, self.off)
        if dt != F32:
            v = v.bitcast(dt)
        v = v[0:shape[0], 0:free]
        if len(shape) == 3:
            v = v.rearrange("p (a b) -> p a b", a=shape[1])
        elif len(shape) == 4:
            v = v.rearrange("p (a b c) -> p a b c", a=shape[1], b=shape[2])
        return Tn(v, self.S.buf(name))

    def allocn(self, name, n, shape, dt):
        return [self.alloc("%s%d" % (name, i), shape, dt) for i in range(n)]


def _bl(x):
    out = []
    for t in x:
        if t is None:
            continue
        out.append(t.b if isinstance(t, Tn) else t)
    return out


class KB:
    def __init__(self, nc, st, mode):
        self.nc = nc
        self.mode = mode
        self.S = Sched(nc, st)
        S = self.S
        self.banks = []
        for i in range(8):
            t = st.enter_context(nc.psum_tensor("psb%d" % i, [128, 512], F32))
            self.banks.append(Tn(t, None))
        self.pool = None
        self.pool_i = {}
        self.reserved = set()
        arena_words = 53208 - NCST - 16
        at = st.enter_context(nc.sbuf_tensor("arena", [128, arena_words], F32))
        self.cst = st.enter_context(nc.sbuf_tensor("cst_sb", [128, NCST], F32))
        self.A = Arena(S, at, arena_words)
        self.bscr = st.enter_context(nc.sbuf_tensor("bscr", [128, 8], F32))
        S._scr = self.bscr
        self.new_epoch_banks()

    def new_epoch_banks(self):
        for t in self.banks:
            t.b = self.S.buf("bank")

    POOLS = {None: list(range(8)), "a": [0, 1, 2, 3, 4], "r": [5, 6, 7]}

    def bank(self):
        pool = self.POOLS[self.pool]
        while True:
            k = self.pool_i.get(self.pool, 0)
            self.pool_i[self.pool] = k + 1
            i = pool[k % len(pool)]
            if i not in self.reserved:
                return self.banks[i]

    def mm(self, out, lhsT, rhs, start=True, stop=True, r=(), w=(), tp=None):
        kw = {} if tp is None else {"tile_position": tp}
        self.S.add("pe", lambda e: e.matmul(out, lhsT=lhsT, rhs=rhs, start=start, stop=stop, **kw),
                   reads=_bl(r), writes=_bl(w))

    def tr(self, out, in_, ident, r=(), w=()):
        self.S.add("pe", lambda e: e.transpose(out, in_, ident), reads=_bl(r), writes=_bl(w))

    def act(self, out, in_, func, r=(), w=(), bias=None, scale=None):
        kw = {}
        if bias is not None:
            kw["bias"] = bias
        if scale is not None:
            kw["scale"] = scale
        self.S.add("act", lambda e: e.activation(out=out, in_=in_, func=func, **kw), reads=_bl(r), writes=_bl(w))

    def ts(self, out, in0, s1, s2, op0, op1=None, r=(), w=(), eng="dve"):
        if op1 is None:
            self.S.add(eng, lambda e: e.tensor_scalar(out=out, in0=in0, scalar1=s1, scalar2=None, op0=op0),
                       reads=_bl(r), writes=_bl(w))
        else:
            self.S.add(eng, lambda e: e.tensor_scalar(out=out, in0=in0, scalar1=s1, scalar2=s2, op0=op0, op1=op1),
                       reads=_bl(r), writes=_bl(w))

    def tt(self, out, in0, in1, op, r=(), w=(), eng="dve"):
        self.S.add(eng, lambda e: e.tensor_tensor(out=out, in0=in0, in1=in1, op=op), reads=_bl(r), writes=_bl(w))

    def stt(self, out, in0, scalar, in1, op0, op1, r=(), w=()):
        self.S.add("dve", lambda e: e.scalar_tensor_tensor(out=out, in0=in0, scalar=scalar, in1=in1, op0=op0, op1=op1),
                   reads=_bl(r), writes=_bl(w))

    def cp(self, out, in_, r=(), w=(), eng="dve"):
        if eng == "act":
            self.S.add("act", lambda e: e.copy(out, in_), reads=_bl(r), writes=_bl(w))
        else:
            self.S.add(eng, lambda e: e.tensor_copy(out=out, in_=in_), reads=_bl(r), writes=_bl(w))

    def dma(self, q, out, in_, r=(), w=()):
        self.S.add(q, lambda e: e.dma_start(out=out, in_=in_), reads=_bl(r), writes=_bl(w), dma=True)


def make_consts():
    c = np.zeros((128, NCST), np.float32)
    c[:, 0:128] = np.eye(128)
    c[:, 128:256] = 1.0
    bo = np.zeros((128, 128), np.float32)
    bo[0:64, 0:64] = 1.0
    bo[64:, 64:] = 1.0
    c[:, 256:384] = bo
    k = np.arange(128)[:, None]
    q = np.arange(128)[None, :]
    c[:, 384:512] = np.where(k > q, -30000.0, 0.0)
    c[:, 512:640] = (k < q).astype(np.float32)
    c[:, 640:768] = (k <= q).astype(np.float32)
    c[:, 768] = D * NORM_EPS
    c[:, 769] = 128 * SUBLN_EPS
    c[:, 770] = 64 * GN_EPS
    c[:, 771] = 0.0
    c[:, 772] = 1.0
    c[:, 776:904] = (k <= q).astype(np.float32) * (-float(np.exp(-0.5)))
    c[:, 904:1032] = (k < q).astype(np.float32) * (-float(np.exp(-0.5)))
    c[:, 1032:1160] = (q < k).astype(np.float32)
    return c


P_F1PRE, P_F1POST, P_MPRE, P_MPOST, P_F2PRE, P_F2POST = 0, 8, 16, 24, 32, 40
P_MU = 48
P_KK = 57
P_KA = 59
P_A0 = 61
P_RK = 63
P_GNW = 65
P_GNB = 67
P_SUBLN = 69
P_SEL = 70
NPRM = 72
Q_F1PRE, Q_F1POST, Q_MPRE, Q_MPOST, Q_F2PRE, Q_F2POST = 0, 8, 16, 24, 32, 40
Q_OMU = 48
Q_SUBLN = 57
Q_LAM = 58
Q_NLAM = 59
Q_GNW8 = 60
Q_A0H = 62
NDER = 64


def build(mode="full"):
    nc = bass.Bass("TRN2", target_bir_lowering=False)
    ph1 = mode in ("full", "p1")
    ph2 = mode in ("full", "p2")
    ph3 = mode in ("full", "p3")

    def dram(name, shape, dt, kind):
        if kind == "Internal":
            return nc.dram_tensor(name, shape, dt).ap()
        return nc.dram_tensor(name, shape, dt, kind=kind).ap()

    IN, OUT, INT = "ExternalInput", "ExternalOutput", "Internal"
    io = {}
    io["cst"] = dram("cst", [128, NCST], F32, IN)
    io["prm"] = dram("prm", [128, NPRM], F32, IN)
    if ph1:
        io["xT"] = dram("xT", [D, TH], F32, IN)
        io["f1g"] = dram("f1g", [D, DFF], F32, IN)
        io["f1u"] = dram("f1u", [D, DFF], F32, IN)
        io["f1d"] = dram("f1d", [DFF, D], F32, IN)
    if ph2:
        io["win"] = dram("win", [D, NCOLS], F32, IN)
        io["wupw"] = dram("wupw", [128, 256], F32, IN)
        io["aup"] = dram("aup", [128, 256], F32, IN)
        io["gup"] = dram("gup", [128, 2, 256], F32, IN)
        io["lamv"] = dram("lamv", [1, 256], F32, IN)
    if ph3:
        io["wo"] = dram("wo", [D, D], F32, IN)
        io["f2g"] = dram("f2g", [D, DFF], F32, IN)
        io["f2u"] = dram("f2u", [D, DFF], F32, IN)
        io["f2d"] = dram("f2d", [DFF, D], F32, IN)
        io["outT"] = dram("outT", [D, TH], F32, OUT)
    def parts(name, n, shape, dt, kind):
        for i in range(n):
            io["%s%d" % (name, i)] = dram("%s%d" % (name, i), shape, dt, kind)

    if mode == "full":
        io["x1T"] = dram("x1T", [D, TH], F32, INT)
        parts("hb", 2, [D, TT], BF16, INT)
        parts("hg", 2, [2 * D, TT], BF16, INT)
        parts("yb", 4, [512, 1024], BF16, INT)
        parts("yg", 4, [1024, 1024], BF16, INT)
    elif mode == "p1":
        io["x1T"] = dram("x1T", [D, TH], F32, OUT)
        parts("hb", 2, [D, TT], BF16, OUT)
    elif mode == "p2":
        parts("hg", 2, [2 * D, TT], BF16, IN)
        parts("yb", 4, [512, 1024], BF16, OUT)
    elif mode == "p3":
        io["x1T"] = dram("x1T", [D, TH], F32, IN)
        parts("yg", 4, [1024, 1024], BF16, IN)

    with ExitStack() as st:
        kb = KB(nc, st, mode)
        kb.io = io
        prologue(kb)
        if ph1:
            phase_ffn(kb, 1)
        if ph2:
            kb.S.barrier()
            kb.new_epoch_banks()
            kb.A.off = kb.base_off
            phase_mixer(kb)
        if ph3:
            kb.S.barrier()
            kb.new_epoch_banks()
            kb.A.off = kb.base_off
            phase_ffn(kb, 3)
        kb.S.emit()
        kb.stats = (kb.S.ninst, kb.S.nwaits, kb.A.peak)
    return nc, kb


def collective(kb, srcname, dstname):
    S = kb.S
    if kb.mode != "full":
        return
    src, dst = kb.io[srcname], kb.io[dstname]
    bs = kb.dbuf[srcname]
    bd = kb.dbuf[dstname]
    S.add("pool", lambda e: e.collective_compute("AllGather", ALU.bypass, replica_groups=PAIRS,
                                                 ins=[src[:, :]], outs=[dst[:, :]]),
          reads=[bs], writes=[bd], dma=True, inc=1)


def prologue(kb):
    S, A, io = kb.S, kb.A, kb.io
    cb = S.buf("cst")
    kb.cstb = cb
    kb.dma("sp", kb.cst[:, :], io["cst"][:, :], w=[cb])
    kb.prm = A.alloc("prm", [128, NPRM], F32)
    kb.der = A.alloc("der", [128, NDER], F32)
    kb.dma("sp", kb.prm[:, :], io["prm"][:, :], w=[kb.prm])
    kb.cbf = A.alloc("cbf", [128, 512], BF16)
    kb.cp(kb.cbf[:, :], kb.cst[:, 0:512], r=[cb], w=[kb.cbf])
    kb.identb = kb.cbf.ap[:, 0:128]
    kb.onesb = kb.cbf.ap[:, 128:256]
    kb.blockb = kb.cbf.ap[:, 256:384]
    kb.maskb = kb.cbf.ap[:, 384:512]
    p, d = kb.prm, kb.der
    rw = dict(r=[p], w=[d])
    kb.ts(d[:, Q_F1PRE:Q_F1PRE + 8], p[:, P_F1PRE:P_F1PRE + 8], 32.0, None, ALU.mult, **rw)
    kb.ts(d[:, Q_F1POST:Q_F1POST + 8], p[:, P_F1POST:P_F1POST + 8], 16.0, None, ALU.mult, **rw)
    kb.ts(d[:, Q_MPRE:Q_MPRE + 8], p[:, P_MPRE:P_MPRE + 8], 32.0, None, ALU.mult, **rw)
    kb.ts(d[:, Q_MPOST:Q_MPOST + 8], p[:, P_MPOST:P_MPOST + 8], 32.0, None, ALU.mult, **rw)
    kb.ts(d[:, Q_F2PRE:Q_F2PRE + 8], p[:, P_F2PRE:P_F2PRE + 8], 32.0, None, ALU.mult, **rw)
    kb.ts(d[:, Q_F2POST:Q_F2POST + 8], p[:, P_F2POST:P_F2POST + 8], 16.0, None, ALU.mult, **rw)
    kb.ts(d[:, Q_OMU:Q_OMU + 9], p[:, P_MU:P_MU + 9], -1.0, 1.0, ALU.mult, ALU.add, **rw)
    kb.ts(d[:, Q_SUBLN:Q_SUBLN + 1], p[:, P_SUBLN:P_SUBLN + 1], (1.0 - LAMBDA_INIT) * float(np.sqrt(128.0)), None,
          ALU.mult, **rw)
    kb.ts(d[:, Q_A0H:Q_A0H + 2], p[:, P_A0:P_A0 + 2], 0.5, None, ALU.mult, **rw)
    kb.dbuf = {k: S.buf(k) for k in ["x1T"] + ["hb%d" % i for i in range(2)] + ["hg%d" % i for i in range(2)]
               + ["yb%d" % i for i in range(4)] + ["yg%d" % i for i in range(4)]}
    kb.base_off = A.off


def rms_rstd(kb, sq, sqr, rstd, ncols, nchunk=8, epscol=768):
    bk = kb.bank()
    for dc in range(nchunk):
        kb.mm(bk[:, 0:ncols], kb.onesb, sq[:, dc, 0:ncols], start=(dc == 0), stop=(dc == nchunk - 1),
              r=[kb.cbf] + sqr, w=[bk])
    kb.act(rstd[:, 0:ncols], bk[:, 0:ncols], AF.Sqrt, r=[bk, kb.cstb], w=[rstd], bias=kb.cst[:, epscol:epscol + 1], scale=1.0)
    kb.S.add("dve", lambda e: e.reciprocal(out=rstd[:, 0:ncols], in_=rstd[:, 0:ncols]), reads=_bl([rstd]), writes=_bl([rstd]))


def phase_ffn(kb, which):
    S, A, io = kb.S, kb.A, kb.io
    d = kb.der
    NTS = TT // 512
    xt = [[A.alloc("xt%d_%d" % (dc, t_), [128, 512], F32) for t_ in range(NTS)] for dc in range(8)]
    fo = [[A.alloc("fo%d_%d" % (dc, t_), [128, 512], F32) for t_ in range(NTS)] for dc in range(8)]
    hT = A.alloc("hT", [128, 8, TT], BF16)
    AT = [A.alloc("AT%d" % i, [128, TT], BF16) for i in range(NFC)]
    wg = A.allocn("wg", 3, [128, 8, 256], BF16)
    wu = A.allocn("wu", 3, [128, 8, 256], BF16)
    wd = A.allocn("wd", 2, [128, NFC, 256], BF16)
    sq = A.alloc("sq", [128, 8, 512], BF16)
    rstd = A.allocn("rstd", 2, [128, 512], F32)
    xt_all = [t for row in xt for t in row]
    if which == 3:
        wo = A.alloc("wo", [128, 8, D], BF16)
        kb.dma("pool", wo[:, :, :], io["wo"].rearrange("(kc p) f -> p kc f", p=128), w=[wo])
        Wg, Wu, Wd = io["f2g"], io["f2u"], io["f2d"]
        qpre, qpost = Q_F2PRE, Q_F2POST
    else:
        Wg, Wu, Wd = io["f1g"], io["f1u"], io["f1d"]
        qpre, qpost = Q_F1PRE, Q_F1POST

    def load_gu(fg):
        s = fg % 3
        kb.dma("pool", wg[s][:, :, :], Wg[:, fg * 256:(fg + 1) * 256].rearrange("(kc p) f -> p kc f", p=128), w=[wg[s]])
        kb.dma("pool", wu[s][:, :, :], Wu[:, fg * 256:(fg + 1) * 256].rearrange("(kc p) f -> p kc f", p=128), w=[wu[s]])

    def load_d(dcp):
        s = dcp % 2
        kb.dma("pool", wd[s][:, :, :], Wd[:, dcp * 256:(dcp + 1) * 256].rearrange("(fc p) d -> p fc d", p=128), w=[wd[s]])

    def norm_to_bf16(src, qcol, dst):
        for ts_ in range(NTS):
            cols = slice(ts_ * 512, (ts_ + 1) * 512)
            for dc in range(8):
                kb.act(sq[:, dc, :], src[dc][ts_][:, :], AF.Square, r=[src[dc][ts_]], w=[sq])
            rs = rstd[ts_ % 2]
            rms_rstd(kb, sq, [sq], rs, 512)
            for dc in range(8):
                kb.stt(dst[:, dc, cols], src[dc][ts_][:, :], d[:, qcol + dc:qcol + dc + 1], rs[:, :], ALU.mult, ALU.mult,
                       r=[src[dc][ts_], d, rs], w=[dst])

    def post_norm_residual(qcol):
        for ts_ in range(NTS):
            for dc in range(8):
                kb.act(sq[:, dc, :], fo[dc][ts_][:, :], AF.Square, r=[fo[dc][ts_]], w=[sq])
            rs = rstd[ts_ % 2]
            rms_rstd(kb, sq, [sq], rs, 512)
            for dc in range(8):
                f_, x_ = fo[dc][ts_], xt[dc][ts_]
                kb.stt(f_[:, :], f_[:, :], d[:, qcol + dc:qcol + dc + 1], rs[:, :], ALU.mult, ALU.mult,
                       r=[f_, d, rs], w=[f_])
                kb.tt(x_[:, :], x_[:, :], f_[:, :], ALU.add, r=[x_, f_], w=[x_])

    def xt_dma(dram_ap, tcols, to_dram, r=(), w=()):
        for dc in range(8):
            for ts_ in range(NTS):
                c0 = tcols.start + ts_ * 512
                dr = dram_ap[dc * 128:(dc + 1) * 128, c0:c0 + 512]
                if to_dram:
                    kb.dma("sp", dr, xt[dc][ts_][:, :], r=[xt[dc][ts_]] + list(r), w=list(w))
                else:
                    kb.dma("sp", xt[dc][ts_][:, :], dr, r=list(r), w=[xt[dc][ts_]] + list(w))

    ntile = TH // TT
    for ti in range(ntile):
        tcols = slice(ti * TT, (ti + 1) * TT)
        if which == 1:
            xt_dma(io["xT"], tcols, False)
        else:
            xt_dma(io["x1T"], tcols, False, r=[kb.dbuf["x1T"]])
        load_gu(0)
        load_gu(1)
        if which == 3:
            yA = [AT[i] for i in range(0, 8)]
            yB = [AT[i] for i in range(8, 16)]
            for kc in range(8):
                kb.dma("sp", yA[kc][:, :], io["yg%d" % ti][kc * 128:(kc + 1) * 128, :],
                       r=[kb.dbuf["yg%d" % ti]], w=[yA[kc]])
                kb.dma("sp", yB[kc][:, :], io["yg%d" % (2 + ti)][kc * 128:(kc + 1) * 128, :],
                       r=[kb.dbuf["yg%d" % (2 + ti)]], w=[yB[kc]])
            p = kb.prm
            for kc in range(8):
                kb.ts(yA[kc][:, :], yA[kc][:, :], p[:, P_SEL:P_SEL + 1], None, ALU.mult, r=[yA[kc], p], w=[yA[kc]])
                kb.stt(hT[:, kc, :], yB[kc][:, :], p[:, P_SEL + 1:P_SEL + 2], yA[kc][:, :], ALU.mult, ALU.add,
                       r=[yB[kc], yA[kc], p], w=[hT])
            for dc in range(8):
                for ts_ in range(NTS):
                    cols = slice(ts_ * 512, (ts_ + 1) * 512)
                    bk = kb.bank()
                    for kc in range(8):
                        kb.mm(bk[:, :], wo[:, kc, dc * 128:(dc + 1) * 128], hT[:, kc, cols], start=(kc == 0), stop=(kc == 7),
                              r=[wo, hT], w=[bk])
                    kb.cp(fo[dc][ts_][:, :], bk[:, :], r=[bk], w=[fo[dc][ts_]], eng="act")
            post_norm_residual(Q_MPOST)
        norm_to_bf16(xt, qpre, hT)
        load_d(0)
        load_d(1)
        sgi = 0
        fo_all = [t for row in fo for t in row]
        for fg in range(NFC // 2):
            s = fg % 3
            for fi in range(2):
                fc = fg * 2 + fi
                bg = [kb.bank() for _ in range(NTS)]
                bu = [kb.bank() for _ in range(NTS)]
                for (wt, bks) in ((wg[s], bg), (wu[s], bu)):
                    for ts_ in range(NTS):
                        for kc in range(8):
                            kb.mm(bks[ts_][:, :], wt[:, kc, fi * 128:(fi + 1) * 128], hT[:, kc, ts_ * 512:(ts_ + 1) * 512],
                                  start=(kc == 0), stop=(kc == 7), r=[wt, hT], w=[bks[ts_]])
                for ts_ in range(NTS):
                    sg = fo_all[sgi % 16]
                    sgi += 1
                    kb.act(sg[:, :], bg[ts_][:, :], AF.Silu, r=[bg[ts_]], w=[sg])
                    kb.tt(AT[fc][:, ts_ * 512:(ts_ + 1) * 512], sg[:, :], bu[ts_][:, :], ALU.mult, r=[sg, bu[ts_]], w=[AT[fc]])
            if fg + 2 < NFC // 2:
                load_gu(fg + 2)
        for dcp in range(4):
            s = dcp % 2
            for di in range(2):
                dc = dcp * 2 + di
                for ts_ in range(NTS):
                    cols = slice(ts_ * 512, (ts_ + 1) * 512)
                    bk = kb.bank()
                    for fc in range(NFC):
                        kb.mm(bk[:, :], wd[s][:, fc, di * 128:(di + 1) * 128], AT[fc][:, cols], start=(fc == 0), stop=(fc == NFC - 1),
                              r=[wd[s], AT[fc]], w=[bk])
                    kb.cp(fo[dc][ts_][:, :], bk[:, :], r=[bk], w=[fo[dc][ts_]], eng="act")
            if dcp + 2 < 4:
                load_d(dcp + 2)
        post_norm_residual(qpost)
        if which == 1:
            xt_dma(io["x1T"], tcols, True, w=[kb.dbuf["x1T"]])
            norm_to_bf16(xt, Q_MPRE, hT)
            kb.dma("sp", io["hb%d" % ti][:, :].rearrange("(dc p) t -> p dc t", p=128), hT[:, :, :], r=[hT], w=[kb.dbuf["hb%d" % ti]])
            collective(kb, "hb%d" % ti, "hg%d" % ti)
        else:
            xt_dma(io["outT"], tcols, True)


def phase_mixer(kb):
    S, A, io = kb.S, kb.A, kb.io
    d, p = kb.der, kb.prm
    cst, cstb = kb.cst, kb.cstb
    TT2 = 512
    NT2 = T // TT2
    ones32 = cst[:, 128:256]
    block32 = cst[:, 256:384]
    tri32 = cst[:, 776:1032]

    win = A.alloc("win", [128, 8, NCOLS], BF16)
    kb.dma("pool", win[:, :, :], io["win"].rearrange("(kc p) f -> p kc f", p=128), w=[win])
    wupw = A.alloc("wupw", [128, 256], F32)
    kb.dma("sp", wupw[:, :], io["wupw"][:, :], w=[wupw])
    aupb = A.alloc("aupb", [128, 256], BF16)
    kb.dma("pool", aupb[:, :], io["aup"][:, :], w=[aupb])
    gupb = A.alloc("gupb", [128, 2, 256], BF16)
    kb.dma("pool", gupb[:, :, :], io["gup"][:, :, :], w=[gupb])
    lamv = A.alloc("lamv", [128, 256], F32)
    kb.dma("sp", lamv[:, :], io["lamv"].partition_broadcast(128), w=[lamv])
    ltmp = A.alloc("ltmp", [128, 128], F32)
    lsum = A.alloc("lsum", [128, 2], F32)
    kb.tt(ltmp[:, 0:64], lamv[:, 0:64], lamv[:, 64:128], ALU.mult, r=[lamv], w=[ltmp])
    kb.tt(ltmp[:, 64:128], lamv[:, 128:192], lamv[:, 192:256], ALU.mult, r=[lamv], w=[ltmp])
    S.add("dve", lambda e: e.reduce_sum(out=lsum[:, 0:1], in_=ltmp[:, 0:64], axis=AX.X), reads=_bl([ltmp]), writes=_bl([lsum]))
    S.add("dve", lambda e: e.reduce_sum(out=lsum[:, 1:2], in_=ltmp[:, 64:128], axis=AX.X), reads=_bl([ltmp]), writes=_bl([lsum]))
    kb.act(lsum[:, :], lsum[:, :], AF.Exp, r=[lsum], w=[lsum])
    kb.tt(d[:, Q_LAM:Q_LAM + 1], lsum[:, 0:1], lsum[:, 1:2], ALU.subtract, r=[lsum], w=[d])
    kb.ts(d[:, Q_LAM:Q_LAM + 1], d[:, Q_LAM:Q_LAM + 1], LAMBDA_INIT, None, ALU.add, r=[d], w=[d])
    kb.ts(d[:, Q_NLAM:Q_NLAM + 1], d[:, Q_LAM:Q_LAM + 1], -1.0, None, ALU.mult, r=[d], w=[d])
    kb.ts(d[:, Q_GNW8:Q_GNW8 + 2], p[:, P_GNW:P_GNW + 2], 8.0, None, ALU.mult, r=[p], w=[d])
    mask512 = A.alloc("mask512", [128, 2, 256], BF16)
    for h in range(2):
        kb.cp(mask512[:, h, :], cst[:, 512:768], r=[cstb], w=[mask512])
    lowm = A.alloc("lowm", [128, 128], BF16)
    kb.cp(lowm[:, :], cst[:, 1032:1160], r=[cstb], w=[lowm])
    i2 = A.alloc("i2", [128, 64], F32)
    kb.tt(i2[:, :], cst[:, 0:64], cst[:, 64:128], ALU.add, r=[cstb], w=[i2])
    identb, onesb, blockb, maskb = kb.identb, kb.onesb, kb.blockb, kb.maskb
    cbf = kb.cbf

    KT = A.allocn("KT", 2, [128, T], BF16)
    Vtm = A.alloc("Vtm", [128, T // 128, 256], BF16)
    hT = A.alloc("hT2", [128, 8, TT2], BF16)
    rawt = A.allocn("rawt", 2, [128, TT2 + 1], F32)
    carry = A.alloc("carry", [128, 9], F32)
    S.add("dve", lambda e: e.memset(carry[:, :], 0.0), writes=_bl([carry]))
    psh = A.allocn("psh", 3, [128, TT2], F32)
    pl = [psh[1], psh[0]]
    tanhwd = A.alloc("tanhwd", [128, TT2], F32)
    lorab = A.alloc("lorab", [128, TT2], BF16)
    sgd0 = A.alloc("sgd0", [128, TT2], BF16)
    sgd1 = A.alloc("sgd1", [128, TT2], BF16)
    sgw = A.alloc("sgw", [128, 4, 256], F32)
    S.add("dve", lambda e: e.memset(tanhwd[:, :], 0.0), writes=_bl([tanhwd]))
    S.add("dve", lambda e: e.memset(tanhwd[64:65, :], 1.0), writes=_bl([tanhwd]))
    S.add("dve", lambda e: e.memset(lorab[:, :], 0.0), writes=_bl([lorab]))
    S.add("dve", lambda e: e.memset(sgd1[:, :], 0.0), writes=_bl([sgd1]))
    Qz = [A.allocn("Qz%d_" % hd, 2, [128, TT2], BF16) for hd in range(2)]
    for hd in range(2):
        for c in range(2):
            S.add("dve", lambda e, hd=hd, c=c: e.memset(Qz[hd][c][:, :], 0.0), writes=_bl([Qz[hd][c]]))
    ft = A.allocn("ft", 6, [128, TT2], F32)
    E12 = A.alloc("E12", [128, 4, 256], F32)
    E3 = A.alloc("E3", [128, 4, 128], F32)
    E4 = A.alloc("E4", [128, 4, 128], F32)
    gC = A.allocn("gC", 2, [128, 4], F32)
    bonv = A.allocn("bonv", 2, [128, TT2], F32)
    sqb = A.alloc("sqb", [128, TT2], BF16)
    ARl = A.allocn("AR", 2, [128, 4 * 2 * 2 * 128], BF16)
    AR = [Tn(t_.ap.rearrange("p (c h a x) -> p c h a x", c=4, h=2, a=2), t_.b) for t_ in ARl]
    for hp in range(2):
        S.add("dve", lambda e, hp=hp: e.memset(ARl[hp][:, :], 0.0), writes=_bl([ARl[hp]]))
    atT = A.allocn("atT", 2, [128, TT2], BF16)
    rtT = A.allocn("rtT", 2, [128, TT2], BF16)
    btT = A.allocn("btT", 2, [128, TT2], BF16)
    ktT = A.allocn("ktT", 2, [128, TT2], BF16)
    bhT = A.allocn("bhT", 2, [128, TT2], BF16)
    khT = A.allocn("khT", 2, [128, TT2], BF16)
    vTb = A.allocn("vTb", 2, [128, TT2], BF16)
    TM4 = [[A.alloc("TM4_%d_%d" % (hp, ck), [128, 4, 128], BF16) for ck in range(4)] for hp in range(2)]
    NSL = 4
    A1m = A.allocn("A1m", NSL, [128, 2, 256], BF16)
    A2m = A.allocn("A2m", NSL, [128, 2, 256], BF16)
    QT0 = A.allocn("QT0", NSL, [128, 2, 128], BF16)
    QX = [A.allocn("QX%d_" % s_, 2, [128, 2, 256], BF16) for s_ in range(NSL)]
    MTb = [A.allocn("MT%d_" % s_, 2, [128, 2, 128], BF16) for s_ in range(NSL)]
    Wl = A.allocn("Wl", NSL, [128, 2, 64], BF16)
    AU = A.allocn("AU", NSL, [128, 2, 128], BF16)
    RpT = A.allocn("RpT", NSL, [128, 128], BF16)
    Sloc = A.allocn("Sloc", NSL, [128, 64], F32)
    STb = A.allocn("STb", 2, [128, 64], BF16)
    STk = A.allocn("STk", 2, [128, 128], BF16)
    PTk = A.allocn("PTk", NSL, [128, 128], BF16)
    for hp in range(2):
        S.add("dve", lambda e, hp=hp: e.memset(STb[hp][:, :], 0.0), writes=_bl([STb[hp]]))
        S.add("dve", lambda e, hp=hp: e.memset(STk[hp][:, :], 0.0), writes=_bl([STk[hp]]))
    for s_ in range(NSL):
        S.add("dve", lambda e, s_=s_: e.memset(PTk[s_][:, :], 0.0), writes=_bl([PTk[s_]]))
    YT = A.allocn("YT", 2, [128, TT2], F32)
    ybf = A.allocn("ybf", 2, [128, TT2], BF16)
    LOOK = 2
    PTb = A.allocn("PTb", LOOK + 2, [128, TT2], BF16)
    osb = A.allocn("osb", 2, [128, TT2], F32)
    rec = A.alloc("rec", [128, TT2], F32)
    od = osb[0]
    ydb = A.allocn("ydb", 2, [128, TT2], BF16)

    def sigmoid_from(out, in_, r, w, tmp, bias=None):
        if bias is None:
            kb.act(tmp, in_, AF.Tanh, r=r, w=w, scale=0.5)
        else:
            kb.act(tmp, in_, AF.Tanh, r=r, w=w, bias=bias, scale=0.5)
        kb.ts(out, tmp, 0.5, 0.5, ALU.mult, ALU.add, r=w, w=w)

    def project_shift(c, col0, M, dst, ti):
        bk = kb.bank()
        for kc in range(8):
            kb.mm(bk[0:M, :], win[:, kc, col0:col0 + M], hT[:, kc, :], start=(kc == 0), stop=(kc == 7), r=[win, hT], w=[bk])
        rt = rawt[c % 2]
        kb.cp(rt[0:M, 1:TT2 + 1], bk[0:M, :], r=[bk], w=[rt], eng="act")
        kb.cp(rt[0:M, 0:1], carry[0:M, c:c + 1], r=[carry], w=[rt])
        kb.ts(dst[0:M, :], rt[0:M, 0:TT2], p[0:M, P_MU + c:P_MU + c + 1], None, ALU.mult, r=[rt, p], w=[dst])
        kb.stt(dst[0:M, :], rt[0:M, 1:TT2 + 1], d[0:M, Q_OMU + c:Q_OMU + c + 1], dst[0:M, :], ALU.mult, ALU.add,
               r=[rt, d, dst], w=[dst])
        kb.cp(carry[0:M, c:c + 1], rt[0:M, TT2:TT2 + 1], r=[rt], w=[carry])

    def tile_proj(ti):
        t0 = ti * TT2
        rk, tok = ti // 4, (ti % 4) * TT2
        hpart, c0 = tok // TT, tok % TT
        kb.dma("sp", hT[:, :, :], io["hg%d" % hpart][rk * D:(rk + 1) * D, c0:c0 + TT2].rearrange("(dc p) t -> p dc t", p=128),
               r=[kb.dbuf["hg%d" % hpart]], w=[hT])
        for hd in range(2):
            bk = kb.bank()
            for kc in range(8):
                kb.mm(bk[:, :], win[:, kc, 1056 + hd * 128:1056 + (hd + 1) * 128], hT[:, kc, :], start=(kc == 0), stop=(kc == 7),
                      r=[win, hT], w=[bk])
            kb.cp(Qz[hd][0][0:64, :], bk[0:64, :], r=[bk], w=[Qz[hd][0]], eng="act")
            kb.cp(Qz[hd][1][64:128, :], bk[64:128, :], r=[bk], w=[Qz[hd][1]], eng="act")
            bk = kb.bank()
            for kc in range(8):
                kb.mm(bk[:, :], win[:, kc, 1312 + hd * 128:1312 + (hd + 1) * 128], hT[:, kc, :], start=(kc == 0), stop=(kc == 7),
                      r=[win, hT], w=[bk])
            kb.cp(KT[hd][:, t0:t0 + TT2], bk[:, :], r=[bk], w=[KT[hd]], eng="act")
        for blk in range(4):
            bk = kb.bank()
            for kc in range(8):
                kb.mm(bk[:, 0:256], hT[:, kc, blk * 128:(blk + 1) * 128], win[:, kc, 1568:1824], start=(kc == 0), stop=(kc == 7),
                      r=[win, hT], w=[bk])
            kb.cp(Vtm[:, ti * 4 + blk, :], bk[:, 0:256], r=[bk], w=[Vtm])
    def gen_fce(ti):
        t0 = ti * TT2
        project_shift(6, 768, 128, pl[0], ti)
        kb.act(tanhwd[0:64, :], pl[0][0:64, :], AF.Tanh, r=[pl[0]], w=[tanhwd])
        kb.cp(lorab[64:128, :], pl[0][64:128, :], r=[pl[0]], w=[lorab])
        project_shift(7, 896, 128, pl[1], ti)
        sigmoid_from(sgd0[:, :], pl[1][:, :], [pl[1]], [sgd0, pl[1]], pl[1][:, :])
        project_shift(8, 1024, 32, pl[0], ti)
        sigmoid_from(sgd1[0:32, :], pl[0][0:32, :], [pl[0]], [sgd1, pl[0]], pl[0][0:32, :])
        for ck in range(4):
            bk = kb.bank()
            kb.mm(bk[:, 0:256], tanhwd[:, ck * 128:(ck + 1) * 128], wupw[:, :], r=[tanhwd, wupw], w=[bk])
            sigmoid_from(sgw[:, ck, :], bk[:, 0:256], [bk], [sgw], sgw[:, ck, :])
        for hp in range(2):
            hs = slice(hp * 128, (hp + 1) * 128)
            cb_ = [kb.bank(), kb.bank()]
            for ck in range(4):
                kb.mm(cb_[ck // 2][:, (ck % 2) * 256:(ck % 2 + 1) * 256], sgw[:, ck, hs], tri32, start=True, stop=True,
                      r=[sgw, cstb], w=[cb_[ck // 2]])
            for b_ in range(2):
                if True:
                    kb.act(E12[:, 2 * b_:2 * b_ + 2, :], cb_[b_][:, :].rearrange("p (c x) -> p c x", c=2), AF.Exp, r=[cb_[b_]], w=[E12])
                if True:
                    kb.act(E3[:, 2 * b_:2 * b_ + 2, :], cb_[b_][:, :].rearrange("p (c x) -> p c x", c=2)[:, :, 0:128], AF.Exp,
                           r=[cb_[b_]], w=[E3], scale=-1.0)
            kb.cp(gC[hp][:, :], E12[:, :, 127], r=[E12], w=[gC[hp]])
            kb.tt(E4[:, :, :], E3[:, :, :], gC[hp][:, :].unsqueeze(2).to_broadcast([128, 4, 128]), ALU.mult, r=[E3, gC[hp]], w=[E4])
            pr, pk, pv = psh
            project_shift(0 + hp, 0 + hp * 128, 128, pr, ti)
            project_shift(2 + hp, 256 + hp * 128, 128, pk, ti)
            project_shift(4 + hp, 512 + hp * 128, 128, pv, ti)
            kkr, rs, iclr, kf, bf_, t1 = ft
            kb.ts(kkr[:, :], pk[:, :], p[:, P_KK + hp:P_KK + hp + 1], None, ALU.mult, r=[pk, p], w=[kkr])
            kb.act(sqb[:, :], kkr[:, :], AF.Square, r=[kkr], w=[sqb])
            bk = kb.bank()
            kb.mm(bk[:, :], blockb, sqb[:, :], r=[cbf, sqb], w=[bk])
            kb.ts(rs[:, :], bk[:, :], 1e-24, None, ALU.max, r=[bk], w=[rs])
            kb.act(rs[:, :], rs[:, :], AF.Sqrt, r=[rs], w=[rs])
            S.add("dve", lambda e, rs=rs: e.reciprocal(out=rs[:, :], in_=rs[:, :]), reads=_bl([rs]), writes=_bl([rs]))
            kb.tt(kkr[:, :], kkr[:, :], rs[:, :], ALU.mult, r=[kkr, rs], w=[kkr])
            bk = kb.bank()
            kb.mm(bk[:, :], aupb[:, hs], lorab[:, :], r=[aupb, lorab], w=[bk])
            sigmoid_from(iclr[:, :], bk[:, :], [bk, d], [iclr], iclr[:, :], bias=d[:, Q_A0H + hp:Q_A0H + hp + 1])
            kb.ts(t1[:, :], iclr[:, :], -1.0, p[:, P_KA + hp:P_KA + hp + 1], ALU.add, ALU.mult, r=[iclr, p], w=[t1])
            kb.stt(kf[:, :], t1[:, :], 1.0, pk[:, :], ALU.add, ALU.mult, r=[t1, pk], w=[kf])
            kb.tt(bf_[:, :], kkr[:, :], iclr[:, :], ALU.mult, r=[kkr, iclr], w=[bf_])
            e1v = E12[:, :, 0:128]
            e2v = E12[:, :, 128:256]
            v3 = lambda t_: t_[:, :].rearrange("p (c x) -> p c x", c=4)
            kb.stt(v3(atT[hp]), v3(kkr), -1.0, e2v, ALU.mult, ALU.mult, r=[kkr, E12], w=[atT[hp]])
            kb.tt(v3(rtT[hp]), v3(pr), e1v, ALU.mult, r=[pr, E12], w=[rtT[hp]])
            for h in range(2):
                hr = slice(h * 64, (h + 1) * 64)
                kb.stt(AR[hp][hr, :, h, 0, :], v3(kkr)[hr], -1.0, e2v[hr], ALU.mult, ALU.mult, r=[kkr, E12], w=[AR[hp]])
                kb.tt(AR[hp][hr, :, h, 1, :], v3(pr)[hr], e1v[hr], ALU.mult, r=[pr, E12], w=[AR[hp]])
            kb.tt(v3(btT[hp]), v3(bf_), E3[:, :, :], ALU.mult, r=[bf_, E3], w=[btT[hp]])
            kb.tt(v3(ktT[hp]), v3(kf), E3[:, :, :], ALU.mult, r=[kf, E3], w=[ktT[hp]])
            kb.tt(v3(bhT[hp]), v3(bf_), E4[:, :, :], ALU.mult, r=[bf_, E4], w=[bhT[hp]])
            kb.tt(v3(khT[hp]), v3(kf), E4[:, :, :], ALU.mult, r=[kf, E4], w=[khT[hp]])
            kb.cp(vTb[hp][:, :], pv[:, :], r=[pv], w=[vTb[hp]], eng="act")
            kb.stt(sqb[:, :], pr[:, :], p[:, P_RK + hp:P_RK + hp + 1], kf[:, :], ALU.mult, ALU.mult, r=[pr, p, kf], w=[sqb])
            bk = kb.bank()
            kb.mm(bk[:, :], blockb, sqb[:, :], r=[cbf, sqb], w=[bk])
            kb.tt(bonv[hp][:, :], bk[:, :], pv[:, :], ALU.mult, r=[bk, pv], w=[bonv[hp]])
            for ck in range(4):
                cs = slice(ck * 128, (ck + 1) * 128)
                bk = kb.bank()
                bkb = bk[:, :].bitcast(BF16)
                srcs = (atT[hp][:, cs], bhT[hp][:, cs], khT[hp][:, cs], vTb[hp][:, cs])
                for i_, s_ in enumerate(srcs):
                    kb.tr(bkb[:, i_ * 128:(i_ + 1) * 128], s_, identb, r=[atT[hp], bhT[hp], khT[hp], vTb[hp], cbf], w=[bk])
                kb.cp(TM4[hp][ck][:, :, :], bkb[:, 0:512].rearrange("p (a x) -> p a x", a=4), r=[bk], w=[TM4[hp][ck]])

        for batch in range(2):
            pairs = [(hp, ck) for ck in (2 * batch, 2 * batch + 1) for hp in range(2)]
            sl = {pr_: i_ for i_, pr_ in enumerate(pairs)}
            for (hp, ck) in pairs:
                s_ = sl[(hp, ck)]
                cs = slice(ck * 128, (ck + 1) * 128)
                b1, b2 = kb.bank(), kb.bank()
                arv = AR[hp][:, ck, :, :, :].rearrange("p h a x -> p (h a x)")
                kb.mm(b1[:, :], btT[hp][:, cs], arv, r=[btT[hp], AR[hp]], w=[b1])
                kb.mm(b2[:, :], ktT[hp][:, cs], arv, r=[ktT[hp], AR[hp]], w=[b2])
                kb.tt(A1m[s_][:, :, :], b1[:, :].rearrange("p (h x) -> p h x", h=2), mask512[:, :, :], ALU.mult,
                      r=[b1, mask512], w=[A1m[s_]])
                kb.tt(A2m[s_][:, :, :], b2[:, :].rearrange("p (h x) -> p h x", h=2), mask512[:, :, :], ALU.mult,
                      r=[b2, mask512], w=[A2m[s_]])
                b3 = kb.bank()
                b3b = b3[:, :].bitcast(BF16)
                for h in range(2):
                    kb.tr(b3b[:, h * 128:(h + 1) * 128], A1m[s_][:, h, 0:128], identb, r=[A1m[s_], cbf], w=[b3])
                kb.cp(QT0[s_][:, :, :], b3b[:, 0:256].rearrange("p (h x) -> p h x", h=2), r=[b3], w=[QT0[s_]], eng="act")
                kb.tt(MTb[s_][0][:, :, :], A1m[s_][:, :, 0:128], identb.unsqueeze(1).to_broadcast([128, 2, 128]), ALU.add,
                      r=[A1m[s_], cbf], w=[MTb[s_][0]])
            pairs_all = pairs
            for k, pairs in [(k_, pg_) for k_ in range(1, 7) for pg_ in (pairs_all[0:2], pairs_all[2:4])]:
                bxs = {}
                for (hp, ck) in pairs:
                    s_ = sl[(hp, ck)]
                    bx = kb.bank()
                    bxs[s_] = bx
                    for h in range(2):
                        if k == 1:
                            X, XT, rd = A1m[s_][:, h, 0:128], QT0[s_][:, h, :], [A1m[s_], QT0[s_]]
                        else:
                            prv = QX[s_][(k - 1) % 2]
                            X, XT, rd = prv[:, h, 0:128], prv[:, h, 128:256], [prv]
                        if k < 6:
                            kb.mm(bx[:, h * 256:h * 256 + 128], XT, X, r=rd, w=[bx])
                        kb.mm(bx[:, h * 256 + 128:h * 256 + 256], X, XT, r=rd, w=[bx])
                for (hp, ck) in pairs:
                    s_ = sl[(hp, ck)]
                    bx = bxs[s_]
                    cur = QX[s_][k % 2]
                    if k < 6:
                        kb.cp(cur[:, :, :], bx[:, :].rearrange("p (h x) -> p h x", h=2), r=[bx], w=[cur], eng="act")
                    else:
                        kb.cp(cur[:, :, 128:256], bx[:, :].rearrange("p (h x) -> p h x", h=2)[:, :, 128:256], r=[bx], w=[cur], eng="act")
                bms = {}
                for (hp, ck) in pairs:
                    s_ = sl[(hp, ck)]
                    cur = QX[s_][k % 2]
                    bm = kb.bank()
                    bms[s_] = bm
                    mprev = MTb[s_][(k - 1) % 2]
                    for h in range(2):
                        kb.mm(bm[:, h * 128:(h + 1) * 128], cur[:, h, 128:256], mprev[:, h, :], r=[cur, mprev], w=[bm])
                for (hp, ck) in pairs:
                    s_ = sl[(hp, ck)]
                    bm = bms[s_]
                    mprev = MTb[s_][(k - 1) % 2]
                    mcur = MTb[s_][k % 2]
                    kb.tt(mcur[:, :, :], bm[:, 0:256].rearrange("p (h x) -> p h x", h=2), mprev[:, :, :], ALU.add,
                          r=[bm, mprev], w=[mcur])
            pairs = pairs_all
            for (hp, ck) in pairs:
                s_ = sl[(hp, ck)]
                tm = TM4[hp][ck]
                bw = kb.bank()
                for h in range(2):
                    kb.mm(bw[:, h * 64:(h + 1) * 64], A2m[s_][:, h, 0:128], tm[:, 3, h * 64:(h + 1) * 64], r=[A2m[s_], tm], w=[bw])
                kb.cp(Wl[s_][:, :, :], bw[:, 0:128].rearrange("p (h x) -> p h x", h=2), r=[bw], w=[Wl[s_]], eng="act")
            for (hp, ck) in pairs:
                s_ = sl[(hp, ck)]
                tm = TM4[hp][ck]
                mt = MTb[s_][0]
                ba = kb.bank()
                for h in range(2):
                    kb.mm(ba[:, h * 128:h * 128 + 64], mt[:, h, :], tm[:, 0, h * 64:(h + 1) * 64], r=[mt, tm], w=[ba])
                    kb.mm(ba[:, h * 128 + 64:(h + 1) * 128], mt[:, h, :], Wl[s_][:, h, :], r=[mt, Wl[s_]], w=[ba])
                kb.cp(AU[s_][:, :, :], ba[:, 0:256].rearrange("p (h x) -> p h x", h=2), r=[ba], w=[AU[s_]])
            for (hp, ck) in pairs:
                s_ = sl[(hp, ck)]
                tm = TM4[hp][ck]
                br = kb.bank()
                for h in range(2):
                    hr = slice(h * 64, (h + 1) * 64)
                    kb.mm(br[hr, 0:128], AU[s_][:, h, 0:64], A1m[s_][:, h, 128:256], start=True, stop=False,
                          r=[AU[s_], A1m[s_]], w=[br], tp=(0, h * 64))
                    kb.mm(br[hr, 0:128], identb[:, hr], rtT[hp][:, ck * 128:(ck + 1) * 128], start=False, stop=True,
                          r=[cbf, rtT[hp]], w=[br], tp=(0, h * 64))
                kb.cp(RpT[s_][:, :], br[:, 0:128], r=[br], w=[RpT[s_]], eng="act")
                bp = kb.bank()
                for h in range(2):
                    hr = slice(h * 64, (h + 1) * 64)
                    kb.mm(bp[hr, 0:64], AU[s_][:, h, 0:64], tm[:, 1, hr], r=[AU[s_], tm], w=[bp], tp=(0, h * 64))
                    kb.mm(bp[hr, 64:128], tm[:, 1, hr], AU[s_][:, h, 64:128], start=True, stop=False, r=[AU[s_], tm], w=[bp],
                          tp=(0, h * 64))
                    kb.mm(bp[hr, 64:128], tm[:, 2, hr], tm[:, 3, hr], start=False, stop=True, r=[tm], w=[bp], tp=(0, h * 64))
                for h in range(2):
                    hr = slice(h * 64, (h + 1) * 64)
                    kb.stt(PTk[s_][hr, hr], i2[hr, :], gC[hp][hr, ck:ck + 1], bp[hr, 0:64], ALU.mult, ALU.add,
                           r=[i2, gC[hp], bp], w=[PTk[s_]])
                kb.cp(Sloc[s_][:, :], bp[:, 64:128], r=[bp], w=[Sloc[s_]], eng="act")
            for ck in (2 * batch, 2 * batch + 1):
                for hp in range(2):
                    s_ = sl[(hp, ck)]
                    tm = TM4[hp][ck]
                    by = kb.bank()
                    for h in range(2):
                        hr = slice(h * 64, (h + 1) * 64)
                        kb.mm(by[hr, 0:128], AU[s_][:, h, 64:128], A1m[s_][:, h, 128:256], start=True, stop=False,
                              r=[AU[s_], A1m[s_]], w=[by], tp=(0, h * 64))
                        kb.mm(by[hr, 0:128], tm[:, 3, hr], A2m[s_][:, h, 128:256], start=False, stop=False,
                              r=[tm, A2m[s_]], w=[by], tp=(0, h * 64))
                        kb.mm(by[hr, 0:128], STk[hp][:, hr], RpT[s_][:, :], start=False, stop=True,
                              r=[STk[hp], RpT[s_]], w=[by], tp=(0, h * 64))
                    kb.cp(YT[hp][:, ck * 128:(ck + 1) * 128], by[:, 0:128], r=[by], w=[YT[hp]], eng="act")
                    bs = kb.bank()
                    kb.mm(bs[:, 0:64], PTk[s_][:, :], STb[hp][:, :], r=[PTk[s_], STb[hp]], w=[bs])
                    kb.tt(STb[hp][:, :], bs[:, 0:64], Sloc[s_][:, :], ALU.add, r=[bs, Sloc[s_]], w=[STb[hp]])
                    for h in range(2):
                        hr = slice(h * 64, (h + 1) * 64)
                        kb.cp(STk[hp][hr, hr], STb[hp][hr, :], r=[STb[hp]], w=[STk[hp]], eng="act")

        for hp in range(2):
            hs = slice(hp * 128, (hp + 1) * 128)
            yc, ysq, rsd = ft[0], ft[1], ft[2]
            bk = kb.bank()
            kb.mm(bk[:, :], block32, YT[hp][:, :], r=[cstb, YT[hp]], w=[bk])
            kb.stt(yc[:, :], bk[:, :], -1.0 / 64.0, YT[hp][:, :], ALU.mult, ALU.add, r=[bk, YT[hp]], w=[yc])
            kb.act(ysq[:, :], yc[:, :], AF.Square, r=[yc], w=[ysq])
            bk = kb.bank()
            kb.mm(bk[:, :], block32, ysq[:, :], r=[cstb, ysq], w=[bk])
            kb.act(rsd[:, :], bk[:, :], AF.Sqrt, r=[bk, cstb], w=[rsd], bias=cst[:, 770:771], scale=1.0)
            S.add("dve", lambda e, rsd=rsd: e.reciprocal(out=rsd[:, :], in_=rsd[:, :]), reads=_bl([rsd]), writes=_bl([rsd]))
            kb.tt(yc[:, :], yc[:, :], rsd[:, :], ALU.mult, r=[yc, rsd], w=[yc])
            kb.ts(yc[:, :], yc[:, :], d[:, Q_GNW8 + hp:Q_GNW8 + hp + 1], p[:, P_GNB + hp:P_GNB + hp + 1], ALU.mult, ALU.add,
                  r=[yc, d, p], w=[yc])
            kb.tt(yc[:, :], yc[:, :], bonv[hp][:, :], ALU.add, r=[yc, bonv[hp]], w=[yc])
            bk = kb.bank()
            kb.mm(bk[:, :], gupb[:, 0, hs], sgd0[:, :], start=True, stop=False, r=[gupb, sgd0], w=[bk])
            kb.mm(bk[:, :], gupb[:, 1, hs], sgd1[:, :], start=False, stop=True, r=[gupb, sgd1], w=[bk])
            kb.tt(ybf[hp][:, :], yc[:, :], bk[:, :], ALU.mult, r=[yc, bk], w=[ybf[hp]])
            yq, yc0 = ti // 2, (ti % 2) * TT2
            kb.dma("sp", io["yb%d" % yq][hp * 128:(hp + 1) * 128, yc0:yc0 + TT2], ybf[hp][:, :], r=[ybf[hp]], w=[kb.dbuf["yb%d" % yq]])

    def gen_attn(ti):
        t0 = ti * TT2
        nkb = 4 * ti + 4
        pti = 0
        for hd in range(2):
            vs = slice(hd * 128, (hd + 1) * 128)
            for c in range(2):
                hr = slice(c * 64, (c + 1) * 64)
                bo, bl = kb.bank(), kb.bank()
                kb.reserved = {i_ for i_, t_ in enumerate(kb.banks) if t_ is bo or t_ is bl}
                def score(kbi, hd=hd, c=c):
                    nonlocal pti
                    off = max(0, kbi - 4 * ti) * 128
                    n = TT2 - off
                    diag = kbi >= 4 * ti
                    bsc = kb.bank()
                    kb.mm(bsc[:, 0:n], KT[hd][:, kbi * 128:(kbi + 1) * 128], Qz[hd][c][:, off:TT2], start=True, stop=(not diag),
                          r=[KT[hd], Qz[hd][c]], w=[bsc])
                    if diag:
                        kb.mm(bsc[:, 0:128], identb, maskb, start=False, stop=True, r=[cbf], w=[bsc])
                    pt_ = PTb[pti % len(PTb)]
                    pti += 1
                    kb.act(pt_[:, 0:n], bsc[:, 0:n], AF.Exp, r=[bsc], w=[pt_], scale=0.125)
                    return (kbi, pt_, off, n)

                def consume(item, vs=vs, bo=bo, bl=bl):
                    kbi, pt_, off, n = item
                    kb.mm(bo[:, off:TT2], Vtm[:, kbi, vs], pt_[:, 0:n], start=(kbi == 0), stop=(kbi == nkb - 1), r=[Vtm, pt_], w=[bo])
                    kb.mm(bl[:, off:TT2], onesb, pt_[:, 0:n], start=(kbi == 0), stop=(kbi == nkb - 1), r=[cbf, pt_], w=[bl])

                pend = []
                for kbi in range(nkb):
                    pend.append(score(kbi))
                    if len(pend) > LOOK:
                        consume(pend.pop(0))
                while pend:
                    consume(pend.pop(0))
                kb.reserved = set()
                S.add("dve", lambda e, bl=bl: e.reciprocal(out=rec[:, :], in_=bl[:, :]), reads=_bl([bl]), writes=_bl([rec]))
                kb.tt(osb[c][:, :], bo[:, :], rec[:, :], ALU.mult, r=[bo, rec], w=[osb[c]])
            kb.stt(od[:, :], osb[1][:, :], d[:, Q_NLAM:Q_NLAM + 1], osb[0][:, :], ALU.mult, ALU.add, r=[osb[0], osb[1], d], w=[od])
            kb.act(osb[1][:, :], od[:, :], AF.Square, r=[od], w=[osb[1]])
            bk = kb.bank()
            kb.mm(bk[:, :], ones32, osb[1][:, :], r=[cstb, osb[1]], w=[bk])
            kb.act(rec[:, :], bk[:, :], AF.Sqrt, r=[bk, cstb], w=[rec], bias=cst[:, 769:770], scale=1.0)
            S.add("dve", lambda e: e.reciprocal(out=rec[:, :], in_=rec[:, :]), reads=_bl([rec]), writes=_bl([rec]))
            kb.stt(ydb[hd][:, :], od[:, :], d[:, Q_SUBLN:Q_SUBLN + 1], rec[:, :], ALU.mult, ALU.mult, r=[od, d, rec], w=[ydb[hd]])
            yq, yc0 = ti // 2, (ti % 2) * TT2
            kb.dma("sp", io["yb%d" % yq][256 + hd * 128:256 + (hd + 1) * 128, yc0:yc0 + TT2], ydb[hd][:, :], r=[ydb[hd]],
                   w=[kb.dbuf["yb%d" % yq]])

    MLO = float(os.environ.get('K_MLO', '0'))
    MHI = float(os.environ.get('K_MHI', '0'))
    for ti in range(NT2):
        tile_proj(ti)
        strands = []
        for pool, fn_ in (("a", gen_attn), ("r", gen_fce)):
            S.capture = []
            kb.pool = pool if MHI > 0 else None
            fn_(ti)
            strands.append(S.capture)
        S.capture = None
        kb.pool = None
        la, lf_all = strands
        lo, hi = int(MLO * len(lf_all)), int(MHI * len(lf_all))
        for o_ in lf_all[:lo]:
            S.add(*o_)
        lf = lf_all[lo:hi]
        i = j = 0
        while i < len(la) or j < len(lf):
            if j >= len(lf) or (i < len(la) and i * len(lf) <= j * len(la)):
                S.add(*la[i])
                i += 1
            else:
                S.add(*lf[j])
                j += 1
        for o_ in lf_all[hi:]:
            S.add(*o_)
        if ti % 2 == 1:
            collective(kb, "yb%d" % (ti // 2), "yg%d" % (ti // 2))


def _chunks(v):
    return np.ascontiguousarray(v.reshape(8, 128).T)


def own_cols(g):
    a = np.arange
    return np.concatenate([g * 256 + a(256), 512 + g * 256 + a(256), 1024 + g * 256 + a(256), 1536 + a(288),
                           1824 + g * 256 + a(256), 2336 + g * 256 + a(256), 2848 + g * 256 + a(256)])


def make_prm(inp, g):
    p = np.zeros((128, NPRM), np.float32)
    p[:, P_F1PRE:P_F1PRE + 8] = _chunks(inp["ffn1_pre_g"][0])
    p[:, P_F1POST:P_F1POST + 8] = _chunks(inp["ffn1_post_g"][0])
    p[:, P_MPRE:P_MPRE + 8] = _chunks(inp["mix_pre_g"][0])
    p[:, P_MPOST:P_MPOST + 8] = _chunks(inp["mix_post_g"][0])
    p[:, P_F2PRE:P_F2PRE + 8] = _chunks(inp["ffn2_pre_g"][0])
    p[:, P_F2POST:P_F2POST + 8] = _chunks(inp["ffn2_post_g"][0])
    oc = own_cols(g)
    mu = inp["shift_mu"][0]
    for c in range(8):
        p[:, P_MU + c] = mu[oc[c * 128:(c + 1) * 128]]
    p[0:32, P_MU + 8] = mu[oc[1024:1056]]
    ch = slice(g * 256, (g + 1) * 256)
    for name, col in (("rwkv_k_k", P_KK), ("rwkv_k_a", P_KA), ("rwkv_a0", P_A0), ("rwkv_r_k", P_RK),
                      ("rwkv_gn_w", P_GNW), ("rwkv_gn_b", P_GNB)):
        v = inp[name][0].reshape(-1)[ch]
        p[:, col] = v[0:128]
        p[:, col + 1] = v[128:256]
    p[:, P_SUBLN] = inp["diff_subln_w"][0]
    p[:, P_SEL + g] = 1.0
    return p


def make_core_inputs(inp, core, shared):
    b, g = core // 2, core % 2
    oc = own_cols(g)
    ch = slice(g * 256, (g + 1) * 256)
    m = dict(shared)
    m["prm"] = make_prm(inp, g)
    m["xT"] = np.ascontiguousarray(inp["x"][b, g * TH:(g + 1) * TH, :].T)
    m["win"] = np.ascontiguousarray(inp["w_in"][0][:, oc])
    z63 = np.zeros((63, 256), np.float32)
    z64 = np.zeros((64, 256), np.float32)
    m["wupw"] = np.ascontiguousarray(np.concatenate([inp["rwkv_w_up"][0][:, ch], inp["rwkv_w0"][0][ch][None, :], z63], 0))
    m["aup"] = np.ascontiguousarray(np.concatenate([z64, inp["rwkv_a_up"][0][:, ch]], 0))
    gu = np.zeros((128, 2, 256), np.float32)
    gu[:, 0, :] = inp["rwkv_g_up"][0][0:128, ch]
    gu[0:32, 1, :] = inp["rwkv_g_up"][0][128:160, ch]
    m["gup"] = gu
    return m


def make_shared(inp):
    wo = inp["w_o"][0]
    return {
        "cst": make_consts(),
        "f1g": inp["ffn1_w_gate"][0], "f1u": inp["ffn1_w_up"][0], "f1d": inp["ffn1_w_down"][0],
        "f2g": inp["ffn2_w_gate"][0], "f2u": inp["ffn2_w_up"][0], "f2d": inp["ffn2_w_down"][0],
        "wo": np.ascontiguousarray(np.concatenate([wo[0:256], wo[512:768], wo[256:512], wo[768:1024]], 0)),
        "lamv": np.ascontiguousarray(np.concatenate([inp["diff_lam_q1"][0], inp["diff_lam_k1"][0],
                                                     inp["diff_lam_q2"][0], inp["diff_lam_k2"][0]])[None, :]),
    }


_CACHE = {}


def kernel(**inputs):
    inp = {k: np.asarray(v, dtype=np.float32) for k, v in inputs.items()}
    if "nc" not in _CACHE:
        _CACHE["nc"] = build("full")[0]
    nc = _CACHE["nc"]
    shared = make_shared(inp)
    in_maps = [make_core_inputs(inp, c, shared) for c in range(8)]
    res = run_bass_kernel_spmd(nc, in_maps, core_ids=list(range(8)))
    out = np.empty((4, T, D), np.float32)
    for c in range(8):
        b, g = c // 2, c % 2
        out[b, g * TH:(g + 1) * TH, :] = res.results[c]["outT"].T
    return out
```

```python
import os
import numpy as np
from contextlib import ExitStack
import concourse.bass as bass
import concourse.mybir as mybir
from concourse.bass_utils import run_bass_kernel_spmd

F32 = mybir.dt.float32
BF16 = mybir.dt.bfloat16
AF = mybir.ActivationFunctionType
ALU = mybir.AluOpType
AX = mybir.AxisListType

D = 1024
DFF = 2816
NFC = 22
T = 4096
TH = 2048
TT = 1024
NCOLS = 1824
NORM_EPS = 1e-6
GN_EPS = 64e-5
SUBLN_EPS = 1e-5
LAMBDA_INIT = 0.8 - 0.6 * 1.0
PAIRS = [[0, 1], [2, 3], [4, 5], [6, 7]]
NCST = 1160


class Buf:
    __slots__ = ("name", "lw", "rd")

    def __init__(self, name, lw=None):
        self.name = name
        self.lw = lw
        self.rd = []


class Op:
    __slots__ = ("eng", "fn", "deps", "idx", "dma", "target", "val", "dsem", "dval", "prewait", "inc")


class Sched:
    NDSEM = 12

    def __init__(self, nc, stack):
        self.nc = nc
        self.stack = stack
        self.streams = {"pe": [], "act": [], "dve": [], "pool": [], "sp": []}
        self.epoch = None
        self.bufs = []
        self.capture = None

    def buf(self, name):
        b = Buf(name, self.epoch)
        self.bufs.append(b)
        return b

    def add(self, eng, fn, reads=(), writes=(), dma=False, inc=None):
        if self.capture is not None:
            self.capture.append((eng, fn, list(reads), list(writes), dma, inc))
            return None
        op = Op()
        op.eng = eng
        op.fn = fn
        op.dma = dma
        op.target = False
        op.val = None
        op.prewait = None
        op.dsem = None
        op.inc = inc
        deps = set()
        for b in reads:
            if b.lw is not None:
                deps.add(b.lw)
        for b in writes:
            if b.lw is not None:
                deps.add(b.lw)
            deps.update(b.rd)
        if eng == "pe" and not dma:
            deps = {d for d in deps if not (d.eng == "pe" and not d.dma)}
        op.deps = deps
        for b in reads:
            b.rd.append(op)
        for b in writes:
            b.lw = op
            b.rd = []
        op.idx = len(self.streams[eng])
        self.streams[eng].append(op)
        return op

    def barrier(self):
        scr = self._scr
        op = self.add("dve", lambda e: e.memset(scr[0:1, 0:1], 0.0), writes=list(self.bufs))
        self.epoch = op
        self.bufs = []
        return op

    def emit(self):
        nc = self.nc
        st = self.stack
        sems = {e: st.enter_context(nc.semaphore("s_" + e)) for e in ("pe", "act", "dve", "pool")}
        dsems = {q: [st.enter_context(nc.semaphore("d_%s%d" % (q, i))) for i in range(self.NDSEM)]
                 for q in ("sp", "pool")}
        for e, ops in self.streams.items():
            for op in ops:
                best = {}
                dd = []
                for d in op.deps:
                    if d.dma:
                        dd.append(d)
                    elif d.eng not in best or best[d.eng].idx < d.idx:
                        best[d.eng] = d
                op.deps = list(best.values()) + dd
                for d in op.deps:
                    d.target = True
        lastops = []
        for e in ("pe", "act", "dve", "pool"):
            comp = [op for op in self.streams[e] if not op.dma]
            if comp:
                comp[-1].target = True
                lastops.append(comp[-1])
            c = 0
            for op in self.streams[e]:
                if op.dma:
                    continue
                if op.target:
                    c += 1
                    op.val = c
        final_waits = []
        for q in ("sp", "pool"):
            i = 0
            last = {}
            for op in self.streams[q]:
                if not op.dma:
                    continue
                k = i % self.NDSEM
                inc = op.inc if op.inc is not None else 16
                prev = last.get(k, 0)
                op.dsem = dsems[q][k]
                op.dval = prev + inc
                if prev > 0:
                    op.prewait = (op.dsem, prev)
                last[k] = op.dval
                i += 1
            final_waits += [(dsems[q][k], v) for k, v in last.items()]
        engobj = {"pe": "tensor", "act": "scalar", "dve": "vector", "pool": "gpsimd", "sp": "sync"}
        self.nwaits = 0
        self.ninst = 0

        def run_stream(e, eng):
            waited = {}

            def wait(sem, val):
                k = id(sem)
                if waited.get(k, 0) >= val:
                    return
                waited[k] = val
                eng.wait_ge(sem, val)
                self.nwaits += 1
            for op in self.streams[e]:
                if op.prewait is not None:
                    wait(*op.prewait)
                for d in op.deps:
                    if d.dma:
                        wait(d.dsem, d.dval)
                    else:
                        wait(sems[d.eng], d.val)
                ins = op.fn(eng)
                self.ninst += 1
                if op.dma:
                    ins.then_inc(op.dsem, op.inc if op.inc is not None else 16)
                elif op.target:
                    ins.then_inc(sems[e], 1)
            if e == "sp":
                for s, v in final_waits:
                    wait(s, v)
                for lo in lastops:
                    wait(sems[lo.eng], lo.val)

        with nc.Block() as block:
            for e in ("sp", "pool", "act", "dve", "pe"):
                getattr(block, engobj[e])(lambda eng, e=e: run_stream(e, eng))


class Tn:
    __slots__ = ("ap", "b")

    def __init__(self, ap, b):
        self.ap = ap
        self.b = b

    def __getitem__(self, k):
        return self.ap[k]


class Arena:
    def __init__(self, S, ap, nwords):
        self.S = S
        self.ap = ap
        self.n = nwords
        self.off = 0
        self.peak = 0

    def alloc(self, name, shape, dt):
        free = 1
        for s in shape[1:]:
            free *= s
        esz = 4 if dt == F32 else 2
        words = (free * esz + 3) // 4
        words = (words + 7) // 8 * 8
        assert self.off + words <= self.n, "arena overflow at %s: %d + %d > %d" % (name, self.off, words, self.n)
        v = self.ap[:, self.off:self.off + words]
        self.off += words
        self.peak = max(self.peak, self.off)
        if dt != F32:
            v = v.bitcast(dt)
        v = v[0:shape[0], 0:free]
        if len(shape) == 3:
            v = v.rearrange("p (a b) -> p a b", a=shape[1])
        elif len(shape) == 4:
            v = v.rearrange("p (a b c) -> p a b c", a=shape[1], b=shape[2])
        return Tn(v, self.S.buf(name))

    def allocn(self, name, n, shape, dt):
        return [self.alloc("%s%d" % (name, i), shape, dt) for i in range(n)]


def _bl(x):
    out = []
    for t in x:
        if t is None:
            continue
        out.append(t.b if isinstance(t, Tn) else t)
    return out


class KB:
    def __init__(self, nc, st, mode):
        self.nc = nc
        self.mode = mode
        self.S = Sched(nc, st)
        S = self.S
        self.banks = []
        for i in range(8):
            t = st.enter_context(nc.psum_tensor("psb%d" % i, [128, 512], F32))
            self.banks.append(Tn(t, None))
        self.pool = None
        self.pool_i = {}
        self.reserved = set()
        arena_words = 53208 - NCST - 16
        at = st.enter_context(nc.sbuf_tensor("arena", [128, arena_words], F32))
        self.cst = st.enter_context(nc.sbuf_tensor("cst_sb", [128, NCST], F32))
        self.A = Arena(S, at, arena_words)
        self.bscr = st.enter_context(nc.sbuf_tensor("bscr", [128, 8], F32))
        S._scr = self.bscr
        self.new_epoch_banks()

    def new_epoch_banks(self):
        for t in self.banks:
            t.b = self.S.buf("bank")

    POOLS = {None: list(range(8)), "a": [0, 1, 2, 3, 4], "r": [5, 6, 7]}

    def bank(self):
        pool = self.POOLS[self.pool]
        while True:
            k = self.pool_i.get(self.pool, 0)
            self.pool_i[self.pool] = k + 1
            i = pool[k % len(pool)]
            if i not in self.reserved:
                return self.banks[i]

    def mm(self, out, lhsT, rhs, start=True, stop=True, r=(), w=(), tp=None):
        kw = {} if tp is None else {"tile_position": tp}
        self.S.add("pe", lambda e: e.matmul(out, lhsT=lhsT, rhs=rhs, start=start, stop=stop, **kw),
                   reads=_bl(r), writes=_bl(w))

    def tr(self, out, in_, ident, r=(), w=()):
        self.S.add("pe", lambda e: e.transpose(out, in_, ident), reads=_bl(r), writes=_bl(w))

    def act(self, out, in_, func, r=(), w=(), bias=None, scale=None):
        kw = {}
        if bias is not None:
            kw["bias"] = bias
        if scale is not None:
            kw["scale"] = scale
        self.S.add("act", lambda e: e.activation(out=out, in_=in_, func=func, **kw), reads=_bl(r), writes=_bl(w))

    def ts(self, out, in0, s1, s2, op0, op1=None, r=(), w=(), eng="dve"):
        if op1 is None:
            self.S.add(eng, lambda e: e.tensor_scalar(out=out, in0=in0, scalar1=s1, scalar2=None, op0=op0),
                       reads=_bl(r), writes=_bl(w))
        else:
            self.S.add(eng, lambda e: e.tensor_scalar(out=out, in0=in0, scalar1=s1, scalar2=s2, op0=op0, op1=op1),
                       reads=_bl(r), writes=_bl(w))

    def tt(self, out, in0, in1, op, r=(), w=(), eng="dve"):
        self.S.add(eng, lambda e: e.tensor_tensor(out=out, in0=in0, in1=in1, op=op), reads=_bl(r), writes=_bl(w))

    def stt(self, out, in0, scalar, in1, op0, op1, r=(), w=()):
        self.S.add("dve", lambda e: e.scalar_tensor_tensor(out=out, in0=in0, scalar=scalar, in1=in1, op0=op0, op1=op1),
                   reads=_bl(r), writes=_bl(w))

    def cp(self, out, in_, r=(), w=(), eng="dve"):
        if eng == "act":
            self.S.add("act", lambda e: e.copy(out, in_), reads=_bl(r), writes=_bl(w))
        else:
            self.S.add(eng, lambda e: e.tensor_copy(out=out, in_=in_), reads=_bl(r), writes=_bl(w))

    def dma(self, q, out, in_, r=(), w=()):
        self.S.add(q, lambda e: e.dma_start(out=out, in_=in_), reads=_bl(r), writes=_bl(w), dma=True)


def make_consts():
    c = np.zeros((128, NCST), np.float32)
    c[:, 0:128] = np.eye(128)
    c[:, 128:256] = 1.0
    bo = np.zeros((128, 128), np.float32)
    bo[0:64, 0:64] = 1.0
    bo[64:, 64:] = 1.0
    c[:, 256:384] = bo
    k = np.arange(128)[:, None]
    q = np.arange(128)[None, :]
    c[:, 384:512] = np.where(k > q, -30000.0, 0.0)
    c[:, 512:640] = (k < q).astype(np.float32)
    c[:, 640:768] = (k <= q).astype(np.float32)
    c[:, 768] = D * NORM_EPS
    c[:, 769] = 128 * SUBLN_EPS
    c[:, 770] = 64 * GN_EPS
    c[:, 771] = 0.0
    c[:, 772] = 1.0
    c[:, 776:904] = (k <= q).astype(np.float32) * (-float(np.exp(-0.5)))
    c[:, 904:1032] = (k < q).astype(np.float32) * (-float(np.exp(-0.5)))
    c[:, 1032:1160] = (q < k).astype(np.float32)
    return c


P_F1PRE, P_F1POST, P_MPRE, P_MPOST, P_F2PRE, P_F2POST = 0, 8, 16, 24, 32, 40
P_MU = 48
P_KK = 57
P_KA = 59
P_A0 = 61
P_RK = 63
P_GNW = 65
P_GNB = 67
P_SUBLN = 69
P_SEL = 70
NPRM = 72
Q_F1PRE, Q_F1POST, Q_MPRE, Q_MPOST, Q_F2PRE, Q_F2POST = 0, 8, 16, 24, 32, 40
Q_OMU = 48
Q_SUBLN = 57
Q_LAM = 58
Q_NLAM = 59
Q_GNW8 = 60
Q_A0H = 62
NDER = 64


def build(mode="full"):
    nc = bass.Bass("TRN2", target_bir_lowering=False)
    ph1 = mode in ("full", "p1")
    ph2 = mode in ("full", "p2")
    ph3 = mode in ("full", "p3")

    def dram(name, shape, dt, kind):
        if kind == "Internal":
            return nc.dram_tensor(name, shape, dt).ap()
        return nc.dram_tensor(name, shape, dt, kind=kind).ap()

    IN, OUT, INT = "ExternalInput", "ExternalOutput", "Internal"
    io = {}
    io["cst"] = dram("cst", [128, NCST], F32, IN)
    io["prm"] = dram("prm", [128, NPRM], F32, IN)
    if ph1:
        io["xT"] = dram("xT", [D, TH], F32, IN)
        io["f1g"] = dram("f1g", [D, DFF], F32, IN)
        io["f1u"] = dram("f1u", [D, DFF], F32, IN)
        io["f1d"] = dram("f1d", [DFF, D], F32, IN)
    if ph2:
        io["win"] = dram("win", [D, NCOLS], F32, IN)
        io["wupw"] = dram("wupw", [128, 256], F32, IN)
        io["aup"] = dram("aup", [128, 256], F32, IN)
        io["gup"] = dram("gup", [128, 2, 256], F32, IN)
        io["lamv"] = dram("lamv", [1, 256], F32, IN)
    if ph3:
        io["wo"] = dram("wo", [D, D], F32, IN)
        io["f2g"] = dram("f2g", [D, DFF], F32, IN)
        io["f2u"] = dram("f2u", [D, DFF], F32, IN)
        io["f2d"] = dram("f2d", [DFF, D], F32, IN)
        io["outT"] = dram("outT", [D, TH], F32, OUT)
    def parts(name, n, shape, dt, kind):
        for i in range(n):
            io["%s%d" % (name, i)] = dram("%s%d" % (name, i), shape, dt, kind)

    if mode == "full":
        io["x1T"] = dram("x1T", [D, TH], F32, INT)
        parts("hb", 2, [D, TT], BF16, INT)
        parts("hg", 2, [2 * D, TT], BF16, INT)
        parts("yb", 4, [512, 1024], BF16, INT)
        parts("yg", 4, [1024, 1024], BF16, INT)
    elif mode == "p1":
        io["x1T"] = dram("x1T", [D, TH], F32, OUT)
        parts("hb", 2, [D, TT], BF16, OUT)
    elif mode == "p2":
        parts("hg", 2, [2 * D, TT], BF16, IN)
        parts("yb", 4, [512, 1024], BF16, OUT)
    elif mode == "p3":
        io["x1T"] = dram("x1T", [D, TH], F32, IN)
        parts("yg", 4, [1024, 1024], BF16, IN)

    with ExitStack() as st:
        kb = KB(nc, st, mode)
        kb.io = io
        prologue(kb)
        if ph1:
            phase_ffn(kb, 1)
        if ph2:
            kb.S.barrier()
            kb.new_epoch_banks()
            kb.A.off = kb.base_off
            phase_mixer(kb)
        if ph3:
            kb.S.barrier()
            kb.new_epoch_banks()
            kb.A.off = kb.base_off
            phase_ffn(kb, 3)
        kb.S.emit()
        kb.stats = (kb.S.ninst, kb.S.nwaits, kb.A.peak)
    return nc, kb


def collective(kb, srcname, dstname):
    S = kb.S
    if kb.mode != "full":
        return
    src, dst = kb.io[srcname], kb.io[dstname]
    bs = kb.dbuf[srcname]
    bd = kb.dbuf[dstname]
    S.add("pool", lambda e: e.collective_compute("AllGather", ALU.bypass, replica_groups=PAIRS,
                                                 ins=[src[:, :]], outs=[dst[:, :]]),
          reads=[bs], writes=[bd], dma=True, inc=1)


def prologue(kb):
    S, A, io = kb.S, kb.A, kb.io
    cb = S.buf("cst")
    kb.cstb = cb
    kb.dma("sp", kb.cst[:, :], io["cst"][:, :], w=[cb])
    kb.prm = A.alloc("prm", [128, NPRM], F32)
    kb.der = A.alloc("der", [128, NDER], F32)
    kb.dma("sp", kb.prm[:, :], io["prm"][:, :], w=[kb.prm])
    kb.cbf = A.alloc("cbf", [128, 512], BF16)
    kb.cp(kb.cbf[:, :], kb.cst[:, 0:512], r=[cb], w=[kb.cbf])
    kb.identb = kb.cbf.ap[:, 0:128]
    kb.onesb = kb.cbf.ap[:, 128:256]
    kb.blockb = kb.cbf.ap[:, 256:384]
    kb.maskb = kb.cbf.ap[:, 384:512]
    p, d = kb.prm, kb.der
    rw = dict(r=[p], w=[d])
    kb.ts(d[:, Q_F1PRE:Q_F1PRE + 8], p[:, P_F1PRE:P_F1PRE + 8], 32.0, None, ALU.mult, **rw)
    kb.ts(d[:, Q_F1POST:Q_F1POST + 8], p[:, P_F1POST:P_F1POST + 8], 16.0, None, ALU.mult, **rw)
    kb.ts(d[:, Q_MPRE:Q_MPRE + 8], p[:, P_MPRE:P_MPRE + 8], 32.0, None, ALU.mult, **rw)
    kb.ts(d[:, Q_MPOST:Q_MPOST + 8], p[:, P_MPOST:P_MPOST + 8], 32.0, None, ALU.mult, **rw)
    kb.ts(d[:, Q_F2PRE:Q_F2PRE + 8], p[:, P_F2PRE:P_F2PRE + 8], 32.0, None, ALU.mult, **rw)
    kb.ts(d[:, Q_F2POST:Q_F2POST + 8], p[:, P_F2POST:P_F2POST + 8], 16.0, None, ALU.mult, **rw)
    kb.ts(d[:, Q_OMU:Q_OMU + 9], p[:, P_MU:P_MU + 9], -1.0, 1.0, ALU.mult, ALU.add, **rw)
    kb.ts(d[:, Q_SUBLN:Q_SUBLN + 1], p[:, P_SUBLN:P_SUBLN + 1], (1.0 - LAMBDA_INIT) * float(np.sqrt(128.0)), None,
          ALU.mult, **rw)
    kb.ts(d[:, Q_A0H:Q_A0H + 2], p[:, P_A0:P_A0 + 2], 0.5, None, ALU.mult, **rw)
    kb.dbuf = {k: S.buf(k) for k in ["x1T"] + ["hb%d" % i for i in range(2)] + ["hg%d" % i for i in range(2)]
               + ["yb%d" % i for i in range(4)] + ["yg%d" % i for i in range(4)]}
    kb.base_off = A.off


def rms_rstd(kb, sq, sqr, rstd, ncols, nchunk=8, epscol=768):
    bk = kb.bank()
    for dc in range(nchunk):
        kb.mm(bk[:, 0:ncols], kb.onesb, sq[:, dc, 0:ncols], start=(dc == 0), stop=(dc == nchunk - 1),
              r=[kb.cbf] + sqr, w=[bk])
    kb.act(rstd[:, 0:ncols], bk[:, 0:ncols], AF.Sqrt, r=[bk, kb.cstb], w=[rstd], bias=kb.cst[:, epscol:epscol + 1], scale=1.0)
    kb.S.add("dve", lambda e: e.reciprocal(out=rstd[:, 0:ncols], in_=rstd[:, 0:ncols]), reads=_bl([rstd]), writes=_bl([rstd]))


def phase_ffn(kb, which):
    S, A, io = kb.S, kb.A, kb.io
    d = kb.der
    NTS = TT // 512
    xt = [[A.alloc("xt%d_%d" % (dc, t_), [128, 512], F32) for t_ in range(NTS)] for dc in range(8)]
    fo = [[A.alloc("fo%d_%d" % (dc, t_), [128, 512], F32) for t_ in range(NTS)] for dc in range(8)]
    hT = A.alloc("hT", [128, 8, TT], BF16)
    AT = [A.alloc("AT%d" % i, [128, TT], BF16) for i in range(NFC)]
    wg = A.allocn("wg", 3, [128, 8, 256], BF16)
    wu = A.allocn("wu", 3, [128, 8, 256], BF16)
    wd = A.allocn("wd", 2, [128, NFC, 256], BF16)
    sq = A.alloc("sq", [128, 8, 512], BF16)
    rstd = A.allocn("rstd", 2, [128, 512], F32)
    xt_all = [t for row in xt for t in row]
    if which == 3:
        wo = A.alloc("wo", [128, 8, D], BF16)
        kb.dma("pool", wo[:, :, :], io["wo"].rearrange("(kc p) f -> p kc f", p=128), w=[wo])
        Wg, Wu, Wd = io["f2g"], io["f2u"], io["f2d"]
        qpre, qpost = Q_F2PRE, Q_F2POST
    else:
        Wg, Wu, Wd = io["f1g"], io["f1u"], io["f1d"]
        qpre, qpost = Q_F1PRE, Q_F1POST

    def load_gu(fg):
        s = fg % 3
        kb.dma("pool", wg[s][:, :, :], Wg[:, fg * 256:(fg + 1) * 256].rearrange("(kc p) f -> p kc f", p=128), w=[wg[s]])
        kb.dma("pool", wu[s][:, :, :], Wu[:, fg * 256:(fg + 1) * 256].rearrange("(kc p) f -> p kc f", p=128), w=[wu[s]])

    def load_d(dcp):
        s = dcp % 2
        kb.dma("pool", wd[s][:, :, :], Wd[:, dcp * 256:(dcp + 1) * 256].rearrange("(fc p) d -> p fc d", p=128), w=[wd[s]])

    def norm_to_bf16(src, qcol, dst):
        for ts_ in range(NTS):
            cols = slice(ts_ * 512, (ts_ + 1) * 512)
            for dc in range(8):
                kb.act(sq[:, dc, :], src[dc][ts_][:, :], AF.Square, r=[src[dc][ts_]], w=[sq])
            rs = rstd[ts_ % 2]
            rms_rstd(kb, sq, [sq], rs, 512)
            for dc in range(8):
                kb.stt(dst[:, dc, cols], src[dc][ts_][:, :], d[:, qcol + dc:qcol + dc + 1], rs[:, :], ALU.mult, ALU.mult,
                       r=[src[dc][ts_], d, rs], w=[dst])

    def post_norm_residual(qcol):
        for ts_ in range(NTS):
            for dc in range(8):
                kb.act(sq[:, dc, :], fo[dc][ts_][:, :], AF.Square, r=[fo[dc][ts_]], w=[sq])
            rs = rstd[ts_ % 2]
            rms_rstd(kb, sq, [sq], rs, 512)
            for dc in range(8):
                f_, x_ = fo[dc][ts_], xt[dc][ts_]
                kb.stt(f_[:, :], f_[:, :], d[:, qcol + dc:qcol + dc + 1], rs[:, :], ALU.mult, ALU.mult,
                       r=[f_, d, rs], w=[f_])
                kb.tt(x_[:, :], x_[:, :], f_[:, :], ALU.add, r=[x_, f_], w=[x_])

    def xt_dma(dram_ap, tcols, to_dram, r=(), w=()):
        for dc in range(8):
            for ts_ in range(NTS):
                c0 = tcols.start + ts_ * 512
                dr = dram_ap[dc * 128:(dc + 1) * 128, c0:c0 + 512]
                if to_dram:
                    kb.dma("sp", dr, xt[dc][ts_][:, :], r=[xt[dc][ts_]] + list(r), w=list(w))
                else:
                    kb.dma("sp", xt[dc][ts_][:, :], dr, r=list(r), w=[xt[dc][ts_]] + list(w))

    ntile = TH // TT
    for ti in range(ntile):
        tcols = slice(ti * TT, (ti + 1) * TT)
        if which == 1:
            xt_dma(io["xT"], tcols, False)
        else:
            xt_dma(io["x1T"], tcols, False, r=[kb.dbuf["x1T"]])
        load_gu(0)
        load_gu(1)
        if which == 3:
            yA = [AT[i] for i in range(0, 8)]
            yB = [AT[i] for i in range(8, 16)]
            for kc in range(8):
                kb.dma("sp", yA[kc][:, :], io["yg%d" % ti][kc * 128:(kc + 1) * 128, :],
                       r=[kb.dbuf["yg%d" % ti]], w=[yA[kc]])
                kb.dma("sp", yB[kc][:, :], io["yg%d" % (2 + ti)][kc * 128:(kc + 1) * 128, :],
                       r=[kb.dbuf["yg%d" % (2 + ti)]], w=[yB[kc]])
            p = kb.prm
            for kc in range(8):
                kb.ts(yA[kc][:, :], yA[kc][:, :], p[:, P_SEL:P_SEL + 1], None, ALU.mult, r=[yA[kc], p], w=[yA[kc]])
                kb.stt(hT[:, kc, :], yB[kc][:, :], p[:, P_SEL + 1:P_SEL + 2], yA[kc][:, :], ALU.mult, ALU.add,
                       r=[yB[kc], yA[kc], p], w=[hT])
            for dc in range(8):
                for ts_ in range(NTS):
                    cols = slice(ts_ * 512, (ts_ + 1) * 512)
                    bk = kb.bank()
                    for kc in range(8):
                        kb.mm(bk[:, :], wo[:, kc, dc * 128:(dc + 1) * 128], hT[:, kc, cols], start=(kc == 0), stop=(kc == 7),
                              r=[wo, hT], w=[bk])
                    kb.cp(fo[dc][ts_][:, :], bk[:, :], r=[bk], w=[fo[dc][ts_]], eng="act")
            post_norm_residual(Q_MPOST)
        norm_to_bf16(xt, qpre, hT)
        load_d(0)
        load_d(1)
        sgi = 0
        fo_all = [t for row in fo for t in row]
        for fg in range(NFC // 2):
            s = fg % 3
            for fi in range(2):
                fc = fg * 2 + fi
                bg = [kb.bank() for _ in range(NTS)]
                bu = [kb.bank() for _ in range(NTS)]
                for (wt, bks) in ((wg[s], bg), (wu[s], bu)):
                    for ts_ in range(NTS):
                        for kc in range(8):
                            kb.mm(bks[ts_][:, :], wt[:, kc, fi * 128:(fi + 1) * 128], hT[:, kc, ts_ * 512:(ts_ + 1) * 512],
                                  start=(kc == 0), stop=(kc == 7), r=[wt, hT], w=[bks[ts_]])
                for ts_ in range(NTS):
                    sg = fo_all[sgi % 16]
                    sgi += 1
                    kb.act(sg[:, :], bg[ts_][:, :], AF.Silu, r=[bg[ts_]], w=[sg])
                    kb.tt(AT[fc][:, ts_ * 512:(ts_ + 1) * 512], sg[:, :], bu[ts_][:, :], ALU.mult, r=[sg, bu[ts_]], w=[AT[fc]])
            if fg + 2 < NFC // 2:
                load_gu(fg + 2)
        for dcp in range(4):
            s = dcp % 2
            for di in range(2):
                dc = dcp * 2 + di
                for ts_ in range(NTS):
                    cols = slice(ts_ * 512, (ts_ + 1) * 512)
                    bk = kb.bank()
                    for fc in range(NFC):
                        kb.mm(bk[:, :], wd[s][:, fc, di * 128:(di + 1) * 128], AT[fc][:, cols], start=(fc == 0), stop=(fc == NFC - 1),
                              r=[wd[s], AT[fc]], w=[bk])
                    kb.cp(fo[dc][ts_][:, :], bk[:, :], r=[bk], w=[fo[dc][ts_]], eng="act")
            if dcp + 2 < 4:
                load_d(dcp + 2)
        post_norm_residual(qpost)
        if which == 1:
            xt_dma(io["x1T"], tcols, True, w=[kb.dbuf["x1T"]])
            norm_to_bf16(xt, Q_MPRE, hT)
            kb.dma("sp", io["hb%d" % ti][:, :].rearrange("(dc p) t -> p dc t", p=128), hT[:, :, :], r=[hT], w=[kb.dbuf["hb%d" % ti]])
            collective(kb, "hb%d" % ti, "hg%d" % ti)
        else:
            xt_dma(io["outT"], tcols, True)


def phase_mixer(kb):
    S, A, io = kb.S, kb.A, kb.io
    d, p = kb.der, kb.prm
    cst, cstb = kb.cst, kb.cstb
    TT2 = 512
    NT2 = T // TT2
    ones32 = cst[:, 128:256]
    block32 = cst[:, 256:384]
    tri32 = cst[:, 776:1032]

    win = A.alloc("win", [128, 8, NCOLS], BF16)
    kb.dma("pool", win[:, :, :], io["win"].rearrange("(kc p) f -> p kc f", p=128), w=[win])
    wupw = A.alloc("wupw", [128, 256], F32)
    kb.dma("sp", wupw[:, :], io["wupw"][:, :], w=[wupw])
    aupb = A.alloc("aupb", [128, 256], BF16)
    kb.dma("pool", aupb[:, :], io["aup"][:, :], w=[aupb])
    gupb = A.alloc("gupb", [128, 2, 256], BF16)
    kb.dma("pool", gupb[:, :, :], io["gup"][:, :, :], w=[gupb])
    lamv = A.alloc("lamv", [128, 256], F32)
    kb.dma("sp", lamv[:, :], io["lamv"].partition_broadcast(128), w=[lamv])
    ltmp = A.alloc("ltmp", [128, 128], F32)
    lsum = A.alloc("lsum", [128, 2], F32)
    kb.tt(ltmp[:, 0:64], lamv[:, 0:64], lamv[:, 64:128], ALU.mult, r=[lamv], w=[ltmp])
    kb.tt(ltmp[:, 64:128], lamv[:, 128:192], lamv[:, 192:256], ALU.mult, r=[lamv], w=[ltmp])
    S.add("dve", lambda e: e.reduce_sum(out=lsum[:, 0:1], in_=ltmp[:, 0:64], axis=AX.X), reads=_bl([ltmp]), writes=_bl([lsum]))
    S.add("dve", lambda e: e.reduce_sum(out=lsum[:, 1:2], in_=ltmp[:, 64:128], axis=AX.X), reads=_bl([ltmp]), writes=_bl([lsum]))
    kb.act(lsum[:, :], lsum[:, :], AF.Exp, r=[lsum], w=[lsum])
    kb.tt(d[:, Q_LAM:Q_LAM + 1], lsum[:, 0:1], lsum[:, 1:2], ALU.subtract, r=[lsum], w=[d])
    kb.ts(d[:, Q_LAM:Q_LAM + 1], d[:, Q_LAM:Q_LAM + 1], LAMBDA_INIT, None, ALU.add, r=[d], w=[d])
    kb.ts(d[:, Q_NLAM:Q_NLAM + 1], d[:, Q_LAM:Q_LAM + 1], -1.0, None, ALU.mult, r=[d], w=[d])
    kb.ts(d[:, Q_GNW8:Q_GNW8 + 2], p[:, P_GNW:P_GNW + 2], 8.0, None, ALU.mult, r=[p], w=[d])
    mask512 = A.alloc("mask512", [128, 2, 256], BF16)
    for h in range(2):
        kb.cp(mask512[:, h, :], cst[:, 512:768], r=[cstb], w=[mask512])
    lowm = A.alloc("lowm", [128, 128], BF16)
    kb.cp(lowm[:, :], cst[:, 1032:1160], r=[cstb], w=[lowm])
    i2 = A.alloc("i2", [128, 64], F32)
    kb.tt(i2[:, :], cst[:, 0:64], cst[:, 64:128], ALU.add, r=[cstb], w=[i2])
    identb, onesb, blockb, maskb = kb.identb, kb.onesb, kb.blockb, kb.maskb
    cbf = kb.cbf

    KT = A.allocn("KT", 2, [128, T], BF16)
    Vtm = A.alloc("Vtm", [128, T // 128, 256], BF16)
    hT = A.alloc("hT2", [128, 8, TT2], BF16)
    rawt = A.allocn("rawt", 2, [128, TT2 + 1], F32)
    carry = A.alloc("carry", [128, 9], F32)
    S.add("dve", lambda e: e.memset(carry[:, :], 0.0), writes=_bl([carry]))
    psh = A.allocn("psh", 3, [128, TT2], F32)
    pl = [psh[1], psh[0]]
    tanhwd = A.alloc("tanhwd", [128, TT2], F32)
    lorab = A.alloc("lorab", [128, TT2], BF16)
    sgd0 = A.alloc("sgd0", [128, TT2], BF16)
    sgd1 = A.alloc("sgd1", [128, TT2], BF16)
    sgw = A.alloc("sgw", [128, 4, 256], F32)
    S.add("dve", lambda e: e.memset(tanhwd[:, :], 0.0), writes=_bl([tanhwd]))
    S.add("dve", lambda e: e.memset(tanhwd[64:65, :], 1.0), writes=_bl([tanhwd]))
    S.add("dve", lambda e: e.memset(lorab[:, :], 0.0), writes=_bl([lorab]))
    S.add("dve", lambda e: e.memset(sgd1[:, :], 0.0), writes=_bl([sgd1]))
    Qz = [A.allocn("Qz%d_" % hd, 2, [128, TT2], BF16) for hd in range(2)]
    for hd in range(2):
        for c in range(2):
            S.add("dve", lambda e, hd=hd, c=c: e.memset(Qz[hd][c][:, :], 0.0), writes=_bl([Qz[hd][c]]))
    ft = A.allocn("ft", 6, [128, TT2], F32)
    E12 = A.alloc("E12", [128, 4, 256], F32)
    E3 = A.alloc("E3", [128, 4, 128], F32)
    E4 = A.alloc("E4", [128, 4, 128], F32)
    gC = A.allocn("gC", 2, [128, 4], F32)
    bonv = A.allocn("bonv", 2, [128, TT2], F32)
    sqb = A.alloc("sqb", [128, TT2], BF16)
    ARl = A.allocn("AR", 2, [128, 4 * 2 * 2 * 128], BF16)
    AR = [Tn(t_.ap.rearrange("p (c h a x) -> p c h a x", c=4, h=2, a=2), t_.b) for t_ in ARl]
    for hp in range(2):
        S.add("dve", lambda e, hp=hp: e.memset(ARl[hp][:, :], 0.0), writes=_bl([ARl[hp]]))
    atT = A.allocn("atT", 2, [128, TT2], BF16)
    rtT = A.allocn("rtT", 2, [128, TT2], BF16)
    btT = A.allocn("btT", 2, [128, TT2], BF16)
    ktT = A.allocn("ktT", 2, [128, TT2], BF16)
    bhT = A.allocn("bhT", 2, [128, TT2], BF16)
    khT = A.allocn("khT", 2, [128, TT2], BF16)
    vTb = A.allocn("vTb", 2, [128, TT2], BF16)
    TM4 = [[A.alloc("TM4_%d_%d" % (hp, ck), [128, 4, 128], BF16) for ck in range(4)] for hp in range(2)]
    NSL = 4
    A1m = A.allocn("A1m", NSL, [128, 2, 256], BF16)
    A2m = A.allocn("A2m", NSL, [128, 2, 256], BF16)
    QT0 = A.allocn("QT0", NSL, [128, 2, 128], BF16)
    QX = [A.allocn("QX%d_" % s_, 2, [128, 2, 256], BF16) for s_ in range(NSL)]
    MTb = [A.allocn("MT%d_" % s_, 2, [128, 2, 128], BF16) for s_ in range(NSL)]
    Wl = A.allocn("Wl", NSL, [128, 2, 64], BF16)
    AU = A.allocn("AU", NSL, [128, 2, 128], BF16)
    RpT = A.allocn("RpT", NSL, [128, 128], BF16)
    Sloc = A.allocn("Sloc", NSL, [128, 64], F32)
    STb = A.allocn("STb", 2, [128, 64], BF16)
    STk = A.allocn("STk", 2, [128, 128], BF16)
    PTk = A.allocn("PTk", NSL, [128, 128], BF16)
    for hp in range(2):
        S.add("dve", lambda e, hp=hp: e.memset(STb[hp][:, :], 0.0), writes=_bl([STb[hp]]))
        S.add("dve", lambda e, hp=hp: e.memset(STk[hp][:, :], 0.0), writes=_bl([STk[hp]]))
    for s_ in range(NSL):
        S.add("dve", lambda e, s_=s_: e.memset(PTk[s_][:, :], 0.0), writes=_bl([PTk[s_]]))
    YT = A.allocn("YT", 2, [128, TT2], F32)
    ybf = A.allocn("ybf", 2, [128, TT2], BF16)
    LOOK = 2
    PTb = A.allocn("PTb", LOOK + 2, [128, TT2], BF16)
    osb = A.allocn("osb", 2, [128, TT2], F32)
    rec = A.alloc("rec", [128, TT2], F32)
    od = osb[0]
    ydb = A.allocn("ydb", 2, [128, TT2], BF16)

    def sigmoid_from(out, in_, r, w, tmp, bias=None):
        if bias is None:
            kb.act(tmp, in_, AF.Tanh, r=r, w=w, scale=0.5)
        else:
            kb.act(tmp, in_, AF.Tanh, r=r, w=w, bias=bias, scale=0.5)
        kb.ts(out, tmp, 0.5, 0.5, ALU.mult, ALU.add, r=w, w=w)

    def project_shift(c, col0, M, dst, ti):
        bk = kb.bank()
        for kc in range(8):
            kb.mm(bk[0:M, :], win[:, kc, col0:col0 + M], hT[:, kc, :], start=(kc == 0), stop=(kc == 7), r=[win, hT], w=[bk])
        rt = rawt[c % 2]
        kb.cp(rt[0:M, 1:TT2 + 1], bk[0:M, :], r=[bk], w=[rt], eng="act")
        kb.cp(rt[0:M, 0:1], carry[0:M, c:c + 1], r=[carry], w=[rt])
        kb.ts(dst[0:M, :], rt[0:M, 0:TT2], p[0:M, P_MU + c:P_MU + c + 1], None, ALU.mult, r=[rt, p], w=[dst])
        kb.stt(dst[0:M, :], rt[0:M, 1:TT2 + 1], d[0:M, Q_OMU + c:Q_OMU + c + 1], dst[0:M, :], ALU.mult, ALU.add,
               r=[rt, d, dst], w=[dst])
        kb.cp(carry[0:M, c:c + 1], rt[0:M, TT2:TT2 + 1], r=[rt], w=[carry])

    def tile_proj(ti):
        t0 = ti * TT2
        rk, tok = ti // 4, (ti % 4) * TT2
        hpart, c0 = tok // TT, tok % TT
        kb.dma("sp", hT[:, :, :], io["hg%d" % hpart][rk * D:(rk + 1) * D, c0:c0 + TT2].rearrange("(dc p) t -> p dc t", p=128),
               r=[kb.dbuf["hg%d" % hpart]], w=[hT])
        for hd in range(2):
            bk = kb.bank()
            for kc in range(8):
                kb.mm(bk[:, :], win[:, kc, 1056 + hd * 128:1056 + (hd + 1) * 128], hT[:, kc, :], start=(kc == 0), stop=(kc == 7),
                      r=[win, hT], w=[bk])
            kb.cp(Qz[hd][0][0:64, :], bk[0:64, :], r=[bk], w=[Qz[hd][0]], eng="act")
            kb.cp(Qz[hd][1][64:128, :], bk[64:128, :], r=[bk], w=[Qz[hd][1]], eng="act")
            bk = kb.bank()
            for kc in range(8):
                kb.mm(bk[:, :], win[:, kc, 1312 + hd * 128:1312 + (hd + 1) * 128], hT[:, kc, :], start=(kc == 0), stop=(kc == 7),
                      r=[win, hT], w=[bk])
            kb.cp(KT[hd][:, t0:t0 + TT2], bk[:, :], r=[bk], w=[KT[hd]], eng="act")
        for blk in range(4):
            bk = kb.bank()
            for kc in range(8):
                kb.mm(bk[:, 0:256], hT[:, kc, blk * 128:(blk + 1) * 128], win[:, kc, 1568:1824], start=(kc == 0), stop=(kc == 7),
                      r=[win, hT], w=[bk])
            kb.cp(Vtm[:, ti * 4 + blk, :], bk[:, 0:256], r=[bk], w=[Vtm])
    def gen_fce(ti):
        t0 = ti * TT2
        project_shift(6, 768, 128, pl[0], ti)
        kb.act(tanhwd[0:64, :], pl[0][0:64, :], AF.Tanh, r=[pl[0]], w=[tanhwd])
        kb.cp(lorab[64:128, :], pl[0][64:128, :], r=[pl[0]], w=[lorab])
        project_shift(7, 896, 128, pl[1], ti)
        sigmoid_from(sgd0[:, :], pl[1][:, :], [pl[1]], [sgd0, pl[1]], pl[1][:, :])
        project_shift(8, 1024, 32, pl[0], ti)
        sigmoid_from(sgd1[0:32, :], pl[0][0:32, :], [pl[0]], [sgd1, pl[0]], pl[0][0:32, :])
        for ck in range(4):
            bk = kb.bank()
            kb.mm(bk[:, 0:256], tanhwd[:, ck * 128:(ck + 1) * 128], wupw[:, :], r=[tanhwd, wupw], w=[bk])
            sigmoid_from(sgw[:, ck, :], bk[:, 0:256], [bk], [sgw], sgw[:, ck, :])
        for hp in range(2):
            hs = slice(hp * 128, (hp + 1) * 128)
            cb_ = [kb.bank(), kb.bank()]
            for ck in range(4):
                kb.mm(cb_[ck // 2][:, (ck % 2) * 256:(ck % 2 + 1) * 256], sgw[:, ck, hs], tri32, start=True, stop=True,
                      r=[sgw, cstb], w=[cb_[ck // 2]])
            for b_ in range(2):
                if True:
                    kb.act(E12[:, 2 * b_:2 * b_ + 2, :], cb_[b_][:, :].rearrange("p (c x) -> p c x", c=2), AF.Exp, r=[cb_[b_]], w=[E12])
                if True:
                    kb.act(E3[:, 2 * b_:2 * b_ + 2, :], cb_[b_][:, :].rearrange("p (c x) -> p c x", c=2)[:, :, 0:128], AF.Exp,
                           r=[cb_[b_]], w=[E3], scale=-1.0)
            kb.cp(gC[hp][:, :], E12[:, :, 127], r=[E12], w=[gC[hp]])
            kb.tt(E4[:, :, :], E3[:, :, :], gC[hp][:, :].unsqueeze(2).to_broadcast([128, 4, 128]), ALU.mult, r=[E3, gC[hp]], w=[E4])
            pr, pk, pv = psh
            project_shift(0 + hp, 0 + hp * 128, 128, pr, ti)
            project_shift(2 + hp, 256 + hp * 128, 128, pk, ti)
            project_shift(4 + hp, 512 + hp * 128, 128, pv, ti)
            kkr, rs, iclr, kf, bf_, t1 = ft
            kb.ts(kkr[:, :], pk[:, :], p[:, P_KK + hp:P_KK + hp + 1], None, ALU.mult, r=[pk, p], w=[kkr])
            kb.act(sqb[:, :], kkr[:, :], AF.Square, r=[kkr], w=[sqb])
            bk = kb.bank()
            kb.mm(bk[:, :], blockb, sqb[:, :], r=[cbf, sqb], w=[bk])
            kb.ts(rs[:, :], bk[:, :], 1e-24, None, ALU.max, r=[bk], w=[rs])
            kb.act(rs[:, :], rs[:, :], AF.Sqrt, r=[rs], w=[rs])
            S.add("dve", lambda e, rs=rs: e.reciprocal(out=rs[:, :], in_=rs[:, :]), reads=_bl([rs]), writes=_bl([rs]))
            kb.tt(kkr[:, :], kkr[:, :], rs[:, :], ALU.mult, r=[kkr, rs], w=[kkr])
            bk = kb.bank()
            kb.mm(bk[:, :], aupb[:, hs], lorab[:, :], r=[aupb, lorab], w=[bk])
            sigmoid_from(iclr[:, :], bk[:, :], [bk, d], [iclr], iclr[:, :], bias=d[:, Q_A0H + hp:Q_A0H + hp + 1])
            kb.ts(t1[:, :], iclr[:, :], -1.0, p[:, P_KA + hp:P_KA + hp + 1], ALU.add, ALU.mult, r=[iclr, p], w=[t1])
            kb.stt(kf[:, :], t1[:, :], 1.0, pk[:, :], ALU.add, ALU.mult, r=[t1, pk], w=[kf])
            kb.tt(bf_[:, :], kkr[:, :], iclr[:, :], ALU.mult, r=[kkr, iclr], w=[bf_])
            e1v = E12[:, :, 0:128]
            e2v = E12[:, :, 128:256]
            v3 = lambda t_: t_[:, :].rearrange("p (c x) -> p c x", c=4)
            kb.stt(v3(atT[hp]), v3(kkr), -1.0, e2v, ALU.mult, ALU.mult, r=[kkr, E12], w=[atT[hp]])
            kb.tt(v3(rtT[hp]), v3(pr), e1v, ALU.mult, r=[pr, E12], w=[rtT[hp]])
            for h in range(2):
                hr = slice(h * 64, (h + 1) * 64)
                kb.stt(AR[hp][hr, :, h, 0, :], v3(kkr)[hr], -1.0, e2v[hr], ALU.mult, ALU.mult, r=[kkr, E12], w=[AR[hp]])
                kb.tt(AR[hp][hr, :, h, 1, :], v3(pr)[hr], e1v[hr], ALU.mult, r=[pr, E12], w=[AR[hp]])
            kb.tt(v3(btT[hp]), v3(bf_), E3[:, :, :], ALU.mult, r=[bf_, E3], w=[btT[hp]])
            kb.tt(v3(ktT[hp]), v3(kf), E3[:, :, :], ALU.mult, r=[kf, E3], w=[ktT[hp]])
            kb.tt(v3(bhT[hp]), v3(bf_), E4[:, :, :], ALU.mult, r=[bf_, E4], w=[bhT[hp]])
            kb.tt(v3(khT[hp]), v3(kf), E4[:, :, :], ALU.mult, r=[kf, E4], w=[khT[hp]])
            kb.cp(vTb[hp][:, :], pv[:, :], r=[pv], w=[vTb[hp]], eng="act")
            kb.stt(sqb[:, :], pr[:, :], p[:, P_RK + hp:P_RK + hp + 1], kf[:, :], ALU.mult, ALU.mult, r=[pr, p, kf], w=[sqb])
            bk = kb.bank()
            kb.mm(bk[:, :], blockb, sqb[:, :], r=[cbf, sqb], w=[bk])
            kb.tt(bonv[hp][:, :], bk[:, :], pv[:, :], ALU.mult, r=[bk, pv], w=[bonv[hp]])
            for ck in range(4):
                cs = slice(ck * 128, (ck + 1) * 128)
                bk = kb.bank()
                bkb = bk[:, :].bitcast(BF16)
                srcs = (atT[hp][:, cs], bhT[hp][:, cs], khT[hp][:, cs], vTb[hp][:, cs])
                for i_, s_ in enumerate(srcs):
                    kb.tr(bkb[:, i_ * 128:(i_ + 1) * 128], s_, identb, r=[atT[hp], bhT[hp], khT[hp], vTb[hp], cbf], w=[bk])
                kb.cp(TM4[hp][ck][:, :, :], bkb[:, 0:512].rearrange("p (a x) -> p a x", a=4), r=[bk], w=[TM4[hp][ck]])

        for batch in range(2):
            pairs = [(hp, ck) for ck in (2 * batch, 2 * batch + 1) for hp in range(2)]
            sl = {pr_: i_ for i_, pr_ in enumerate(pairs)}
            for (hp, ck) in pairs:
                s_ = sl[(hp, ck)]
                cs = slice(ck * 128, (ck + 1) * 128)
                b1, b2 = kb.bank(), kb.bank()
                arv = AR[hp][:, ck, :, :, :].rearrange("p h a x -> p (h a x)")
                kb.mm(b1[:, :], btT[hp][:, cs], arv, r=[btT[hp], AR[hp]], w=[b1])
                kb.mm(b2[:, :], ktT[hp][:, cs], arv, r=[ktT[hp], AR[hp]], w=[b2])
                kb.tt(A1m[s_][:, :, :], b1[:, :].rearrange("p (h x) -> p h x", h=2), mask512[:, :, :], ALU.mult,
                      r=[b1, mask512], w=[A1m[s_]])
                kb.tt(A2m[s_][:, :, :], b2[:, :].rearrange("p (h x) -> p h x", h=2), mask512[:, :, :], ALU.mult,
                      r=[b2, mask512], w=[A2m[s_]])
                b3 = kb.bank()
                b3b = b3[:, :].bitcast(BF16)
                for h in range(2):
                    kb.tr(b3b[:, h * 128:(h + 1) * 128], A1m[s_][:, h, 0:128], identb, r=[A1m[s_], cbf], w=[b3])
                kb.cp(QT0[s_][:, :, :], b3b[:, 0:256].rearrange("p (h x) -> p h x", h=2), r=[b3], w=[QT0[s_]], eng="act")
                kb.tt(MTb[s_][0][:, :, :], A1m[s_][:, :, 0:128], identb.unsqueeze(1).to_broadcast([128, 2, 128]), ALU.add,
                      r=[A1m[s_], cbf], w=[MTb[s_][0]])
            pairs_all = pairs
            for k, pairs in [(k_, pg_) for k_ in range(1, 7) for pg_ in (pairs_all[0:2], pairs_all[2:4])]:
                bxs = {}
                for (hp, ck) in pairs:
                    s_ = sl[(hp, ck)]
                    bx = kb.bank()
                    bxs[s_] = bx
                    for h in range(2):
                        if k == 1:
                            X, XT, rd = A1m[s_][:, h, 0:128], QT0[s_][:, h, :], [A1m[s_], QT0[s_]]
                        else:
                            prv = QX[s_][(k - 1) % 2]
                            X, XT, rd = prv[:, h, 0:128], prv[:, h, 128:256], [prv]
                        if k < 6:
                            kb.mm(bx[:, h * 256:h * 256 + 128], XT, X, r=rd, w=[bx])
                        kb.mm(bx[:, h * 256 + 128:h * 256 + 256], X, XT, r=rd, w=[bx])
                for (hp, ck) in pairs:
                    s_ = sl[(hp, ck)]
                    bx = bxs[s_]
                    cur = QX[s_][k % 2]
                    if k < 6:
                        kb.cp(cur[:, :, :], bx[:, :].rearrange("p (h x) -> p h x", h=2), r=[bx], w=[cur], eng="act")
                    else:
                        kb.cp(cur[:, :, 128:256], bx[:, :].rearrange("p (h x) -> p h x", h=2)[:, :, 128:256], r=[bx], w=[cur], eng="act")
                bms = {}
                for (hp, ck) in pairs:
                    s_ = sl[(hp, ck)]
                    cur = QX[s_][k % 2]
                    bm = kb.bank()
                    bms[s_] = bm
                    mprev = MTb[s_][(k - 1) % 2]
                    for h in range(2):
                        kb.mm(bm[:, h * 128:(h + 1) * 128], cur[:, h, 128:256], mprev[:, h, :], r=[cur, mprev], w=[bm])
                for (hp, ck) in pairs:
                    s_ = sl[(hp, ck)]
                    bm = bms[s_]
                    mprev = MTb[s_][(k - 1) % 2]
                    mcur = MTb[s_][k % 2]
                    kb.tt(mcur[:, :, :], bm[:, 0:256].rearrange("p (h x) -> p h x", h=2), mprev[:, :, :], ALU.add,
                          r=[bm, mprev], w=[mcur])
            pairs = pairs_all
            for (hp, ck) in pairs:
                s_ = sl[(hp, ck)]
                tm = TM4[hp][ck]
                bw = kb.bank()
                for h in range(2):
                    kb.mm(bw[:, h * 64:(h + 1) * 64], A2m[s_][:, h, 0:128], tm[:, 3, h * 64:(h + 1) * 64], r=[A2m[s_], tm], w=[bw])
                kb.cp(Wl[s_][:, :, :], bw[:, 0:128].rearrange("p (h x) -> p h x", h=2), r=[bw], w=[Wl[s_]], eng="act")
            for (hp, ck) in pairs:
                s_ = sl[(hp, ck)]
                tm = TM4[hp][ck]
                mt = MTb[s_][0]
                ba = kb.bank()
                for h in range(2):
                    kb.mm(ba[:, h * 128:h * 128 + 64], mt[:, h, :], tm[:, 0, h * 64:(h + 1) * 64], r=[mt, tm], w=[ba])
                    kb.mm(ba[:, h * 128 + 64:(h + 1) * 128], mt[:, h, :], Wl[s_][:, h, :], r=[mt, Wl[s_]], w=[ba])
                kb.cp(AU[s_][:, :, :], ba[:, 0:256].rearrange("p (h x) -> p h x", h=2), r=[ba], w=[AU[s_]])
            for (hp, ck) in pairs:
                s_ = sl[(hp, ck)]
                tm = TM4[hp][ck]
                br = kb.bank()
                for h in range(2):
                    hr = slice(h * 64, (h + 1) * 64)
                    kb.mm(br[hr, 0:128], AU[s_][:, h, 0:64], A1m[s_][:, h, 128:256], start=True, stop=False,
                          r=[AU[s_], A1m[s_]], w=[br], tp=(0, h * 64))
                    kb.mm(br[hr, 0:128], identb[:, hr], rtT[hp][:, ck * 128:(ck + 1) * 128], start=False, stop=True,
                          r=[cbf, rtT[hp]], w=[br], tp=(0, h * 64))
                kb.cp(RpT[s_][:, :], br[:, 0:128], r=[br], w=[RpT[s_]], eng="act")
                bp = kb.bank()
                for h in range(2):
                    hr = slice(h * 64, (h + 1) * 64)
                    kb.mm(bp[hr, 0:64], AU[s_][:, h, 0:64], tm[:, 1, hr], r=[AU[s_], tm], w=[bp], tp=(0, h * 64))
                    kb.mm(bp[hr, 64:128], tm[:, 1, hr], AU[s_][:, h, 64:128], start=True, stop=False, r=[AU[s_], tm], w=[bp],
                          tp=(0, h * 64))
                    kb.mm(bp[hr, 64:128], tm[:, 2, hr], tm[:, 3, hr], start=False, stop=True, r=[tm], w=[bp], tp=(0, h * 64))
                for h in range(2):
                    hr = slice(h * 64, (h + 1) * 64)
                    kb.stt(PTk[s_][hr, hr], i2[hr, :], gC[hp][hr, ck:ck + 1], bp[hr, 0:64], ALU.mult, ALU.add,
                           r=[i2, gC[hp], bp], w=[PTk[s_]])
                kb.cp(Sloc[s_][:, :], bp[:, 64:128], r=[bp], w=[Sloc[s_]], eng="act")
            for ck in (2 * batch, 2 * batch + 1):
                for hp in range(2):
                    s_ = sl[(hp, ck)]
                    tm = TM4[hp][ck]
                    by = kb.bank()
                    for h in range(2):
                        hr = slice(h * 64, (h + 1) * 64)
                        kb.mm(by[hr, 0:128], AU[s_][:, h, 64:128], A1m[s_][:, h, 128:256], start=True, stop=False,
                              r=[AU[s_], A1m[s_]], w=[by], tp=(0, h * 64))
                        kb.mm(by[hr, 0:128], tm[:, 3, hr], A2m[s_][:, h, 128:256], start=False, stop=False,
                              r=[tm, A2m[s_]], w=[by], tp=(0, h * 64))
                        kb.mm(by[hr, 0:128], STk[hp][:, hr], RpT[s_][:, :], start=False, stop=True,
                              r=[STk[hp], RpT[s_]], w=[by], tp=(0, h * 64))
                    kb.cp(YT[hp][:, ck * 128:(ck + 1) * 128], by[:, 0:128], r=[by], w=[YT[hp]], eng="act")
                    bs = kb.bank()
                    kb.mm(bs[:, 0:64], PTk[s_][:, :], STb[hp][:, :], r=[PTk[s_], STb[hp]], w=[bs])
                    kb.tt(STb[hp][:, :], bs[:, 0:64], Sloc[s_][:, :], ALU.add, r=[bs, Sloc[s_]], w=[STb[hp]])
                    for h in range(2):
                        hr = slice(h * 64, (h + 1) * 64)
                        kb.cp(STk[hp][hr, hr], STb[hp][hr, :], r=[STb[hp]], w=[STk[hp]], eng="act")

        for hp in range(2):
            hs = slice(hp * 128, (hp + 1) * 128)
            yc, ysq, rsd = ft[0], ft[1], ft[2]
            bk = kb.bank()
            kb.mm(bk[:, :], block32, YT[hp][:, :], r=[cstb, YT[hp]], w=[bk])
            kb.stt(yc[:, :], bk[:, :], -1.0 / 64.0, YT[hp][:, :], ALU.mult, ALU.add, r=[bk, YT[hp]], w=[yc])
            kb.act(ysq[:, :], yc[:, :], AF.Square, r=[yc], w=[ysq])
            bk = kb.bank()
            kb.mm(bk[:, :], block32, ysq[:, :], r=[cstb, ysq], w=[bk])
            kb.act(rsd[:, :], bk[:, :], AF.Sqrt, r=[bk, cstb], w=[rsd], bias=cst[:, 770:771], scale=1.0)
            S.add("dve", lambda e, rsd=rsd: e.reciprocal(out=rsd[:, :], in_=rsd[:, :]), reads=_bl([rsd]), writes=_bl([rsd]))
            kb.tt(yc[:, :], yc[:, :], rsd[:, :], ALU.mult, r=[yc, rsd], w=[yc])
            kb.ts(yc[:, :], yc[:, :], d[:, Q_GNW8 + hp:Q_GNW8 + hp + 1], p[:, P_GNB + hp:P_GNB + hp + 1], ALU.mult, ALU.add,
                  r=[yc, d, p], w=[yc])
            kb.tt(yc[:, :], yc[:, :], bonv[hp][:, :], ALU.add, r=[yc, bonv[hp]], w=[yc])
            bk = kb.bank()
            kb.mm(bk[:, :], gupb[:, 0, hs], sgd0[:, :], start=True, stop=False, r=[gupb, sgd0], w=[bk])
            kb.mm(bk[:, :], gupb[:, 1, hs], sgd1[:, :], start=False, stop=True, r=[gupb, sgd1], w=[bk])
            kb.tt(ybf[hp][:, :], yc[:, :], bk[:, :], ALU.mult, r=[yc, bk], w=[ybf[hp]])
            yq, yc0 = ti // 2, (ti % 2) * TT2
            kb.dma("sp", io["yb%d" % yq][hp * 128:(hp + 1) * 128, yc0:yc0 + TT2], ybf[hp][:, :], r=[ybf[hp]], w=[kb.dbuf["yb%d" % yq]])

    def gen_attn(ti):
        t0 = ti * TT2
        nkb = 4 * ti + 4
        pti = 0
        prev_acc = set()
        for hd in range(2):
            vs = slice(hd * 128, (hd + 1) * 128)
            for c in range(2):
                hr = slice(c * 64, (c + 1) * 64)
                kb.reserved = set(prev_acc)
                bo, bl = kb.bank(), kb.bank()
                cur_acc = {i_ for i_, t_ in enumerate(kb.banks) if t_ is bo or t_ is bl}
                kb.reserved = cur_acc | set(prev_acc)
                def score(kbi, hd=hd, c=c):
                    nonlocal pti
                    off = max(0, kbi - 4 * ti) * 128
                    n = TT2 - off
                    diag = kbi >= 4 * ti
                    bsc = kb.bank()
                    kb.mm(bsc[:, 0:n], KT[hd][:, kbi * 128:(kbi + 1) * 128], Qz[hd][c][:, off:TT2], start=True, stop=(not diag),
                          r=[KT[hd], Qz[hd][c]], w=[bsc])
                    if diag:
                        kb.mm(bsc[:, 0:128], identb, maskb, start=False, stop=True, r=[cbf], w=[bsc])
                    pt_ = PTb[pti % len(PTb)]
                    pti += 1
                    kb.act(pt_[:, 0:n], bsc[:, 0:n], AF.Exp, r=[bsc], w=[pt_], scale=0.125)
                    return (kbi, pt_, off, n)

                def consume(item, vs=vs, bo=bo, bl=bl):
                    kbi, pt_, off, n = item
                    kb.mm(bo[:, off:TT2], Vtm[:, kbi, vs], pt_[:, 0:n], start=(kbi == 0), stop=(kbi == nkb - 1), r=[Vtm, pt_], w=[bo])
                    kb.mm(bl[:, off:TT2], onesb, pt_[:, 0:n], start=(kbi == 0), stop=(kbi == nkb - 1), r=[cbf, pt_], w=[bl])

                pend = []
                for kbi in range(nkb):
                    pend.append(score(kbi))
                    if len(pend) > LOOK:
                        consume(pend.pop(0))
                while pend:
                    consume(pend.pop(0))
                prev_acc = cur_acc
                kb.reserved = set()
                S.add("dve", lambda e, bl=bl: e.reciprocal(out=rec[:, :], in_=bl[:, :]), reads=_bl([bl]), writes=_bl([rec]))
                kb.tt(osb[c][:, :], bo[:, :], rec[:, :], ALU.mult, r=[bo, rec], w=[osb[c]])
            kb.stt(od[:, :], osb[1][:, :], d[:, Q_NLAM:Q_NLAM + 1], osb[0][:, :], ALU.mult, ALU.add, r=[osb[0], osb[1], d], w=[od])
            kb.act(osb[1][:, :], od[:, :], AF.Square, r=[od], w=[osb[1]])
            bk = kb.bank()
            kb.mm(bk[:, :], ones32, osb[1][:, :], r=[cstb, osb[1]], w=[bk])
            kb.act(rec[:, :], bk[:, :], AF.Sqrt, r=[bk, cstb], w=[rec], bias=cst[:, 769:770], scale=1.0)
            S.add("dve", lambda e: e.reciprocal(out=rec[:, :], in_=rec[:, :]), reads=_bl([rec]), writes=_bl([rec]))
            kb.stt(ydb[hd][:, :], od[:, :], d[:, Q_SUBLN:Q_SUBLN + 1], rec[:, :], ALU.mult, ALU.mult, r=[od, d, rec], w=[ydb[hd]])
            yq, yc0 = ti // 2, (ti % 2) * TT2
            kb.dma("sp", io["yb%d" % yq][256 + hd * 128:256 + (hd + 1) * 128, yc0:yc0 + TT2], ydb[hd][:, :], r=[ydb[hd]],
                   w=[kb.dbuf["yb%d" % yq]])

    MLO = float(os.environ.get('K_MLO', '0'))
    MHI = float(os.environ.get('K_MHI', '0'))
    for ti in range(NT2):
        tile_proj(ti)
        strands = []
        for pool, fn_ in (("a", gen_attn), ("r", gen_fce)):
            S.capture = []
            kb.pool = pool if MHI > 0 else None
            fn_(ti)
            strands.append(S.capture)
        S.capture = None
        kb.pool = None
        la, lf_all = strands
        lo, hi = int(MLO * len(lf_all)), int(MHI * len(lf_all))
        for o_ in lf_all[:lo]:
            S.add(*o_)
        lf = lf_all[lo:hi]
        i = j = 0
        while i < len(la) or j < len(lf):
            if j >= len(lf) or (i < len(la) and i * len(lf) <= j * len(la)):
                S.add(*la[i])
                i += 1
            else:
                S.add(*lf[j])
                j += 1
        for o_ in lf_all[hi:]:
            S.add(*o_)
        if ti % 2 == 1:
            collective(kb, "yb%d" % (ti // 2), "yg%d" % (ti // 2))


def _chunks(v):
    return np.ascontiguousarray(v.reshape(8, 128).T)


def own_cols(g):
    a = np.arange
    return np.concatenate([g * 256 + a(256), 512 + g * 256 + a(256), 1024 + g * 256 + a(256), 1536 + a(288),
                           1824 + g * 256 + a(256), 2336 + g * 256 + a(256), 2848 + g * 256 + a(256)])


def make_prm(inp, g):
    p = np.zeros((128, NPRM), np.float32)
    p[:, P_F1PRE:P_F1PRE + 8] = _chunks(inp["ffn1_pre_g"][0])
    p[:, P_F1POST:P_F1POST + 8] = _chunks(inp["ffn1_post_g"][0])
    p[:, P_MPRE:P_MPRE + 8] = _chunks(inp["mix_pre_g"][0])
    p[:, P_MPOST:P_MPOST + 8] = _chunks(inp["mix_post_g"][0])
    p[:, P_F2PRE:P_F2PRE + 8] = _chunks(inp["ffn2_pre_g"][0])
    p[:, P_F2POST:P_F2POST + 8] = _chunks(inp["ffn2_post_g"][0])
    oc = own_cols(g)
    mu = inp["shift_mu"][0]
    for c in range(8):
        p[:, P_MU + c] = mu[oc[c * 128:(c + 1) * 128]]
    p[0:32, P_MU + 8] = mu[oc[1024:1056]]
    ch = slice(g * 256, (g + 1) * 256)
    for name, col in (("rwkv_k_k", P_KK), ("rwkv_k_a", P_KA), ("rwkv_a0", P_A0), ("rwkv_r_k", P_RK),
                      ("rwkv_gn_w", P_GNW), ("rwkv_gn_b", P_GNB)):
        v = inp[name][0].reshape(-1)[ch]
        p[:, col] = v[0:128]
        p[:, col + 1] = v[128:256]
    p[:, P_SUBLN] = inp["diff_subln_w"][0]
    p[:, P_SEL + g] = 1.0
    return p


def make_core_inputs(inp, core, shared):
    b, g = core // 2, core % 2
    oc = own_cols(g)
    ch = slice(g * 256, (g + 1) * 256)
    m = dict(shared)
    m["prm"] = make_prm(inp, g)
    m["xT"] = np.ascontiguousarray(inp["x"][b, g * TH:(g + 1) * TH, :].T)
    m["win"] = np.ascontiguousarray(inp["w_in"][0][:, oc])
    z63 = np.zeros((63, 256), np.float32)
    z64 = np.zeros((64, 256), np.float32)
    m["wupw"] = np.ascontiguousarray(np.concatenate([inp["rwkv_w_up"][0][:, ch], inp["rwkv_w0"][0][ch][None, :], z63], 0))
    m["aup"] = np.ascontiguousarray(np.concatenate([z64, inp["rwkv_a_up"][0][:, ch]], 0))
    gu = np.zeros((128, 2, 256), np.float32)
    gu[:, 0, :] = inp["rwkv_g_up"][0][0:128, ch]
    gu[0:32, 1, :] = inp["rwkv_g_up"][0][128:160, ch]
    m["gup"] = gu
    return m


def make_shared(inp):
    wo = inp["w_o"][0]
    return {
        "cst": make_consts(),
        "f1g": inp["ffn1_w_gate"][0], "f1u": inp["ffn1_w_up"][0], "f1d": inp["ffn1_w_down"][0],
        "f2g": inp["ffn2_w_gate"][0], "f2u": inp["ffn2_w_up"][0], "f2d": inp["ffn2_w_down"][0],
        "wo": np.ascontiguousarray(np.concatenate([wo[0:256], wo[512:768], wo[256:512], wo[768:1024]], 0)),
        "lamv": np.ascontiguousarray(np.concatenate([inp["diff_lam_q1"][0], inp["diff_lam_k1"][0],
                                                     inp["diff_lam_q2"][0], inp["diff_lam_k2"][0]])[None, :]),
    }


_CACHE = {}


def kernel(**inputs):
    inp = {k: np.asarray(v, dtype=np.float32) for k, v in inputs.items()}
    if "nc" not in _CACHE:
        _CACHE["nc"] = build("full")[0]
    nc = _CACHE["nc"]
    shared = make_shared(inp)
    in_maps = [make_core_inputs(inp, c, shared) for c in range(8)]
    res = run_bass_kernel_spmd(nc, in_maps, core_ids=list(range(8)))
    out = np.empty((4, T, D), np.float32)
    for c in range(8):
        b, g = c // 2, c % 2
        out[b, g * TH:(g + 1) * TH, :] = res.results[c]["outT"].T
    return out
```
